# Optimizing a Trainium2 kernel written in Bass

```python
import math
import jax, jax.numpy as jnp
from jax import lax
import numpy as np

D_MODEL = 1024
BATCH = 8
SEQ = 4096
DEPTH = 1

CHUNK = 64
Q_BLOCK = 128
D_FF = 2816
FFN_RES_WEIGHT = 0.5
ADA_SUBLAYERS = 3
ADA_WIDTH = ADA_SUBLAYERS * 3 * D_MODEL
MLA_HEADS = 8
MLA_Q_RANK = 256
MLA_KV_RANK = 128
MLA_NOPE = 64
MLA_ROPE = 32
MLA_V = 64
ROPE_THETA = 10000.0
CA_HEADS = 8
CA_HEAD_DIM = 64
CA_LEFT_CHUNKS = 8
CA_BAND = CA_LEFT_CHUNKS + 1
MAX_REL_DIST = 256
CA_WIDTH = CA_HEADS * CA_HEAD_DIM
MLA_OUT_WIDTH = MLA_HEADS * MLA_V
W_IN_COLS = MLA_Q_RANK + MLA_KV_RANK + MLA_ROPE + 3 * CA_WIDTH + 2 * D_MODEL
EPS = 1e-6
NEG_INF = -1e30

kernel_name = "hybrid_mla_chunkattn_macaron_adaln"


def rmsnorm(x, g):
    xf = x.astype(jnp.float32)
    y = xf * lax.rsqrt(jnp.mean(xf * xf, axis=-1, keepdims=True) + EPS)
    return (y * g.astype(jnp.float32)).astype(x.dtype)


def modulate(h, shift, scale):
    return h * (1 + scale[:, None, :]) + shift[:, None, :]


def swiglu(h, w_in, w_out):
    gu = h @ w_in
    g, u = jnp.split(gu, 2, axis=-1)
    return (jax.nn.silu(g) * u) @ w_out


def rope(x, cos, sin):
    half = x.shape[-1] // 2
    x1, x2 = x[..., :half], x[..., half:]
    return jnp.concatenate([x1 * cos - x2 * sin, x1 * sin + x2 * cos], axis=-1).astype(x.dtype)


def mla_attention(q_lat, kv_lat, k_pe_raw, positions, q_norm, w_uq, kv_norm, w_ukv):
    B, S = q_lat.shape[:2]
    cq = rmsnorm(q_lat, q_norm)
    q = (cq @ w_uq).reshape(B, S, MLA_HEADS, MLA_NOPE + MLA_ROPE)
    q_nope, q_pe = q[..., :MLA_NOPE], q[..., MLA_NOPE:]
    ckv = rmsnorm(kv_lat, kv_norm)
    kv = (ckv @ w_ukv).reshape(B, S, MLA_HEADS, MLA_NOPE + MLA_V)
    k_nope, v = kv[..., :MLA_NOPE], kv[..., MLA_NOPE:]

    inv_freq = ROPE_THETA ** (-jnp.arange(0, MLA_ROPE, 2, dtype=jnp.float32) / MLA_ROPE)
    ang = positions.astype(jnp.float32)[..., None] * inv_freq
    cos, sin = jnp.cos(ang), jnp.sin(ang)
    q_pe = rope(q_pe, cos[:, :, None, :], sin[:, :, None, :])
    k_pe = rope(k_pe_raw, cos, sin)

    scale = (MLA_NOPE + MLA_ROPE) ** -0.5
    key_chunk = jnp.arange(S) // CHUNK

    def block(qi):
        start = qi * Q_BLOCK
        qn = lax.dynamic_slice_in_dim(q_nope, start, Q_BLOCK, axis=1)
        qp = lax.dynamic_slice_in_dim(q_pe, start, Q_BLOCK, axis=1)
        s = (jnp.einsum('bqhd,bkhd->bhqk', qn, k_nope)
             + jnp.einsum('bqhd,bkd->bhqk', qp, k_pe)).astype(jnp.float32) * scale
        q_chunk = (start + jnp.arange(Q_BLOCK)) // CHUNK
        mask = key_chunk[None, :] <= q_chunk[:, None]
        s = jnp.where(mask[None, None], s, NEG_INF)
        p = jax.nn.softmax(s, axis=-1).astype(v.dtype)
        return jnp.einsum('bhqk,bkhd->bqhd', p, v)

    o = lax.map(block, jnp.arange(S // Q_BLOCK))
    return jnp.moveaxis(o, 0, 1).reshape(B, S, MLA_OUT_WIDTH)


def chunk_attention(q, k, v, rel_bias):
    B, S = q.shape[:2]
    NC = S // CHUNK
    KB = CA_BAND * CHUNK
    q = q.reshape(B, NC, CHUNK, CA_HEADS, CA_HEAD_DIM)
    pad = ((0, 0), (CA_LEFT_CHUNKS, 0), (0, 0), (0, 0), (0, 0))
    kpad = jnp.pad(k.reshape(B, NC, CHUNK, CA_HEADS, CA_HEAD_DIM), pad)
    vpad = jnp.pad(v.reshape(B, NC, CHUNK, CA_HEADS, CA_HEAD_DIM), pad)
    idx = jnp.arange(NC)[:, None] + jnp.arange(CA_BAND)[None, :]
    kb = kpad[:, idx].reshape(B, NC, KB, CA_HEADS, CA_HEAD_DIM)
    vb = vpad[:, idx].reshape(B, NC, KB, CA_HEADS, CA_HEAD_DIM)

    s = jnp.einsum('bnqhd,bnkhd->bnhqk', q, kb).astype(jnp.float32) * CA_HEAD_DIM ** -0.5
    rel = CA_LEFT_CHUNKS * CHUNK + jnp.arange(CHUNK)[:, None] - jnp.arange(KB)[None, :]
    rel = jnp.clip(rel, -MAX_REL_DIST, MAX_REL_DIST) + MAX_REL_DIST
    bias = jnp.transpose(rel_bias[rel], (2, 0, 1)).astype(jnp.float32)
    s = s + bias[None, None]
    key_chunk = jnp.arange(NC)[:, None] - CA_LEFT_CHUNKS + jnp.arange(KB)[None, :] // CHUNK
    valid = key_chunk >= 0
    s = jnp.where(valid[None, :, None, None, :], s, NEG_INF)
    p = jax.nn.softmax(s, axis=-1).astype(vb.dtype)
    o = jnp.einsum('bnhqk,bnkhd->bnqhd', p, vb)
    return o.reshape(B, S, CA_WIDTH)


def setup_inputs(seed: int = 0) -> dict:
    key = jax.random.key(seed)
    ks = jax.random.split(key, 24)
    L, D = DEPTH, D_MODEL

    def w(k, shape, fan_in, mult=1.0):
        return jax.random.normal(k, shape, jnp.float32) * (mult * fan_in ** -0.5)

    def gain(k, shape):
        return 1.0 + 0.1 * jax.random.normal(k, shape, jnp.float32)

    offsets = jax.random.randint(ks[2], (BATCH, 1), 0, 64, dtype=jnp.int32) * CHUNK
    positions = offsets + jnp.arange(SEQ, dtype=jnp.int32)[None, :]
    return {
        "x": jax.random.normal(ks[0], (BATCH, SEQ, D), jnp.float32),
        "c": jax.random.normal(ks[1], (BATCH, D), jnp.float32),
        "positions": positions,
        "w_ada": w(ks[3], (L, D, ADA_WIDTH), D),
        "b_ada": 0.1 * jax.random.normal(ks[4], (L, ADA_WIDTH), jnp.float32),
        "ffn1_norm": gain(ks[5], (L, D)),
        "ffn1_w_in": w(ks[6], (L, D, 2 * D_FF), D),
        "ffn1_w_out": w(ks[7], (L, D_FF, D), D_FF),
        "mix_norm": gain(ks[8], (L, D)),
        "w_in": w(ks[9], (L, D, W_IN_COLS), D),
        "mla_q_norm": gain(ks[10], (L, MLA_Q_RANK)),
        "mla_w_uq": w(ks[11], (L, MLA_Q_RANK, MLA_HEADS * (MLA_NOPE + MLA_ROPE)), MLA_Q_RANK),
        "mla_kv_norm": gain(ks[12], (L, MLA_KV_RANK)),
        "mla_w_ukv": w(ks[13], (L, MLA_KV_RANK, MLA_HEADS * (MLA_NOPE + MLA_V)), MLA_KV_RANK),
        "rel_bias": 0.5 * jax.random.normal(ks[14], (L, 2 * MAX_REL_DIST + 1, CA_HEADS), jnp.float32),
        "w_branch_a": w(ks[15], (L, MLA_OUT_WIDTH, D), MLA_OUT_WIDTH),
        "w_branch_b": w(ks[16], (L, CA_WIDTH, D), CA_WIDTH),
        "w_out": w(ks[17], (L, D, D), D),
        "ffn2_norm": gain(ks[18], (L, D)),
        "ffn2_w_in": w(ks[19], (L, D, 2 * D_FF), D),
        "ffn2_w_out": w(ks[20], (L, D_FF, D), D_FF),
        "final_norm": gain(ks[21], (D,)),
    }


def reference(x, c, positions, w_ada, b_ada, ffn1_norm, ffn1_w_in, ffn1_w_out,
              mix_norm, w_in, mla_q_norm, mla_w_uq, mla_kv_norm, mla_w_ukv, rel_bias,
              w_branch_a, w_branch_b, w_out, ffn2_norm, ffn2_w_in, ffn2_w_out, final_norm):
    c_act = jax.nn.silu(c)
    splits = []
    acc = 0
    for width in (MLA_Q_RANK, MLA_KV_RANK, MLA_ROPE, CA_WIDTH, CA_WIDTH, CA_WIDTH, D_MODEL):
        acc += width
        splits.append(acc)

    for l in range(DEPTH):
        ada = c_act @ w_ada[l] + b_ada[l]
        (sh1, sc1, g1, sh2, sc2, g2, sh3, sc3, g3) = jnp.split(ada, 3 * ADA_SUBLAYERS, axis=-1)

        h = modulate(rmsnorm(x, ffn1_norm[l]), sh1, sc1)
        x = x + FFN_RES_WEIGHT * g1[:, None, :] * swiglu(h, ffn1_w_in[l], ffn1_w_out[l])

        h = modulate(rmsnorm(x, mix_norm[l]), sh2, sc2)
        z = h @ w_in[l]
        q_lat, kv_lat, k_pe_raw, ca_q, ca_k, ca_v, gate_a, gate_b = jnp.split(z, splits, axis=-1)
        y_a = mla_attention(q_lat, kv_lat, k_pe_raw, positions, mla_q_norm[l], mla_w_uq[l],
                            mla_kv_norm[l], mla_w_ukv[l]) @ w_branch_a[l]
        B, S = ca_q.shape[:2]
        shp = (B, S, CA_HEADS, CA_HEAD_DIM)
        y_b = chunk_attention(ca_q.reshape(shp), ca_k.reshape(shp), ca_v.reshape(shp),
                              rel_bias[l]) @ w_branch_b[l]
        merged = jax.nn.sigmoid(gate_a) * y_a + jax.nn.sigmoid(gate_b) * y_b
        x = x + g2[:, None, :] * (merged @ w_out[l])

        h = modulate(rmsnorm(x, ffn2_norm[l]), sh3, sc3)
        x = x + FFN_RES_WEIGHT * g3[:, None, :] * swiglu(h, ffn2_w_in[l], ffn2_w_out[l])

    return rmsnorm(x, final_norm)
```

```python
import contextlib
import numpy as np
import concourse.bass as bass
import concourse.mybir as mybir
from concourse.bass_utils import run_bass_kernel_spmd

F32 = mybir.dt.float32
BF16 = mybir.dt.bfloat16
I32 = mybir.dt.int32
AF = mybir.ActivationFunctionType
ALU = mybir.AluOpType

S = 4096
D = 1024
T = 512
NT = S // T
FF = 2816
JF = FF // 128
EPS = 1e-6
NSLOT = 3
NTMP = 4
G = 512

BADA, N1C, N2C, N3C, NFC, QNC, KVNC, CTC, IVFC, SGNC, NV = 0, 72, 80, 88, 96, 104, 106, 107, 115, 116, 120
A1, A2, A3, G1H, G2H, G3H = 0, 8, 16, 24, 32, 40
SH1, SC1, G1, SH2, SC2, G2, SH3, SC3, G3 = 0, 8, 16, 24, 32, 40, 48, 56, 64

U_ADA = 0
U_F1W1 = 18
U_F1W2 = 29
U_WIN = 37
U_MLA = 45
U_WA = 46
U_WB = 47
U_WO = 48
U_F2W1 = 50
U_F2W2 = 61
NU = 69

SCALE_MLA = 96.0 ** -0.5
SCALE_CA = 64.0 ** -0.5
TWO_PI_S = 2.0 * np.pi * (1.0 - 2e-6)


def unit_elems(u):
    if U_F1W2 <= u < U_F1W2 + 8 or U_F2W2 <= u < U_F2W2 + 8:
        return FF
    if u == U_MLA:
        return 3072
    return 4096


class View:
    __slots__ = ("ap", "keys")

    def __init__(self, ap, keys):
        self.ap = ap
        self.keys = keys


class Ten:
    def __init__(self, name, h, esz, idx=None):
        self.name = name
        self.h = h
        self.esz = esz
        self.idx = idx

    def keys(self, lo, hi):
        g0 = (lo * self.esz) // G
        g1 = (hi * self.esz - 1) // G
        return [(self.name, g) for g in range(g0, g1 + 1)]

    def v(self, lo, hi, p0=0, p1=128):
        return View(self.h[p0:p1, lo:hi], self.keys(lo, hi))


class Sub:
    def __init__(self, ten, off):
        self.ten = ten
        self.off = off

    def v(self, lo, hi, p0=0, p1=128):
        return self.ten.v(self.off + lo, self.off + hi, p0, p1)


class Prog:
    ENG = ("pe", "act", "dve", "pool", "sp")

    def __init__(self):
        self.streams = {e: [] for e in self.ENG}
        self.count = {e: 0 for e in ("pe", "act", "dve", "pool")}
        self.seen = {e: {} for e in self.ENG}
        self.lastw = {}
        self.readers = {}
        self.dmacount = {}
        self.know = {}

    def _deps(self, eng, reads, writes):
        deps = {}

        def need(tok, raw):
            if tok is None:
                return
            sk, val = tok
            if sk == eng and eng == "pe":
                return
            if deps.get(sk, 0) < val:
                deps[sk] = val

        lw = self.lastw
        for k in reads:
            need(lw.get(k), True)
        for k in writes:
            need(lw.get(k), False)
            rd = self.readers.get(k)
            if rd:
                for sk, val in rd.items():
                    need((sk, val), False)
        out = []
        seen = self.seen[eng]
        for sk, val in sorted(deps.items(), key=lambda kv: -kv[1]):
            if seen.get(sk, 0) >= val:
                continue
            seen[sk] = val
            out.append((sk, val))
            kn = self.know.get((sk, val))
            if kn:
                for k2, v2 in kn.items():
                    if seen.get(k2, 0) < v2:
                        seen[k2] = v2
        return out

    def _commit(self, tok, reads, writes):
        sk, val = tok
        for k in reads:
            d = self.readers.get(k)
            if d is None:
                d = self.readers[k] = {}
            if d.get(sk, 0) < val:
                d[sk] = val
        for k in writes:
            self.lastw[k] = tok
            self.readers[k] = {}

    def op(self, eng, fn, reads, writes):
        waits = self._deps(eng, reads, writes)
        self.count[eng] += 1
        tok = (eng, self.count[eng])
        self.know[tok] = dict(self.seen[eng])
        self._commit(tok, reads, writes)
        self.streams[eng].append((waits, fn, (eng, 1)))

    def dma(self, queue, fn, semkey, reads, writes):
        waits = self._deps(queue, reads, writes)
        self.dmacount[semkey] = self.dmacount.get(semkey, 0) + 16
        tok = (semkey, self.dmacount[semkey])
        self.know[tok] = dict(self.seen[queue])
        self._commit(tok, reads, writes)
        self.streams[queue].append((waits, fn, (semkey, 16)))
        return tok

    def final_wait(self, eng, toks):
        waits = []
        for sk, val in toks:
            if self.seen[eng].get(sk, 0) < val:
                self.seen[eng][sk] = val
                waits.append((sk, val))
        self.streams[eng].append((waits, None, None))


def build_program(ntiles=NT, stop=None):
    nc = bass.Bass("TRN2", target_bir_lowering=False)
    xT = nc.dram_tensor("xT", [D, S], F32, kind="ExternalInput").ap()
    pos = nc.dram_tensor("pos", [1, S], I32, kind="ExternalInput").ap()
    vecs = nc.dram_tensor("vecs", [128, NV], F32, kind="ExternalInput").ap()
    biasg = nc.dram_tensor("biasg", [128, 8 * 640], F32, kind="ExternalInput").ap()
    wsrc = nc.dram_tensor("wsrc", [NU, 128, 4096], F32, kind="ExternalInput").ap()
    wbf = nc.dram_tensor("wbf", [NU, 128, 4096], BF16, kind="Internal").ap()
    outT = nc.dram_tensor("outT", [D, S], F32, kind="ExternalOutput").ap()

    pr = Prog()
    with contextlib.ExitStack() as es:
        def sb(name, shape, dt):
            h = es.enter_context(nc.sbuf_tensor(name, shape, dt))
            esz = 4 if dt in (F32, I32) else 2
            return Ten(name, h, esz)

        ring = [sb("ring%d" % i, [128, 4096], BF16) for i in range(NSLOT)]
        X = sb("X", [128, 4096], F32)
        hT = sb("hT", [128, 4096], BF16)
        sqm = sb("sqm", [128, 4096], BF16)
        arena = sb("arena", [128, 11264], BF16)
        oA = sb("oA", [128, 2048], BF16)
        oB = sb("oB", [128, 2048], BF16)
        pbMt = sb("pbMt", [128, 2048], BF16)
        pbM = [Sub(pbMt, i * 512) for i in range(4)]
        pbC = [sb("pbC%d" % i, [128, 512], BF16) for i in range(4)]
        ckv = sb("ckv", [128, S], BF16)
        kpe = sb("kpe", [128, S], BF16)
        VE = sb("VE", [128, 32 * 768], BF16)
        cak = sb("cak", [128, 4096], BF16)
        VEc = sb("VEc", [128, 8 * 768], BF16)
        Eb = sb("Eb", [128, 8 * 640], BF16)
        tmps = [sb("tmp%d" % i, [128, 640], F32) for i in range(NTMP)]
        for i, tm in enumerate(tmps):
            tm.idx = i
        rs = sb("rs", [128, 512], F32)
        cs = sb("cs", [128, 1024], F32)
        cosT = Sub(cs, 0)
        sinS = Sub(cs, 512)
        negh = sb("negh", [128, 1], F32)
        epsb = sb("epsb", [128, 1], F32)
        warm = sb("warm", [128, 1], F32)
        ones_f = sb("ones_f", [128, 1], F32)
        vec = sb("vec", [128, NV], F32)
        adaT = sb("adaT", [128, 72], F32)
        dv = sb("dv", [128, 48], F32)
        cact = sb("cact", [128, 8], BF16)
        ones = sb("ones", [128, 128], BF16)
        identb = sb("identb", [128, 128], BF16)
        mk = sb("mk", [128, 256], BF16)

        PS = []
        for i in range(8):
            h = es.enter_context(nc.psum_tensor("ps%d" % i, [128, 512], F32))
            PS.append(Ten("ps%d" % i, h, 4))

        semnames = ["pe", "act", "dve", "pool", "misc", "pos"]
        semnames += ["ring%d" % i for i in range(NSLOT)]
        semnames += ["rgp%d" % i for i in range(NSLOT)] + ["wb%d" % i for i in range(NSLOT)]
        semnames += ["x%d" % k for k in range(8)]
        semnames += ["tmp%d" % i for i in range(NTMP)]
        semnames += ["ost%d" % i for i in range(8)]
        SEM = {n: es.enter_context(nc.semaphore(n)) for n in semnames}

        def ACT(out, in_, func, scale=1.0, bias=None):
            reads = list(in_.keys)
            sc = scale
            if isinstance(scale, View):
                reads += scale.keys
                sc = scale.ap
            bi = bias
            if isinstance(bias, View):
                reads += bias.keys
                bi = bias.ap
            oa, ia = out.ap, in_.ap

            def fn(e):
                if bi is None:
                    return e.activation(out=oa, in_=ia, func=func, scale=sc)
                return e.activation(out=oa, in_=ia, func=func, scale=sc, bias=bi)
            pr.op("act", fn, reads, out.keys)

        def TT(out, a, b, op, eng="dve"):
            oa, aa, ba = out.ap, a.ap, b.ap
            pr.op(eng, lambda e: e.tensor_tensor(out=oa, in0=aa, in1=ba, op=op), a.keys + b.keys, out.keys)

        def TS(out, a, s1, s2, op0, op1=None):
            reads = list(a.keys)
            v1, v2 = s1, s2
            if isinstance(s1, View):
                reads += s1.keys
                v1 = s1.ap
            if isinstance(s2, View):
                reads += s2.keys
                v2 = s2.ap
            oa, aa = out.ap, a.ap

            def fn(e):
                if op1 is None:
                    return e.tensor_scalar(out=oa, in0=aa, scalar1=v1, scalar2=None, op0=op0)
                return e.tensor_scalar(out=oa, in0=aa, scalar1=v1, scalar2=v2, op0=op0, op1=op1)
            pr.op("dve", fn, reads, out.keys)

        def STT(out, a, s, b, op0, op1):
            reads = a.keys + b.keys
            sv = s
            if isinstance(s, View):
                reads = reads + s.keys
                sv = s.ap
            oa, aa, ba = out.ap, a.ap, b.ap
            pr.op("dve", lambda e: e.scalar_tensor_tensor(out=oa, in0=aa, scalar=sv, in1=ba, op0=op0, op1=op1),
                  reads, out.keys)

        def COPY(out, a, eng="dve"):
            oa, aa = out.ap, a.ap
            pr.op(eng, lambda e: e.tensor_copy(out=oa, in_=aa), a.keys, out.keys)

        def ACOPY(out, a):
            oa, aa = out.ap, a.ap
            pr.op("act", lambda e: e.copy(out=oa, in_=aa), a.keys, out.keys)

        def RECIP(out, a):
            oa, aa = out.ap, a.ap
            pr.op("dve", lambda e: e.reciprocal(out=oa, in_=aa), a.keys, out.keys)

        def MEMSET(out, val, eng="dve"):
            oa = out.ap
            pr.op(eng, lambda e: e.memset(oa, val), [], out.keys)

        def POW(out, a):
            oa, aa = out.ap, a.ap
            ba = negh.h[:, 0:1].to_broadcast([128, 512])
            pr.op("pool", lambda e: e.tensor_tensor(out=oa, in0=aa, in1=ba, op=ALU.pow),
                  a.keys + negh.keys(0, 1), out.keys)

        def MM(out, lhsT, rhs, start, stop, skip=False, also=None, alsor=None):
            oa, la, ra = out.ap, lhsT.ap, rhs.ap
            if also:
                out = View(out.ap, out.keys + also)
            if alsor:
                rhs = View(rhs.ap, rhs.keys + alsor)
            if st["nxtkeys"]:
                rhs = View(rhs.ap, rhs.keys + st["nxtkeys"])
                st["nxtkeys"] = None
            if skip:
                pr.op("pe", lambda e: e.matmul(oa, lhsT=la, rhs=ra, start=start, stop=stop, skip_group_check=True),
                      lhsT.keys + rhs.keys, out.keys)
            else:
                pr.op("pe", lambda e: e.matmul(oa, lhsT=la, rhs=ra, start=start, stop=stop),
                      lhsT.keys + rhs.keys, out.keys)

        def DMA(queue, out_ap, in_ap, semkey, reads, writes):
            return pr.dma(queue, lambda e: e.dma_start(out=out_ap, in_=in_ap), semkey, reads, writes)

        tmp_ctr = [0]

        def tmp():
            t_ = tmps[tmp_ctr[0] % NTMP]
            tmp_ctr[0] += 1
            return t_

        def Xk(k):
            return X.v(k * 512, (k + 1) * 512)

        def hTk(k):
            return hT.v(k * 512, (k + 1) * 512)

        def sqk(k):
            return sqm.v(k * 512, (k + 1) * 512)

        def actk(j):
            return arena.v(j * 512, (j + 1) * 512)

        QABS0, QPE0, CAQ0, CQN0 = 0, 4096, 8192, 10240

        stg = []
        for tn in (oA, oB, pbMt):
            for c in range(2):
                stg.append(View(tn.h[:, c * 1024:(c + 1) * 1024].bitcast(F32), tn.keys(c * 1024, (c + 1) * 1024)))
        for c in range(2):
            stg.append(cs.v(c * 512, (c + 1) * 512))

        per_tile = ([U_F1W1 + i for i in range(11)] + [U_F1W2 + m for m in range(8)]
                    + [U_WIN + 0, U_WIN + 1, U_WIN + 2, U_WIN + 3, U_MLA,
                       U_WIN + 4, U_WA, U_WIN + 5, U_WIN + 6, U_WB, U_WIN + 7, U_WO, U_WO + 1]
                    + [U_F2W1 + i for i in range(11)] + [U_F2W2 + m for m in range(8)])
        stream = [U_ADA + a for a in range(4)]
        for i in range(11):
            stream += [U_F1W1 + i, U_ADA + 4 + i]
        for m in range(8):
            stream += [U_F1W2 + m] + ([U_ADA + 15 + m] if m < 3 else [])
        stream += per_tile[19:]
        for _ in range(ntiles - 1):
            stream += per_tile
        st = {"cons": 0, "issued": 0, "nxtkeys": None}

        def next_unit(expect=None, live_before=0):
            while st["issued"] < min(len(stream), st["cons"] - live_before + NSLOT):
                n = st["issued"]
                u = stream[n]
                E = unit_elems(u)
                s_ = n % NSLOT
                if n < 18 + len(per_tile):
                    DMA("pool", ring[s_].h[:, 0:E], wsrc[u][:, 0:E], "rgp%d" % s_, [], ring[s_].keys(0, E))
                    if u >= U_F1W1:
                        DMA("sp", wbf[u][:, 0:E], ring[s_].h[:, 0:E], "wb%d" % s_, ring[s_].keys(0, E), [("wbf", u)])
                else:
                    DMA("sp", ring[s_].h[:, 0:E], wbf[u][:, 0:E], "ring%d" % s_, [("wbf", u)], ring[s_].keys(0, E))
                st["issued"] += 1
            n = st["cons"]
            if expect is not None:
                assert stream[n] == expect, (stream[n], expect)
            st["cons"] += 1
            if n + 1 < st["issued"]:
                st["nxtkeys"] = ring[(n + 1) % NSLOT].keys(0, unit_elems(stream[n + 1]))
            return ring[n % NSLOT]


        DMA("sp", vec.h[:, :], vecs, "misc", [], vec.keys(0, NV))
        MEMSET(ones.v(0, 128), 1.0)
        MEMSET(identb.v(0, 128), 0.0, eng="pool")
        _ia = identb.h[:, :]
        pr.op("pool", lambda e: e.affine_select(out=_ia, in_=_ia, pattern=[[-1, 128]], compare_op=ALU.not_equal,
                                                fill=1.0, base=0, channel_multiplier=1),
              identb.keys(0, 128), identb.keys(0, 128))
        MEMSET(negh.v(0, 1), -0.5, eng="pool")
        MEMSET(epsb.v(0, 1), EPS)
        MEMSET(ones_f.v(0, 1), 1.0)
        MEMSET(mk.v(0, 256), 0.0)
        MEMSET(mk.v(64, 128, 0, 1), 1.0)
        MEMSET(mk.v(128, 192, 0, 1), -30000.0)
        MEMSET(kpe.v(0, S), 0.0)
        for c0 in range(0, 32 * 768, 4096):
            MEMSET(VE.v(c0, c0 + 4096), 1.0)
        MEMSET(VEc.v(0, 4096), 1.0)
        MEMSET(VEc.v(4096, 8 * 768), 1.0)

        for h in range(8):
            tb = tmp()
            DMA("sp", tb.h[:, 0:640], biasg[:, h * 640:(h + 1) * 640], "tmp%d" % tb.idx, [], tb.keys(0, 640))
            ACT(Eb.v(h * 640, (h + 1) * 640), tb.v(0, 640), AF.Identity, scale=1.0 / SCALE_CA)
            MEMSET(Eb.v(h * 640 + 512 + 64, h * 640 + 640, 0, 64), -30000.0)
            MEMSET(Eb.v(h * 640, h * 640 + 64, 64, 128), -30000.0)

        for k in range(8):
            DMA("sp", X.h[:, k * 512:(k + 1) * 512], xT[k * 128:(k + 1) * 128, 0:512], "x%d" % k, [], Xk(k).keys)

        ACT(cact.v(0, 8), vec.v(CTC, CTC + 8), AF.Silu)

        def ada_unit(a):
            R = next_unit(U_ADA + a)
            for cc in range(4):
                c = 4 * a + cc
                for k in range(8):
                    b0 = (cc * 8 + k) * 128
                    MM(PS[7].v(c, c + 1), R.v(b0, b0 + 128), cact.v(k, k + 1), k == 0, k == 7)
            if a == 3:
                TT(adaT.v(0, 16), PS[7].v(0, 16), vec.v(BADA, BADA + 16), ALU.add)
                STT(dv.v(A1, A1 + 8), adaT.v(SC1, SC1 + 8), 1.0, vec.v(N1C, N1C + 8), ALU.add, ALU.mult)
            elif a == 5:
                TT(adaT.v(16, 24), PS[7].v(16, 24), vec.v(BADA + 16, BADA + 24), ALU.add)
                TS(dv.v(G1H, G1H + 8), adaT.v(G1, G1 + 8), 0.5, None, ALU.mult)
            elif a == 17:
                TT(adaT.v(24, 72), PS[7].v(24, 72), vec.v(BADA + 24, BADA + 72), ALU.add)
                for (acol, sccol, ncol) in ((A2, SC2, N2C), (A3, SC3, N3C)):
                    STT(dv.v(acol, acol + 8), adaT.v(sccol, sccol + 8), 1.0, vec.v(ncol, ncol + 8), ALU.add, ALU.mult)
                for (gcol, src) in ((G2H, G2), (G3H, G3)):
                    TS(dv.v(gcol, gcol + 8), adaT.v(src, src + 8), 0.5, None, ALU.mult)

        for a in range(4):
            ada_unit(a)

        if stop == "ada":
            ntiles = 0
            o_ = tmp()
            MEMSET(o_.v(0, 512), 0.0)
            COPY(o_.v(0, 72), adaT.v(0, 72))
            COPY(o_.v(72, 120), dv.v(0, 48))
            DMA("sp", outT[0:128, 0:512], o_.h[:, 0:512], "ost%d" % o_.idx, o_.keys(0, 512), [("out", 0, 0)])
            o2 = tmp()
            COPY(o2.v(0, 512), Eb.v(0, 512))
            DMA("sp", outT[128:256, 0:512], o2.h[:, 0:512], "ost%d" % o2.idx, o2.keys(0, 512), [("out", 0, 1)])

        def stats_rstd(nchunks, sq_of, inv_n, ps_bank, rs_t):
            for k in range(nchunks):
                MM(ps_bank.v(0, 512), ones.v(0, 128), sq_of(k), k == 0, k == nchunks - 1)
            sd = tmp()
            ACT(sd.v(0, 512), ps_bank.v(0, 512), AF.Ln, scale=inv_n, bias=epsb.v(0, 1))
            ACT(rs_t.v(0, 512), sd.v(0, 512), AF.Exp, scale=-0.5)

        def prewarm_ln():
            ACT(warm.v(0, 1), ones_f.v(0, 1), AF.Ln)

        def norm_sq(src=Xk):
            for k in range(8):
                ACT(sqk(k), src(k), AF.Square)

        def norm_apply(acol, bcol, src=Xk):
            stats_rstd(8, sqk, 1.0 / D, PS[6], rs)
            for k in range(8):
                t_ = tmp()
                STT(t_.v(0, 512), src(k), dv.v(acol + k, acol + k + 1), rs.v(0, 512), ALU.mult, ALU.mult)
                ACT(hTk(k), t_.v(0, 512), AF.Identity, bias=adaT.v(bcol + k, bcol + k + 1))

        def norm_mod(acol, bcol, src=Xk):
            norm_sq(src)
            norm_apply(acol, bcol, src)

        def ffn(u_w1, u_w2, ghcol, res=Xk, hooks=None, mid_hook=None, mhooks=None):
            R = None
            hooks = hooks or {}
            mhooks = mhooks or {}
            R = next_unit(u_w1)
            for k in range(8):
                for jj in range(2):
                    for half in range(2):
                        b0 = (jj * 8 + k) * 256 + 128 * half
                        MM(PS[2 * jj + half].v(0, 512), R.v(b0, b0 + 128), hTk(k), k == 0, k == 7)
            for jj in range(2):
                sg = tmp()
                ACT(sg.v(0, 512), PS[2 * jj].v(0, 512), AF.Silu)
                TT(actk(jj), sg.v(0, 512), PS[2 * jj + 1].v(0, 512), ALU.mult)
            for j in range(2, JF):
                if j in hooks:
                    hooks[j]()
                if j % 2 == 0:
                    R = next_unit(u_w1 + j // 2)
                jj = j % 2
                pg, pu = PS[2 * jj], PS[2 * jj + 1]
                for k in range(8):
                    b0 = (jj * 8 + k) * 256
                    MM(pg.v(0, 512), R.v(b0, b0 + 128), hTk(k), k == 0, k == 7,
                       also=(pu.keys(0, 512) if k == 0 else None))
                for k in range(8):
                    b0 = (jj * 8 + k) * 256 + 128
                    MM(pu.v(0, 512), R.v(b0, b0 + 128), hTk(k), k == 0, k == 7)
                sg = tmp()
                ACT(sg.v(0, 512), pg.v(0, 512), AF.Silu)
                TT(actk(j), sg.v(0, 512), pu.v(0, 512), ALU.mult)
            prewarm_ln()
            if mid_hook is not None:
                mid_hook()
            for m in range(8):
                if m in mhooks:
                    mhooks[m]()
                R = next_unit(u_w2 + m)
                py = PS[4 + m % 2]
                for j in range(JF):
                    MM(py.v(0, 512), R.v(j * 128, (j + 1) * 128), actk(j), j == 0, j == JF - 1)
                STT(Xk(m), py.v(0, 512), dv.v(ghcol + m, ghcol + m + 1), res(m), ALU.mult, ALU.add)

        def final_norm(t):
            stats_rstd(8, sqk, 1.0 / D, PS[6], rs)
            for k in range(8):
                STT(Xk(k), Xk(k), vec.v(NFC + k, NFC + k + 1), rs.v(0, 512), ALU.mult, ALU.mult)
                DMA("sp", outT[k * 128:(k + 1) * 128, t * 512:(t + 1) * 512], X.h[:, k * 512:(k + 1) * 512],
                    "ost%d" % k, Xk(k).keys, [("out", t, k)])

        def rope_tables(t):
            a, b, c, d = tmps[0], tmps[1], tmps[2], tmps[3]
            tmp_ctr[0] = 0
            posi = View(a.h[0:32, 0:512].bitcast(I32), a.keys(0, 512))
            DMA("pool", posi.ap, pos[0:1, t * 512:(t + 1) * 512].partition_broadcast(32), "pos", [], posi.keys)
            u_ = b.v(0, 512, 0, 32)
            TS(u_, posi, vec.v(IVFC, IVFC + 1, 0, 32), None, ALU.mult)
            ki = View(c.h[0:32, 0:512].bitcast(I32), c.keys(0, 512))
            COPY(ki, u_)
            f_ = d.v(0, 512, 0, 32)
            TT(f_, u_, ki, ALU.subtract)
            s_ = c.v(0, 512, 0, 32)
            ACT(s_, f_, AF.Sin, scale=TWO_PI_S)
            TS(sinS.v(0, 512, 0, 32), s_, vec.v(SGNC, SGNC + 1, 0, 32), None, ALU.mult)
            v_ = a.v(0, 512, 0, 32)
            TS(v_, u_, 0.25, None, ALU.add)
            COPY(ki, v_)
            TT(f_, v_, ki, ALU.subtract)
            ACT(cosT.v(0, 512, 0, 32), f_, AF.Sin, scale=TWO_PI_S)

        def rope_apply(out, psA, psB):
            t1, t2 = tmp(), tmp()
            TT(t1.v(0, 512, 0, 32), psA, cosT.v(0, 512, 0, 32), ALU.mult)
            TT(t2.v(0, 512, 0, 32), psB, sinS.v(0, 512, 0, 32), ALU.mult)
            TT(out, t1.v(0, 512, 0, 32), t2.v(0, 512, 0, 32), ALU.add)

        def ve_write(dst, base, ps_bank):
            src = ps_bank.h[:, 0:512].rearrange("p (i hh v) -> p i hh v", hh=2, v=64)
            dview = dst.h[:, base:base + 768].rearrange("p (i c) -> p i c", c=192)
            keys_r = ps_bank.keys(0, 512)
            keys_w = dst.keys(base, base + 768)
            COPY(View(dview[:, :, 0:64], keys_w), View(src[:, :, 0, :], keys_r))
            oa, ia = dview[:, :, 128:192], src[:, :, 1, :]
            pr.op("act", lambda e: e.copy(out=oa, in_=ia), keys_r, keys_w)

        def normalise_pair(i, o_t, on_dve=False):
            pe_, po_ = PS[4 + 2 * (i % 2)], PS[5 + 2 * (i % 2)]
            if on_dve:
                t1 = tmp()
                RECIP(t1.v(0, 512, 64, 128), pe_.v(0, 512, 64, 128))
                TT(o_t.v(i * 512, (i + 1) * 512, 0, 64), pe_.v(0, 512, 0, 64), t1.v(0, 512, 64, 128), ALU.mult)
                t2 = tmp()
                RECIP(t2.v(0, 512, 0, 64), po_.v(0, 512, 0, 64))
                TT(o_t.v(i * 512, (i + 1) * 512, 64, 128), po_.v(0, 512, 64, 128), t2.v(0, 512, 0, 64), ALU.mult)
                return
            t1 = tmp()
            ACT(t1.v(0, 512, 64, 128), pe_.v(0, 512, 64, 128), AF.Ln)
            ACT(t1.v(0, 512, 64, 128), t1.v(0, 512, 64, 128), AF.Exp, scale=-1.0)
            TT(o_t.v(i * 512, (i + 1) * 512, 0, 64), pe_.v(0, 512, 0, 64), t1.v(0, 512, 64, 128), ALU.mult)
            t2 = tmp()
            ACT(t2.v(0, 512, 0, 64), po_.v(0, 512, 0, 64), AF.Ln)
            ACT(t2.v(0, 512, 0, 64), t2.v(0, 512, 0, 64), AF.Exp, scale=-1.0)
            TT(o_t.v(i * 512, (i + 1) * 512, 64, 128), po_.v(0, 512, 64, 128), t2.v(0, 512, 0, 64), ALU.mult)

        for t in range(ntiles):
            tc0, tc1 = t * 512, (t + 1) * 512

            if t == 0:
                norm_mod(A1, SH1)
                h0 = {j: (lambda a=3 + j // 2: ada_unit(a)) for j in range(2, JF, 2)}
                ffn(U_F1W1, U_F1W2, G1H, hooks=h0, mid_hook=(lambda: ada_unit(14)),
                    mhooks={m: (lambda a=14 + m: ada_unit(a)) for m in (1, 2, 3)})
            else:
                ffn(U_F1W1, U_F1W2, G1H, res=(lambda m: stg[m]),
                    hooks={2: norm_sq, 6: (lambda tt=t - 1: final_norm(tt))})
            if stop == "ffn1":
                break

            norm_mod(A2, SH2)
            rope_tables(t)
            R = next_unit(U_WIN + 0)
            for k in range(8):
                for (bank, c0, mcols) in ((0, 0, 128), (1, 128, 128), (2, 256, 128), (3, 384, 32), (4, 416, 32)):
                    MM(PS[bank].v(0, 512, 0, mcols), R.v(k * 512 + c0, k * 512 + c0 + mcols), hTk(k), k == 0, k == 7)
            ACT(sqk(0), PS[0].v(0, 512), AF.Square)
            ACT(sqk(1), PS[1].v(0, 512), AF.Square)
            stats_rstd(2, sqk, 1.0 / 256, PS[6], rs)
            for kk in range(2):
                STT(arena.v(CQN0 + kk * 512, CQN0 + (kk + 1) * 512), PS[kk].v(0, 512),
                    vec.v(QNC + kk, QNC + kk + 1), rs.v(0, 512), ALU.mult, ALU.mult)
            ACT(sqk(2), PS[2].v(0, 512), AF.Square)
            rs2 = tmp()
            stats_rstd(1, lambda k: sqk(2), 1.0 / 128, PS[6], rs2)
            STT(ckv.v(tc0, tc1), PS[2].v(0, 512), vec.v(KVNC, KVNC + 1), rs2.v(0, 512), ALU.mult, ALU.mult)
            rope_apply(kpe.v(tc0, tc1, 0, 32), PS[3].v(0, 512, 0, 32), PS[4].v(0, 512, 0, 32))

            R = next_unit(U_WIN + 1)
            for i in range(4):
                pq = PS[5 + 2 * (i % 2)]
                for k in range(8):
                    MM(pq.v(0, 512), R.v(k * 512 + i * 128, k * 512 + (i + 1) * 128), hTk(k), k == 0, k == 7)
                ACOPY(arena.v(CAQ0 + i * 512, CAQ0 + (i + 1) * 512), pq.v(0, 512))
            R = next_unit(U_WIN + 2)
            for i in range(4):
                pk = PS[5 + 2 * (i % 2)]
                for k in range(8):
                    MM(pk.v(0, 512), R.v(k * 512 + i * 128, k * 512 + (i + 1) * 128), hTk(k), k == 0, k == 7)
                cb = i * 1024 + (t % 2) * 512
                COPY(cak.v(cb, cb + 512), pk.v(0, 512))
            R = next_unit(U_WIN + 3)
            for sub in range(4):
                pv = PS[5 + 2 * (sub % 2)]
                for k in range(8):
                    MM(pv.v(0, 512), hT.v(k * 512 + sub * 128, k * 512 + (sub + 1) * 128), R.v(k * 512, (k + 1) * 512),
                       k == 0, k == 7)
                ve_write(VEc, ((4 * t + sub) % 8) * 768, pv)

            R = next_unit(U_MLA)
            UQN, UQPE, UKT, UV = 0, 1024, 2048, 2560
            cqn = lambda kk: arena.v(CQN0 + kk * 512, CQN0 + (kk + 1) * 512)
            for i in range(4):
                pn = PS[i]
                for kk in range(2):
                    b0 = UQN + kk * 512 + i * 128
                    MM(pn.v(0, 512), R.v(b0, b0 + 128), cqn(kk), kk == 0, kk == 1)
                ACOPY(pbM[i].v(0, 512), pn.v(0, 512))
            for h in range(8):
                pA, pB = PS[4 + 2 * (h % 2)], PS[5 + 2 * (h % 2)]
                for (pp, off) in ((pA, 0), (pB, 32)):
                    for kk in range(2):
                        b0 = UQPE + kk * 512 + h * 64 + off
                        MM(pp.v(0, 512, 0, 32), R.v(b0, b0 + 32), cqn(kk), kk == 0, kk == 1)
                rope_apply(arena.v(QPE0 + h * 512, QPE0 + (h + 1) * 512, 0, 32),
                           pA.v(0, 512, 0, 32), pB.v(0, 512, 0, 32))
            for i in range(4):
                qn = pbM[i]
                for hh in range(2):
                    h = 2 * i + hh
                    pa = PS[(2 * i + hh) % 4]
                    MM(pa.v(0, 512), R.v(UKT + i * 128, UKT + (i + 1) * 128, 64 * hh, 64 * hh + 64),
                       qn.v(0, 512, 64 * hh, 64 * hh + 64), True, True)
                    if hh == 0:
                        COPY(arena.v(QABS0 + h * 512, QABS0 + (h + 1) * 512), pa.v(0, 512))
                    else:
                        ACOPY(arena.v(QABS0 + h * 512, QABS0 + (h + 1) * 512), pa.v(0, 512))
            for sub in range(4):
                pv = PS[sub % 2]
                MM(pv.v(0, 512), ckv.v(tc0 + sub * 128, tc0 + (sub + 1) * 128), R.v(UV, UV + 512), True, True)
                ve_write(VE, (4 * t + sub) * 768, pv)

            items = [(i, hh, kt) for i in range(4) for hh in range(2) for kt in range(4 * t + 4)]
            nk = 4 * t + 4

            def mla_S(n):
                i, hh, kt = items[n]
                h = 2 * i + hh
                j = kt - 4 * t
                q0 = 128 * j if j >= 0 else 0
                ps = PS[n % 4]
                MM(ps.v(q0, 512), ckv.v(kt * 128, (kt + 1) * 128),
                   arena.v(QABS0 + h * 512 + q0, QABS0 + (h + 1) * 512), True, False)
                MM(ps.v(q0, 512), kpe.v(kt * 128, (kt + 1) * 128),
                   arena.v(QPE0 + h * 512 + q0, QPE0 + (h + 1) * 512), False, j < 0)
                if j >= 0:
                    MM(ps.v(q0, q0 + 128), mk.v(0, 128), mk.v(128, 256), False, True)
                ACT(pbM[n % 4].v(q0, 512), ps.v(q0, 512), AF.Exp, scale=SCALE_MLA)

            def mla_PV(n, nxt=False):
                i, hh, kt = items[n]
                j = kt - 4 * t
                q0 = 128 * j if j >= 0 else 0
                po = PS[4 + 2 * (i % 2) + hh]
                c0 = kt * 768 + i * 192 + 64 * hh
                MM(po.v(q0, 512), VE.v(c0, c0 + 128), pbM[n % 4].v(q0, 512), kt == 0, kt == nk - 1,
                   alsor=(pbM[(n + 1) % 4].v(0, 512).keys if nxt else None))
                if kt == nk - 1 and hh == 1:
                    normalise_pair(i, oA, on_dve=(i < 3))

            LA = 4
            for n in range(min(LA, len(items))):
                mla_S(n)
            for n in range(0, len(items), 2):
                two = n + 1 < len(items)
                mla_PV(n, nxt=two)
                if two:
                    mla_PV(n + 1)
                for q_ in (n + LA, n + LA + 1):
                    if q_ < len(items):
                        mla_S(q_)

            kt_lo = max(0, 4 * t - 4)
            citems = [(i, hh, Kt) for i in range(4) for hh in range(2) for Kt in range(kt_lo, 4 * t + 4)]

            for h in range(8):
                i_, hh_ = h // 2, h % 2
                src = arena.v(CAQ0 + i_ * 512, CAQ0 + (i_ + 1) * 512, 64 * hh_, 64 * hh_ + 64)
                COPY(arena.v(QABS0 + h * 512, QABS0 + (h + 1) * 512, 64 * hh_, 64 * hh_ + 64), src, eng="pool")
                MEMSET(arena.v(QABS0 + h * 512, QABS0 + (h + 1) * 512, 64 * (1 - hh_), 64 * (1 - hh_) + 64), 0.0,
                       eng="pool")

            def ca_rng(Kt):
                dk = Kt - 4 * t
                s0, s1 = max(0, dk), min(3, dk + 4)
                return dk, 128 * s0, 128 * (s1 + 1)

            def ca_S(n):
                i, hh, Kt = citems[n]
                h = 2 * i + hh
                dk, c0, c1 = ca_rng(Kt)
                w = Kt % 8
                ps = PS[n % 4]
                kb = i * 1024 + w * 128
                MM(ps.v(c0, c1), cak.v(kb, kb + 128),
                   arena.v(QABS0 + h * 512 + c0, QABS0 + h * 512 + c1), True, False)
                e0 = h * 640 + c0 - 128 * dk
                MM(ps.v(c0, c1), identb.v(0, 128), Eb.v(e0, e0 + (c1 - c0)), False, True)
                ACT(pbC[n % 4].v(c0, c1), ps.v(c0, c1), AF.Exp, scale=SCALE_CA)

            def ca_PV(n, nxt=False):
                i, hh, Kt = citems[n]
                dk, c0, c1 = ca_rng(Kt)
                w = Kt % 8
                po = PS[4 + 2 * (i % 2) + hh]
                v0 = w * 768 + i * 192 + 64 * hh
                MM(po.v(c0, c1), VEc.v(v0, v0 + 128), pbC[n % 4].v(c0, c1), Kt == kt_lo, Kt == 4 * t + 3, skip=True,
                   alsor=(pbC[(n + 1) % 4].keys(0, 512) if nxt else None))
                if Kt == 4 * t + 3 and hh == 1:
                    normalise_pair(i, oB)

            for n in range(min(LA, len(citems))):
                ca_S(n)
            for n in range(0, len(citems), 2):
                two = n + 1 < len(citems)
                ca_PV(n, nxt=two)
                if two:
                    ca_PV(n + 1)
                for q_ in (n + LA, n + LA + 1):
                    if q_ < len(citems):
                        ca_S(q_)

            if stop == "attn":
                for q_, o_t in enumerate((oA, oB)):
                    for i in range(4):
                        o_ = tmp()
                        COPY(o_.v(0, 512), o_t.v(i * 512, (i + 1) * 512))
                        DMA("sp", outT[q_ * 128:(q_ + 1) * 128, i * 512:(i + 1) * 512], o_.h[:, 0:512],
                            "ost%d" % o_.idx, o_.keys(0, 512), [("out", q_, i)])
                for q_, src in enumerate((arena.v(QABS0, QABS0 + 512), arena.v(QPE0, QPE0 + 512), ckv.v(0, 512), kpe.v(0, 512),
                                          arena.v(CAQ0, CAQ0 + 512), cak.v(0, 512))):
                    o_ = tmp()
                    COPY(o_.v(0, 512), src)
                    DMA("sp", outT[(2 + q_) * 128:(3 + q_) * 128, 0:512], o_.h[:, 0:512],
                        "ost%d" % o_.idx, o_.keys(0, 512), [("out", 2 + q_, 0)])
                break

            mgk = sqk
            for phase, (ug0, uw, ug1, o_t) in enumerate(((U_WIN + 4, U_WA, U_WIN + 5, oA), (U_WIN + 6, U_WB, U_WIN + 7, oB))):
                Rg = Rw = None
                for m in range(8):
                    if m == 0:
                        Rg = next_unit(ug0)
                        Rw = next_unit(uw, live_before=1)
                    if m == 4:
                        Rg = next_unit(ug1, live_before=1)
                    pg, py = PS[m % 2], PS[2 + m % 2]
                    for k in range(8):
                        b0 = k * 512 + (m % 4) * 128
                        MM(pg.v(0, 512), Rg.v(b0, b0 + 128), hTk(k), k == 0, k == 7)
                    for i in range(4):
                        b0 = i * 1024 + m * 128
                        MM(py.v(0, 512), Rw.v(b0, b0 + 128), o_t.v(i * 512, (i + 1) * 512), i == 0, i == 3)
                    th = tmp()
                    ACT(th.v(0, 512), pg.v(0, 512), AF.Tanh, scale=0.5)
                    if phase == 0:
                        STT(mgk(m), th.v(0, 512), 1.0, py.v(0, 512), ALU.add, ALU.mult)
                    else:
                        u2 = tmp()
                        STT(u2.v(0, 512), th.v(0, 512), 1.0, py.v(0, 512), ALU.add, ALU.mult)
                        TT(mgk(m), mgk(m), u2.v(0, 512), ALU.add)
            prewarm_ln()
            R = None
            for mp in range(8):
                if mp % 4 == 0:
                    R = next_unit(U_WO + mp // 4)
                po = PS[4 + mp % 2]
                for m in range(8):
                    b0 = m * 512 + (mp % 4) * 128
                    MM(po.v(0, 512), R.v(b0, b0 + 128), mgk(m), m == 0, m == 7)
                STT(Xk(mp), po.v(0, 512), dv.v(G2H + mp, G2H + mp + 1), Xk(mp), ALU.mult, ALU.add)
            if stop == "mix":
                break

            if t + 1 < ntiles:
                for k in range(8):
                    DMA("sp", stg[k].ap, xT[k * 128:(k + 1) * 128, tc1:tc1 + 512], "x%d" % k, [], stg[k].keys)

            norm_mod(A3, SH3)
            if t + 1 < ntiles:
                stg_src = lambda k: stg[k]
                ffn(U_F2W1, U_F2W2, G3H, hooks={14: (lambda: norm_sq(stg_src))},
                    mid_hook=(lambda: norm_apply(A1, SH1, src=stg_src)))
            else:
                ffn(U_F2W1, U_F2W2, G3H)
                norm_sq()
                final_norm(t)


        if stop is not None and stop not in ("ada", "attn"):
            for k in range(8):
                o_ = tmp()
                COPY(o_.v(0, 512), Xk(k))
                DMA("sp", outT[k * 128:(k + 1) * 128, 0:512], o_.h[:, 0:512], "ost%d" % o_.idx,
                    o_.keys(0, 512), [("out", 0, k)])

        pr.final_wait("sp", [("ost%d" % i, pr.dmacount.get("ost%d" % i, 0)) for i in range(8)])
        for e in ("pe", "act", "dve", "pool"):
            assert pr.count[e] < 60000, (e, pr.count[e])

        with nc.Block() as block:
            def replay(name):
                def run(e):
                    for waits, fn, inc in pr.streams[name]:
                        for sk, val in waits:
                            e.wait_ge(SEM[sk], val)
                        if fn is not None:
                            fn(e).then_inc(SEM[inc[0]], inc[1])
                return run

            block.sync(replay("sp"))
            block.gpsimd(replay("pool"))
            block.scalar(replay("act"))
            block.vector(replay("dve"))
            block.tensor(replay("pe"))
    return nc, pr


def _fm(v, n):
    return np.ascontiguousarray(np.asarray(v, np.float32).reshape(n, 128).T)


def _kxc(w):
    K = w.shape[0] // 128
    return np.ascontiguousarray(w.reshape(K, 128, w.shape[1]).transpose(1, 0, 2).reshape(128, -1))


def prep_weights(w_ada, ffn1_w_in, ffn1_w_out, w_in, mla_w_uq, mla_w_ukv, w_branch_a, w_branch_b, w_out,
                 ffn2_w_in, ffn2_w_out):
    W = np.zeros((NU, 128, 4096), np.float32)
    wa = np.asarray(w_ada, np.float32).reshape(8, 128, 18, 4, 128)
    W[U_ADA:U_ADA + 18] = wa.transpose(2, 1, 3, 0, 4).reshape(18, 128, 4096)

    def w1_units(w1):
        w1 = np.asarray(w1, np.float32)
        g = w1[:, :FF].reshape(8, 128, JF, 128)
        u = w1[:, FF:].reshape(8, 128, JF, 128)
        gu = np.concatenate([g, u], axis=3)
        gu = gu.transpose(2, 1, 0, 3).reshape(11, 2, 128, 8, 256)
        return gu.transpose(0, 2, 1, 3, 4).reshape(11, 128, 4096)

    def w2_units(w2):
        w2 = np.asarray(w2, np.float32).reshape(JF, 128, 8, 128)
        return w2.transpose(2, 1, 0, 3).reshape(8, 128, FF)

    W[U_F1W1:U_F1W1 + 11] = w1_units(ffn1_w_in)
    W[U_F1W2:U_F1W2 + 8, :, :FF] = w2_units(ffn1_w_out)
    W[U_F2W1:U_F2W1 + 11] = w1_units(ffn2_w_in)
    W[U_F2W2:U_F2W2 + 8, :, :FF] = w2_units(ffn2_w_out)

    win = np.asarray(w_in, np.float32)
    c1 = np.zeros((D, 512), np.float32)
    c1[:, 0:416] = win[:, 0:416]
    c1[:, 416:432] = win[:, 400:416]
    c1[:, 432:448] = win[:, 384:400]
    W[U_WIN + 0] = _kxc(c1)
    for n, c0 in enumerate((416, 928, 1440, 1952, 2464, 2976, 3488)):
        W[U_WIN + 1 + n] = _kxc(win[:, c0:c0 + 512])

    uq = np.asarray(mla_w_uq, np.float32).reshape(256, 8, 96)
    ukv = np.asarray(mla_w_ukv, np.float32).reshape(128, 8, 128)
    uqn = uq[:, :, 0:64].reshape(256, 512)
    pe = uq[:, :, 64:96]
    pes = np.concatenate([pe[:, :, 16:32], pe[:, :, 0:16]], axis=2)
    uqpe = np.concatenate([pe, pes], axis=2).reshape(256, 512)
    ukT = ukv[:, :, 0:64].transpose(1, 2, 0).reshape(4, 128, 128)
    ukT = ukT.transpose(1, 0, 2).reshape(128, 512)
    uv = ukv[:, :, 64:128].reshape(128, 512)
    W[U_MLA, :, 0:1024] = _kxc(uqn)
    W[U_MLA, :, 1024:2048] = _kxc(uqpe)
    W[U_MLA, :, 2048:2560] = ukT
    W[U_MLA, :, 2560:3072] = uv
    W[U_WA] = _kxc(np.asarray(w_branch_a, np.float32))
    W[U_WB] = _kxc(np.asarray(w_branch_b, np.float32))
    wo = np.asarray(w_out, np.float32)
    W[U_WO] = _kxc(wo[:, 0:512])
    W[U_WO + 1] = _kxc(wo[:, 512:1024])
    return W


def prep_bias(rel_bias):
    rb = np.asarray(rel_bias, np.float32)
    kl = np.arange(128)[:, None, None]
    r = np.arange(5)[None, :, None]
    ql = np.arange(128)[None, None, :]
    d = 128 * r + ql - kl
    idx = np.minimum(d, 256) + 256
    g = rb[idx]
    return np.ascontiguousarray(g.transpose(0, 3, 1, 2).reshape(128, 8 * 640))


def prep_vecs(b, c, b_ada, ffn1_norm, mix_norm, ffn2_norm, final_norm, mla_q_norm, mla_kv_norm):
    v = np.zeros((128, NV), np.float32)
    v[:, BADA:BADA + 72] = _fm(b_ada, 72)
    v[:, N1C:N1C + 8] = _fm(ffn1_norm, 8)
    v[:, N2C:N2C + 8] = _fm(mix_norm, 8)
    v[:, N3C:N3C + 8] = _fm(ffn2_norm, 8)
    v[:, NFC:NFC + 8] = _fm(final_norm, 8)
    v[:, QNC:QNC + 2] = _fm(mla_q_norm, 2)
    v[:, KVNC:KVNC + 1] = _fm(mla_kv_norm, 1)
    v[:, CTC:CTC + 8] = _fm(c[b], 8)
    inv_freq = (np.float32(10000.0) ** (-np.arange(0, 32, 2, dtype=np.float32) / np.float32(32))).astype(np.float32)
    iv = (inv_freq.astype(np.float64) / (2 * np.pi)).astype(np.float32)
    v[0:16, IVFC] = iv
    v[16:32, IVFC] = iv
    v[0:16, SGNC] = -1.0
    v[16:32, SGNC] = 1.0
    return v


_CACHE = {}


def kernel(x, c, positions, w_ada, b_ada, ffn1_norm, ffn1_w_in, ffn1_w_out, mix_norm, w_in, mla_q_norm, mla_w_uq,
           mla_kv_norm, mla_w_ukv, rel_bias, w_branch_a, w_branch_b, w_out, ffn2_norm, ffn2_w_in, ffn2_w_out,
           final_norm):
    x = np.asarray(x, np.float32)
    c = np.asarray(c, np.float32)
    positions = np.asarray(positions, np.int32)
    B = x.shape[0]
    W = prep_weights(w_ada[0], ffn1_w_in[0], ffn1_w_out[0], w_in[0], mla_w_uq[0], mla_w_ukv[0], w_branch_a[0],
                     w_branch_b[0], w_out[0], ffn2_w_in[0], ffn2_w_out[0])
    bg = prep_bias(rel_bias[0])
    in_maps = []
    for b in range(B):
        in_maps.append({
            "xT": np.ascontiguousarray(x[b].T),
            "pos": np.ascontiguousarray(positions[b][None, :]),
            "vecs": prep_vecs(b, c, b_ada[0], ffn1_norm[0], mix_norm[0], ffn2_norm[0], final_norm, mla_q_norm[0],
                              mla_kv_norm[0]),
            "biasg": bg,
            "wsrc": W,
        })
    if "nc" not in _CACHE:
        _CACHE["nc"] = build_program()[0]
    nc = _CACHE["nc"]
    res = run_bass_kernel_spmd(nc, in_maps, core_ids=list(range(B)))
    out = np.stack([np.ascontiguousarray(res.results[b]["outT"].T) for b in range(B)], axis=0)
    return out.astype(np.float32)
```

```python
import contextlib
import numpy as np
import concourse.bass as bass
import concourse.mybir as mybir
from concourse.bass_utils import run_bass_kernel_spmd

F32 = mybir.dt.float32
BF16 = mybir.dt.bfloat16
I32 = mybir.dt.int32
AF = mybir.ActivationFunctionType
ALU = mybir.AluOpType

S = 4096
D = 1024
T = 512
NT = S // T
FF = 2816
JF = FF // 128
EPS = 1e-6
NSLOT = 3
NTMP = 4
G = 512

BADA, N1C, N2C, N3C, NFC, QNC, KVNC, CTC, IVFC, SGNC, NV = 0, 72, 80, 88, 96, 104, 106, 107, 115, 116, 120
A1, A2, A3, G1H, G2H, G3H = 0, 8, 16, 24, 32, 40
SH1, SC1, G1, SH2, SC2, G2, SH3, SC3, G3 = 0, 8, 16, 24, 32, 40, 48, 56, 64

U_ADA = 0
U_F1W1 = 18
U_F1W2 = 29
U_WIN = 37
U_MLA = 45
U_WA = 46
U_WB = 47
U_WO = 48
U_F2W1 = 50
U_F2W2 = 61
NU = 69

SCALE_MLA = 96.0 ** -0.5
SCALE_CA = 64.0 ** -0.5
TWO_PI_S = 2.0 * np.pi * (1.0 - 2e-6)


def unit_elems(u):
    if U_F1W2 <= u < U_F1W2 + 8 or U_F2W2 <= u < U_F2W2 + 8:
        return FF
    if u == U_MLA:
        return 3072
    return 4096


class View:
    __slots__ = ("ap", "keys")

    def __init__(self, ap, keys):
        self.ap = ap
        self.keys = keys


class Ten:
    def __init__(self, name, h, esz, idx=None):
        self.name = name
        self.h = h
        self.esz = esz
        self.idx = idx

    def keys(self, lo, hi):
        g0 = (lo * self.esz) // G
        g1 = (hi * self.esz - 1) // G
        return [(self.name, g) for g in range(g0, g1 + 1)]

    def v(self, lo, hi, p0=0, p1=128):
        return View(self.h[p0:p1, lo:hi], self.keys(lo, hi))


class Sub:
    def __init__(self, ten, off):
        self.ten = ten
        self.off = off

    def v(self, lo, hi, p0=0, p1=128):
        return self.ten.v(self.off + lo, self.off + hi, p0, p1)


class Prog:
    ENG = ("pe", "act", "dve", "pool", "sp")

    def __init__(self):
        self.streams = {e: [] for e in self.ENG}
        self.count = {e: 0 for e in ("pe", "act", "dve", "pool")}
        self.seen = {e: {} for e in self.ENG}
        self.lastw = {}
        self.readers = {}
        self.dmacount = {}
        self.know = {}

    def _deps(self, eng, reads, writes):
        deps = {}

        def need(tok, raw):
            if tok is None:
                return
            sk, val = tok
            if sk == eng and eng == "pe":
                return
            if deps.get(sk, 0) < val:
                deps[sk] = val

        lw = self.lastw
        for k in reads:
            need(lw.get(k), True)
        for k in writes:
            need(lw.get(k), False)
            rd = self.readers.get(k)
            if rd:
                for sk, val in rd.items():
                    need((sk, val), False)
        out = []
        seen = self.seen[eng]
        for sk, val in sorted(deps.items(), key=lambda kv: -kv[1]):
            if seen.get(sk, 0) >= val:
                continue
            seen[sk] = val
            out.append((sk, val))
            kn = self.know.get((sk, val))
            if kn:
                for k2, v2 in kn.items():
                    if seen.get(k2, 0) < v2:
                        seen[k2] = v2
        return out

    def _commit(self, tok, reads, writes):
        sk, val = tok
        for k in reads:
            d = self.readers.get(k)
            if d is None:
                d = self.readers[k] = {}
            if d.get(sk, 0) < val:
                d[sk] = val
        for k in writes:
            self.lastw[k] = tok
            self.readers[k] = {}

    def op(self, eng, fn, reads, writes):
        waits = self._deps(eng, reads, writes)
        self.count[eng] += 1
        tok = (eng, self.count[eng])
        self.know[tok] = dict(self.seen[eng])
        self._commit(tok, reads, writes)
        self.streams[eng].append((waits, fn, (eng, 1)))

    def dma(self, queue, fn, semkey, reads, writes):
        waits = self._deps(queue, reads, writes)
        self.dmacount[semkey] = self.dmacount.get(semkey, 0) + 16
        tok = (semkey, self.dmacount[semkey])
        self.know[tok] = dict(self.seen[queue])
        self._commit(tok, reads, writes)
        self.streams[queue].append((waits, fn, (semkey, 16)))
        return tok

    def final_wait(self, eng, toks):
        waits = []
        for sk, val in toks:
            if self.seen[eng].get(sk, 0) < val:
                self.seen[eng][sk] = val
                waits.append((sk, val))
        self.streams[eng].append((waits, None, None))


def build_program(ntiles=NT, stop=None):
    nc = bass.Bass("TRN2", target_bir_lowering=False)
    xT = nc.dram_tensor("xT", [D, S], F32, kind="ExternalInput").ap()
    pos = nc.dram_tensor("pos", [1, S], I32, kind="ExternalInput").ap()
    vecs = nc.dram_tensor("vecs", [128, NV], F32, kind="ExternalInput").ap()
    biasg = nc.dram_tensor("biasg", [128, 8 * 640], F32, kind="ExternalInput").ap()
    wsrc = nc.dram_tensor("wsrc", [NU, 128, 4096], F32, kind="ExternalInput").ap()
    wbf = nc.dram_tensor("wbf", [NU, 128, 4096], BF16, kind="Internal").ap()
    outT = nc.dram_tensor("outT", [D, S], F32, kind="ExternalOutput").ap()

    pr = Prog()
    with contextlib.ExitStack() as es:
        def sb(name, shape, dt):
            h = es.enter_context(nc.sbuf_tensor(name, shape, dt))
            esz = 4 if dt in (F32, I32) else 2
            return Ten(name, h, esz)

        ring = [sb("ring%d" % i, [128, 4096], BF16) for i in range(NSLOT)]
        X = sb("X", [128, 4096], F32)
        hT = sb("hT", [128, 4096], BF16)
        sqm = sb("sqm", [128, 4096], BF16)
        arena = sb("arena", [128, 11264], BF16)
        oA = sb("oA", [128, 2048], BF16)
        oB = sb("oB", [128, 2048], BF16)
        pbMt = sb("pbMt", [128, 2048], BF16)
        pbM = [Sub(pbMt, i * 512) for i in range(4)]
        pbC = [sb("pbC%d" % i, [128, 512], BF16) for i in range(4)]
        ckv = sb("ckv", [128, S], BF16)
        kpe = sb("kpe", [128, S], BF16)
        VE = sb("VE", [128, 32 * 768], BF16)
        cak = sb("cak", [128, 4096], BF16)
        VEc = sb("VEc", [128, 8 * 768], BF16)
        Eb = sb("Eb", [128, 8 * 640], BF16)
        tmps = [sb("tmp%d" % i, [128, 640], F32) for i in range(NTMP)]
        for i, tm in enumerate(tmps):
            tm.idx = i
        rs = sb("rs", [128, 512], F32)
        cs = sb("cs", [128, 1024], F32)
        cosT = Sub(cs, 0)
        sinS = Sub(cs, 512)
        negh = sb("negh", [128, 1], F32)
        epsb = sb("epsb", [128, 1], F32)
        warm = sb("warm", [128, 1], F32)
        ones_f = sb("ones_f", [128, 1], F32)
        vec = sb("vec", [128, NV], F32)
        adaT = sb("adaT", [128, 72], F32)
        dv = sb("dv", [128, 48], F32)
        cact = sb("cact", [128, 8], BF16)
        ones = sb("ones", [128, 128], BF16)
        identb = sb("identb", [128, 128], BF16)
        mk = sb("mk", [128, 256], BF16)

        PS = []
        for i in range(8):
            h = es.enter_context(nc.psum_tensor("ps%d" % i, [128, 512], F32))
            PS.append(Ten("ps%d" % i, h, 4))

        semnames = ["pe", "act", "dve", "pool", "misc", "pos"]
        semnames += ["ring%d" % i for i in range(NSLOT)]
        semnames += ["rgp%d" % i for i in range(NSLOT)] + ["wb%d" % i for i in range(NSLOT)]
        semnames += ["x%d" % k for k in range(8)]
        semnames += ["tmp%d" % i for i in range(NTMP)]
        semnames += ["ost%d" % i for i in range(8)]
        SEM = {n: es.enter_context(nc.semaphore(n)) for n in semnames}

        def ACT(out, in_, func, scale=1.0, bias=None):
            reads = list(in_.keys)
            sc = scale
            if isinstance(scale, View):
                reads += scale.keys
                sc = scale.ap
            bi = bias
            if isinstance(bias, View):
                reads += bias.keys
                bi = bias.ap
            oa, ia = out.ap, in_.ap

            def fn(e):
                if bi is None:
                    return e.activation(out=oa, in_=ia, func=func, scale=sc)
                return e.activation(out=oa, in_=ia, func=func, scale=sc, bias=bi)
            pr.op("act", fn, reads, out.keys)

        def TT(out, a, b, op, eng="dve"):
            oa, aa, ba = out.ap, a.ap, b.ap
            pr.op(eng, lambda e: e.tensor_tensor(out=oa, in0=aa, in1=ba, op=op), a.keys + b.keys, out.keys)

        def TS(out, a, s1, s2, op0, op1=None):
            reads = list(a.keys)
            v1, v2 = s1, s2
            if isinstance(s1, View):
                reads += s1.keys
                v1 = s1.ap
            if isinstance(s2, View):
                reads += s2.keys
                v2 = s2.ap
            oa, aa = out.ap, a.ap

            def fn(e):
                if op1 is None:
                    return e.tensor_scalar(out=oa, in0=aa, scalar1=v1, scalar2=None, op0=op0)
                return e.tensor_scalar(out=oa, in0=aa, scalar1=v1, scalar2=v2, op0=op0, op1=op1)
            pr.op("dve", fn, reads, out.keys)

        def STT(out, a, s, b, op0, op1):
            reads = a.keys + b.keys
            sv = s
            if isinstance(s, View):
                reads = reads + s.keys
                sv = s.ap
            oa, aa, ba = out.ap, a.ap, b.ap
            pr.op("dve", lambda e: e.scalar_tensor_tensor(out=oa, in0=aa, scalar=sv, in1=ba, op0=op0, op1=op1),
                  reads, out.keys)

        def COPY(out, a, eng="dve"):
            oa, aa = out.ap, a.ap
            pr.op(eng, lambda e: e.tensor_copy(out=oa, in_=aa), a.keys, out.keys)

        def ACOPY(out, a):
            oa, aa = out.ap, a.ap
            pr.op("act", lambda e: e.copy(out=oa, in_=aa), a.keys, out.keys)

        def RECIP(out, a):
            oa, aa = out.ap, a.ap
            pr.op("dve", lambda e: e.reciprocal(out=oa, in_=aa), a.keys, out.keys)

        def MEMSET(out, val, eng="dve"):
            oa = out.ap
            pr.op(eng, lambda e: e.memset(oa, val), [], out.keys)

        def POW(out, a):
            oa, aa = out.ap, a.ap
            ba = negh.h[:, 0:1].to_broadcast([128, 512])
            pr.op("pool", lambda e: e.tensor_tensor(out=oa, in0=aa, in1=ba, op=ALU.pow),
                  a.keys + negh.keys(0, 1), out.keys)

        def MM(out, lhsT, rhs, start, stop, skip=False, also=None, alsor=None):
            oa, la, ra = out.ap, lhsT.ap, rhs.ap
            if also:
                out = View(out.ap, out.keys + also)
            if alsor:
                rhs = View(rhs.ap, rhs.keys + alsor)
            if skip:
                pr.op("pe", lambda e: e.matmul(oa, lhsT=la, rhs=ra, start=start, stop=stop, skip_group_check=True),
                      lhsT.keys + rhs.keys, out.keys)
            else:
                pr.op("pe", lambda e: e.matmul(oa, lhsT=la, rhs=ra, start=start, stop=stop),
                      lhsT.keys + rhs.keys, out.keys)

        def DMA(queue, out_ap, in_ap, semkey, reads, writes):
            return pr.dma(queue, lambda e: e.dma_start(out=out_ap, in_=in_ap), semkey, reads, writes)

        tmp_ctr = [0]

        def tmp():
            t_ = tmps[tmp_ctr[0] % NTMP]
            tmp_ctr[0] += 1
            return t_

        def Xk(k):
            return X.v(k * 512, (k + 1) * 512)

        def hTk(k):
            return hT.v(k * 512, (k + 1) * 512)

        def sqk(k):
            return sqm.v(k * 512, (k + 1) * 512)

        def actk(j):
            return arena.v(j * 512, (j + 1) * 512)

        QABS0, QPE0, CAQ0, CQN0 = 0, 4096, 8192, 10240

        stg = []
        for tn in (oA, oB, pbMt):
            for c in range(2):
                stg.append(View(tn.h[:, c * 1024:(c + 1) * 1024].bitcast(F32), tn.keys(c * 1024, (c + 1) * 1024)))
        for c in range(2):
            stg.append(cs.v(c * 512, (c + 1) * 512))

        per_tile = ([U_F1W1 + i for i in range(11)] + [U_F1W2 + m for m in range(8)]
                    + [U_WIN + 0, U_WIN + 1, U_WIN + 2, U_WIN + 3, U_MLA,
                       U_WIN + 4, U_WA, U_WIN + 5, U_WIN + 6, U_WB, U_WIN + 7, U_WO, U_WO + 1]
                    + [U_F2W1 + i for i in range(11)] + [U_F2W2 + m for m in range(8)])
        stream = [U_ADA + a for a in range(4)]
        for i in range(11):
            stream += [U_F1W1 + i, U_ADA + 4 + i]
        for m in range(8):
            stream += [U_F1W2 + m] + ([U_ADA + 15 + m] if m < 3 else [])
        stream += per_tile[19:]
        for _ in range(ntiles - 1):
            stream += per_tile
        st = {"cons": 0, "issued": 0}

        def next_unit(expect=None, live_before=0):
            while st["issued"] < min(len(stream), st["cons"] - live_before + NSLOT):
                n = st["issued"]
                u = stream[n]
                E = unit_elems(u)
                s_ = n % NSLOT
                if n < 18 + 2 * len(per_tile) and ntiles >= 2:
                    DMA("pool", ring[s_].h[:, 0:E], wsrc[u][:, 0:E], "rgp%d" % s_, [], ring[s_].keys(0, E))
                    if n >= 18 + len(per_tile):
                        DMA("sp", wbf[u][:, 0:E], ring[s_].h[:, 0:E], "wb%d" % s_, ring[s_].keys(0, E), [("wbf", u)])
                elif n < 18 + len(per_tile):
                    DMA("pool", ring[s_].h[:, 0:E], wsrc[u][:, 0:E], "rgp%d" % s_, [], ring[s_].keys(0, E))
                else:
                    DMA("sp", ring[s_].h[:, 0:E], wbf[u][:, 0:E], "ring%d" % s_, [("wbf", u)], ring[s_].keys(0, E))
                st["issued"] += 1
            n = st["cons"]
            if expect is not None:
                assert stream[n] == expect, (stream[n], expect)
            st["cons"] += 1
            return ring[n % NSLOT]


        DMA("sp", vec.h[:, :], vecs, "misc", [], vec.keys(0, NV))
        MEMSET(ones.v(0, 128), 1.0)
        MEMSET(identb.v(0, 128), 0.0, eng="pool")
        _ia = identb.h[:, :]
        pr.op("pool", lambda e: e.affine_select(out=_ia, in_=_ia, pattern=[[-1, 128]], compare_op=ALU.not_equal,
                                                fill=1.0, base=0, channel_multiplier=1),
              identb.keys(0, 128), identb.keys(0, 128))
        MEMSET(negh.v(0, 1), -0.5, eng="pool")
        MEMSET(epsb.v(0, 1), EPS)
        MEMSET(ones_f.v(0, 1), 1.0)
        MEMSET(mk.v(0, 256), 0.0)
        MEMSET(mk.v(64, 128, 0, 1), 1.0)
        MEMSET(mk.v(128, 192, 0, 1), -30000.0)
        MEMSET(kpe.v(0, S), 0.0)
        for c0 in range(0, 32 * 768, 4096):
            MEMSET(VE.v(c0, c0 + 4096), 1.0)
        MEMSET(VEc.v(0, 4096), 1.0)
        MEMSET(VEc.v(4096, 8 * 768), 1.0)

        for h in range(8):
            tb = tmp()
            DMA("sp", tb.h[:, 0:640], biasg[:, h * 640:(h + 1) * 640], "tmp%d" % tb.idx, [], tb.keys(0, 640))
            ACT(Eb.v(h * 640, (h + 1) * 640), tb.v(0, 640), AF.Identity, scale=1.0 / SCALE_CA)
            MEMSET(Eb.v(h * 640 + 512 + 64, h * 640 + 640, 0, 64), -30000.0)
            MEMSET(Eb.v(h * 640, h * 640 + 64, 64, 128), -30000.0)

        for k in range(8):
            DMA("sp", X.h[:, k * 512:(k + 1) * 512], xT[k * 128:(k + 1) * 128, 0:512], "x%d" % k, [], Xk(k).keys)

        ACT(cact.v(0, 8), vec.v(CTC, CTC + 8), AF.Silu)

        def ada_unit(a):
            R = next_unit(U_ADA + a)
            for cc in range(4):
                c = 4 * a + cc
                for k in range(8):
                    b0 = (cc * 8 + k) * 128
                    MM(PS[7].v(c, c + 1), R.v(b0, b0 + 128), cact.v(k, k + 1), k == 0, k == 7)
            if a == 3:
                TT(adaT.v(0, 16), PS[7].v(0, 16), vec.v(BADA, BADA + 16), ALU.add)
                STT(dv.v(A1, A1 + 8), adaT.v(SC1, SC1 + 8), 1.0, vec.v(N1C, N1C + 8), ALU.add, ALU.mult)
            elif a == 5:
                TT(adaT.v(16, 24), PS[7].v(16, 24), vec.v(BADA + 16, BADA + 24), ALU.add)
                TS(dv.v(G1H, G1H + 8), adaT.v(G1, G1 + 8), 0.5, None, ALU.mult)
            elif a == 17:
                TT(adaT.v(24, 72), PS[7].v(24, 72), vec.v(BADA + 24, BADA + 72), ALU.add)
                for (acol, sccol, ncol) in ((A2, SC2, N2C), (A3, SC3, N3C)):
                    STT(dv.v(acol, acol + 8), adaT.v(sccol, sccol + 8), 1.0, vec.v(ncol, ncol + 8), ALU.add, ALU.mult)
                for (gcol, src) in ((G2H, G2), (G3H, G3)):
                    TS(dv.v(gcol, gcol + 8), adaT.v(src, src + 8), 0.5, None, ALU.mult)

        for a in range(4):
            ada_unit(a)

        if stop == "ada":
            ntiles = 0
            o_ = tmp()
            MEMSET(o_.v(0, 512), 0.0)
            COPY(o_.v(0, 72), adaT.v(0, 72))
            COPY(o_.v(72, 120), dv.v(0, 48))
            DMA("sp", outT[0:128, 0:512], o_.h[:, 0:512], "ost%d" % o_.idx, o_.keys(0, 512), [("out", 0, 0)])
            o2 = tmp()
            COPY(o2.v(0, 512), Eb.v(0, 512))
            DMA("sp", outT[128:256, 0:512], o2.h[:, 0:512], "ost%d" % o2.idx, o2.keys(0, 512), [("out", 0, 1)])

        def stats_rstd(nchunks, sq_of, inv_n, ps_bank, rs_t):
            for k in range(nchunks):
                MM(ps_bank.v(0, 512), ones.v(0, 128), sq_of(k), k == 0, k == nchunks - 1)
            sd = tmp()
            ACT(sd.v(0, 512), ps_bank.v(0, 512), AF.Ln, scale=inv_n, bias=epsb.v(0, 1))
            ACT(rs_t.v(0, 512), sd.v(0, 512), AF.Exp, scale=-0.5)

        def prewarm_ln():
            ACT(warm.v(0, 1), ones_f.v(0, 1), AF.Ln)

        def norm_sq(src=Xk):
            for k in range(8):
                ACT(sqk(k), src(k), AF.Square)

        def norm_apply(acol, bcol, src=Xk):
            stats_rstd(8, sqk, 1.0 / D, PS[6], rs)
            for k in range(8):
                t_ = tmp()
                STT(t_.v(0, 512), src(k), dv.v(acol + k, acol + k + 1), rs.v(0, 512), ALU.mult, ALU.mult)
                ACT(hTk(k), t_.v(0, 512), AF.Identity, bias=adaT.v(bcol + k, bcol + k + 1))

        def norm_mod(acol, bcol, src=Xk):
            norm_sq(src)
            norm_apply(acol, bcol, src)

        def ffn(u_w1, u_w2, ghcol, res=Xk, hooks=None, mid_hook=None, mhooks=None):
            R = None
            hooks = hooks or {}
            mhooks = mhooks or {}
            R = next_unit(u_w1)
            for k in range(8):
                for jj in range(2):
                    for half in range(2):
                        b0 = (jj * 8 + k) * 256 + 128 * half
                        MM(PS[2 * jj + half].v(0, 512), R.v(b0, b0 + 128), hTk(k), k == 0, k == 7)
            for jj in range(2):
                sg = tmp()
                ACT(sg.v(0, 512), PS[2 * jj].v(0, 512), AF.Silu)
                TT(actk(jj), sg.v(0, 512), PS[2 * jj + 1].v(0, 512), ALU.mult)
            for j in range(2, JF):
                if j in hooks:
                    hooks[j]()
                if j % 2 == 0:
                    R = next_unit(u_w1 + j // 2)
                jj = j % 2
                pg, pu = PS[2 * jj], PS[2 * jj + 1]
                for k in range(8):
                    b0 = (jj * 8 + k) * 256
                    MM(pg.v(0, 512), R.v(b0, b0 + 128), hTk(k), k == 0, k == 7,
                       also=(pu.keys(0, 512) if k == 0 else None))
                for k in range(8):
                    b0 = (jj * 8 + k) * 256 + 128
                    MM(pu.v(0, 512), R.v(b0, b0 + 128), hTk(k), k == 0, k == 7)
                sg = tmp()
                ACT(sg.v(0, 512), pg.v(0, 512), AF.Silu)
                TT(actk(j), sg.v(0, 512), pu.v(0, 512), ALU.mult)
            prewarm_ln()
            if mid_hook is not None:
                mid_hook()
            for m in range(8):
                if m in mhooks:
                    mhooks[m]()
                R = next_unit(u_w2 + m)
                py = PS[4 + m % 2]
                for j in range(JF):
                    MM(py.v(0, 512), R.v(j * 128, (j + 1) * 128), actk(j), j == 0, j == JF - 1)
                STT(Xk(m), py.v(0, 512), dv.v(ghcol + m, ghcol + m + 1), res(m), ALU.mult, ALU.add)

        def final_norm(t):
            stats_rstd(8, sqk, 1.0 / D, PS[6], rs)
            for k in range(8):
                STT(Xk(k), Xk(k), vec.v(NFC + k, NFC + k + 1), rs.v(0, 512), ALU.mult, ALU.mult)
                DMA("sp", outT[k * 128:(k + 1) * 128, t * 512:(t + 1) * 512], X.h[:, k * 512:(k + 1) * 512],
                    "ost%d" % k, Xk(k).keys, [("out", t, k)])

        def rope_tables(t):
            a, b, c, d = tmps[0], tmps[1], tmps[2], tmps[3]
            tmp_ctr[0] = 0
            posi = View(a.h[0:32, 0:512].bitcast(I32), a.keys(0, 512))
            DMA("pool", posi.ap, pos[0:1, t * 512:(t + 1) * 512].partition_broadcast(32), "pos", [], posi.keys)
            u_ = b.v(0, 512, 0, 32)
            TS(u_, posi, vec.v(IVFC, IVFC + 1, 0, 32), None, ALU.mult)
            ki = View(c.h[0:32, 0:512].bitcast(I32), c.keys(0, 512))
            COPY(ki, u_)
            f_ = d.v(0, 512, 0, 32)
            TT(f_, u_, ki, ALU.subtract)
            s_ = c.v(0, 512, 0, 32)
            ACT(s_, f_, AF.Sin, scale=TWO_PI_S)
            TS(sinS.v(0, 512, 0, 32), s_, vec.v(SGNC, SGNC + 1, 0, 32), None, ALU.mult)
            v_ = a.v(0, 512, 0, 32)
            TS(v_, u_, 0.25, None, ALU.add)
            COPY(ki, v_)
            TT(f_, v_, ki, ALU.subtract)
            ACT(cosT.v(0, 512, 0, 32), f_, AF.Sin, scale=TWO_PI_S)

        def rope_apply(out, psA, psB):
            t1, t2 = tmp(), tmp()
            TT(t1.v(0, 512, 0, 32), psA, cosT.v(0, 512, 0, 32), ALU.mult)
            TT(t2.v(0, 512, 0, 32), psB, sinS.v(0, 512, 0, 32), ALU.mult)
            TT(out, t1.v(0, 512, 0, 32), t2.v(0, 512, 0, 32), ALU.add)

        def ve_write(dst, base, ps_bank):
            src = ps_bank.h[:, 0:512].rearrange("p (i hh v) -> p i hh v", hh=2, v=64)
            dview = dst.h[:, base:base + 768].rearrange("p (i c) -> p i c", c=192)
            keys_r = ps_bank.keys(0, 512)
            keys_w = dst.keys(base, base + 768)
            COPY(View(dview[:, :, 0:64], keys_w), View(src[:, :, 0, :], keys_r))
            oa, ia = dview[:, :, 128:192], src[:, :, 1, :]
            pr.op("act", lambda e: e.copy(out=oa, in_=ia), keys_r, keys_w)

        def normalise_pair(i, o_t, on_dve=False):
            pe_, po_ = PS[4 + 2 * (i % 2)], PS[5 + 2 * (i % 2)]
            if on_dve:
                t1 = tmp()
                RECIP(t1.v(0, 512, 64, 128), pe_.v(0, 512, 64, 128))
                TT(o_t.v(i * 512, (i + 1) * 512, 0, 64), pe_.v(0, 512, 0, 64), t1.v(0, 512, 64, 128), ALU.mult)
                t2 = tmp()
                RECIP(t2.v(0, 512, 0, 64), po_.v(0, 512, 0, 64))
                TT(o_t.v(i * 512, (i + 1) * 512, 64, 128), po_.v(0, 512, 64, 128), t2.v(0, 512, 0, 64), ALU.mult)
                return
            t1 = tmp()
            ACT(t1.v(0, 512, 64, 128), pe_.v(0, 512, 64, 128), AF.Ln)
            ACT(t1.v(0, 512, 64, 128), t1.v(0, 512, 64, 128), AF.Exp, scale=-1.0)
            TT(o_t.v(i * 512, (i + 1) * 512, 0, 64), pe_.v(0, 512, 0, 64), t1.v(0, 512, 64, 128), ALU.mult)
            t2 = tmp()
            ACT(t2.v(0, 512, 0, 64), po_.v(0, 512, 0, 64), AF.Ln)
            ACT(t2.v(0, 512, 0, 64), t2.v(0, 512, 0, 64), AF.Exp, scale=-1.0)
            TT(o_t.v(i * 512, (i + 1) * 512, 64, 128), po_.v(0, 512, 64, 128), t2.v(0, 512, 0, 64), ALU.mult)

        for t in range(ntiles):
            tc0, tc1 = t * 512, (t + 1) * 512

            if t == 0:
                norm_mod(A1, SH1)
                h0 = {j: (lambda a=3 + j // 2: ada_unit(a)) for j in range(2, JF, 2)}
                ffn(U_F1W1, U_F1W2, G1H, hooks=h0, mid_hook=(lambda: ada_unit(14)),
                    mhooks={m: (lambda a=14 + m: ada_unit(a)) for m in (1, 2, 3)})
            else:
                ffn(U_F1W1, U_F1W2, G1H, res=(lambda m: stg[m]),
                    hooks={2: norm_sq, 6: (lambda tt=t - 1: final_norm(tt))})
            if stop == "ffn1":
                break

            norm_mod(A2, SH2)
            rope_tables(t)
            R = next_unit(U_WIN + 0)
            for k in range(8):
                for (bank, c0, mcols) in ((0, 0, 128), (1, 128, 128), (2, 256, 128), (3, 384, 32), (4, 416, 32)):
                    MM(PS[bank].v(0, 512, 0, mcols), R.v(k * 512 + c0, k * 512 + c0 + mcols), hTk(k), k == 0, k == 7)
            ACT(sqk(0), PS[0].v(0, 512), AF.Square)
            ACT(sqk(1), PS[1].v(0, 512), AF.Square)
            stats_rstd(2, sqk, 1.0 / 256, PS[6], rs)
            for kk in range(2):
                STT(arena.v(CQN0 + kk * 512, CQN0 + (kk + 1) * 512), PS[kk].v(0, 512),
                    vec.v(QNC + kk, QNC + kk + 1), rs.v(0, 512), ALU.mult, ALU.mult)
            ACT(sqk(2), PS[2].v(0, 512), AF.Square)
            rs2 = tmp()
            stats_rstd(1, lambda k: sqk(2), 1.0 / 128, PS[6], rs2)
            STT(ckv.v(tc0, tc1), PS[2].v(0, 512), vec.v(KVNC, KVNC + 1), rs2.v(0, 512), ALU.mult, ALU.mult)
            rope_apply(kpe.v(tc0, tc1, 0, 32), PS[3].v(0, 512, 0, 32), PS[4].v(0, 512, 0, 32))

            R = next_unit(U_WIN + 1)
            for i in range(4):
                pq = PS[5 + 2 * (i % 2)]
                for k in range(8):
                    MM(pq.v(0, 512), R.v(k * 512 + i * 128, k * 512 + (i + 1) * 128), hTk(k), k == 0, k == 7)
                ACOPY(arena.v(CAQ0 + i * 512, CAQ0 + (i + 1) * 512), pq.v(0, 512))
            R = next_unit(U_WIN + 2)
            for i in range(4):
                pk = PS[5 + 2 * (i % 2)]
                for k in range(8):
                    MM(pk.v(0, 512), R.v(k * 512 + i * 128, k * 512 + (i + 1) * 128), hTk(k), k == 0, k == 7)
                cb = i * 1024 + (t % 2) * 512
                COPY(cak.v(cb, cb + 512), pk.v(0, 512))
            R = next_unit(U_WIN + 3)
            for sub in range(4):
                pv = PS[5 + 2 * (sub % 2)]
                for k in range(8):
                    MM(pv.v(0, 512), hT.v(k * 512 + sub * 128, k * 512 + (sub + 1) * 128), R.v(k * 512, (k + 1) * 512),
                       k == 0, k == 7)
                ve_write(VEc, ((4 * t + sub) % 8) * 768, pv)

            R = next_unit(U_MLA)
            UQN, UQPE, UKT, UV = 0, 1024, 2048, 2560
            cqn = lambda kk: arena.v(CQN0 + kk * 512, CQN0 + (kk + 1) * 512)
            for i in range(4):
                pn = PS[i]
                for kk in range(2):
                    b0 = UQN + kk * 512 + i * 128
                    MM(pn.v(0, 512), R.v(b0, b0 + 128), cqn(kk), kk == 0, kk == 1)
                ACOPY(pbM[i].v(0, 512), pn.v(0, 512))
            for h in range(8):
                pA, pB = PS[4 + 2 * (h % 2)], PS[5 + 2 * (h % 2)]
                for (pp, off) in ((pA, 0), (pB, 32)):
                    for kk in range(2):
                        b0 = UQPE + kk * 512 + h * 64 + off
                        MM(pp.v(0, 512, 0, 32), R.v(b0, b0 + 32), cqn(kk), kk == 0, kk == 1)
                rope_apply(arena.v(QPE0 + h * 512, QPE0 + (h + 1) * 512, 0, 32),
                           pA.v(0, 512, 0, 32), pB.v(0, 512, 0, 32))
            for i in range(4):
                qn = pbM[i]
                for hh in range(2):
                    h = 2 * i + hh
                    pa = PS[(2 * i + hh) % 4]
                    MM(pa.v(0, 512), R.v(UKT + i * 128, UKT + (i + 1) * 128, 64 * hh, 64 * hh + 64),
                       qn.v(0, 512, 64 * hh, 64 * hh + 64), True, True)
                    if hh == 0:
                        COPY(arena.v(QABS0 + h * 512, QABS0 + (h + 1) * 512), pa.v(0, 512))
                    else:
                        ACOPY(arena.v(QABS0 + h * 512, QABS0 + (h + 1) * 512), pa.v(0, 512))
            for sub in range(4):
                pv = PS[sub % 2]
                MM(pv.v(0, 512), ckv.v(tc0 + sub * 128, tc0 + (sub + 1) * 128), R.v(UV, UV + 512), True, True)
                ve_write(VE, (4 * t + sub) * 768, pv)

            items = [(i, hh, kt) for i in range(4) for hh in range(2) for kt in range(4 * t + 4)]
            nk = 4 * t + 4

            def mla_S(n):
                i, hh, kt = items[n]
                h = 2 * i + hh
                j = kt - 4 * t
                q0 = 128 * j if j >= 0 else 0
                ps = PS[n % 4]
                MM(ps.v(q0, 512), ckv.v(kt * 128, (kt + 1) * 128),
                   arena.v(QABS0 + h * 512 + q0, QABS0 + (h + 1) * 512), True, False)
                MM(ps.v(q0, 512), kpe.v(kt * 128, (kt + 1) * 128),
                   arena.v(QPE0 + h * 512 + q0, QPE0 + (h + 1) * 512), False, j < 0)
                if j >= 0:
                    MM(ps.v(q0, q0 + 128), mk.v(0, 128), mk.v(128, 256), False, True)
                ACT(pbM[n % 4].v(q0, 512), ps.v(q0, 512), AF.Exp, scale=SCALE_MLA)

            def mla_PV(n, nxt=False):
                i, hh, kt = items[n]
                j = kt - 4 * t
                q0 = 128 * j if j >= 0 else 0
                po = PS[4 + 2 * (i % 2) + hh]
                c0 = kt * 768 + i * 192 + 64 * hh
                MM(po.v(q0, 512), VE.v(c0, c0 + 128), pbM[n % 4].v(q0, 512), kt == 0, kt == nk - 1,
                   alsor=(pbM[(n + 1) % 4].v(0, 512).keys if nxt else None))
                if kt == nk - 1 and hh == 1:
                    normalise_pair(i, oA, on_dve=(i < 3))

            LA = 4
            for n in range(min(LA, len(items))):
                mla_S(n)
            for n in range(0, len(items), 2):
                two = n + 1 < len(items)
                mla_PV(n, nxt=two)
                if two:
                    mla_PV(n + 1)
                for q_ in (n + LA, n + LA + 1):
                    if q_ < len(items):
                        mla_S(q_)

            kt_lo = max(0, 4 * t - 4)
            citems = [(i, hh, Kt) for i in range(4) for hh in range(2) for Kt in range(kt_lo, 4 * t + 4)]

            for h in range(8):
                i_, hh_ = h // 2, h % 2
                src = arena.v(CAQ0 + i_ * 512, CAQ0 + (i_ + 1) * 512, 64 * hh_, 64 * hh_ + 64)
                COPY(arena.v(QABS0 + h * 512, QABS0 + (h + 1) * 512, 64 * hh_, 64 * hh_ + 64), src, eng="pool")
                MEMSET(arena.v(QABS0 + h * 512, QABS0 + (h + 1) * 512, 64 * (1 - hh_), 64 * (1 - hh_) + 64), 0.0,
                       eng="pool")

            def ca_rng(Kt):
                dk = Kt - 4 * t
                s0, s1 = max(0, dk), min(3, dk + 4)
                return dk, 128 * s0, 128 * (s1 + 1)

            def ca_S(n):
                i, hh, Kt = citems[n]
                h = 2 * i + hh
                dk, c0, c1 = ca_rng(Kt)
                w = Kt % 8
                ps = PS[n % 4]
                kb = i * 1024 + w * 128
                MM(ps.v(c0, c1), cak.v(kb, kb + 128),
                   arena.v(QABS0 + h * 512 + c0, QABS0 + h * 512 + c1), True, False)
                e0 = h * 640 + c0 - 128 * dk
                MM(ps.v(c0, c1), identb.v(0, 128), Eb.v(e0, e0 + (c1 - c0)), False, True)
                ACT(pbC[n % 4].v(c0, c1), ps.v(c0, c1), AF.Exp, scale=SCALE_CA)

            def ca_PV(n, nxt=False):
                i, hh, Kt = citems[n]
                dk, c0, c1 = ca_rng(Kt)
                w = Kt % 8
                po = PS[4 + 2 * (i % 2) + hh]
                v0 = w * 768 + i * 192 + 64 * hh
                MM(po.v(c0, c1), VEc.v(v0, v0 + 128), pbC[n % 4].v(c0, c1), Kt == kt_lo, Kt == 4 * t + 3, skip=True,
                   alsor=(pbC[(n + 1) % 4].keys(0, 512) if nxt else None))
                if Kt == 4 * t + 3 and hh == 1:
                    normalise_pair(i, oB)

            for n in range(min(LA, len(citems))):
                ca_S(n)
            for n in range(0, len(citems), 2):
                two = n + 1 < len(citems)
                ca_PV(n, nxt=two)
                if two:
                    ca_PV(n + 1)
                for q_ in (n + LA, n + LA + 1):
                    if q_ < len(citems):
                        ca_S(q_)

            if stop == "attn":
                for q_, o_t in enumerate((oA, oB)):
                    for i in range(4):
                        o_ = tmp()
                        COPY(o_.v(0, 512), o_t.v(i * 512, (i + 1) * 512))
                        DMA("sp", outT[q_ * 128:(q_ + 1) * 128, i * 512:(i + 1) * 512], o_.h[:, 0:512],
                            "ost%d" % o_.idx, o_.keys(0, 512), [("out", q_, i)])
                for q_, src in enumerate((arena.v(QABS0, QABS0 + 512), arena.v(QPE0, QPE0 + 512), ckv.v(0, 512), kpe.v(0, 512),
                                          arena.v(CAQ0, CAQ0 + 512), cak.v(0, 512))):
                    o_ = tmp()
                    COPY(o_.v(0, 512), src)
                    DMA("sp", outT[(2 + q_) * 128:(3 + q_) * 128, 0:512], o_.h[:, 0:512],
                        "ost%d" % o_.idx, o_.keys(0, 512), [("out", 2 + q_, 0)])
                break

            mgk = sqk
            for phase, (ug0, uw, ug1, o_t) in enumerate(((U_WIN + 4, U_WA, U_WIN + 5, oA), (U_WIN + 6, U_WB, U_WIN + 7, oB))):
                Rg = Rw = None
                for m in range(8):
                    if m == 0:
                        Rg = next_unit(ug0)
                        Rw = next_unit(uw, live_before=1)
                    if m == 4:
                        Rg = next_unit(ug1, live_before=1)
                    pg, py = PS[m % 2], PS[2 + m % 2]
                    for k in range(8):
                        b0 = k * 512 + (m % 4) * 128
                        MM(pg.v(0, 512), Rg.v(b0, b0 + 128), hTk(k), k == 0, k == 7)
                    for i in range(4):
                        b0 = i * 1024 + m * 128
                        MM(py.v(0, 512), Rw.v(b0, b0 + 128), o_t.v(i * 512, (i + 1) * 512), i == 0, i == 3)
                    th = tmp()
                    ACT(th.v(0, 512), pg.v(0, 512), AF.Tanh, scale=0.5)
                    if phase == 0:
                        STT(mgk(m), th.v(0, 512), 1.0, py.v(0, 512), ALU.add, ALU.mult)
                    else:
                        u2 = tmp()
                        STT(u2.v(0, 512), th.v(0, 512), 1.0, py.v(0, 512), ALU.add, ALU.mult)
                        TT(mgk(m), mgk(m), u2.v(0, 512), ALU.add)
            prewarm_ln()
            R = None
            for mp in range(8):
                if mp % 4 == 0:
                    R = next_unit(U_WO + mp // 4)
                po = PS[4 + mp % 2]
                for m in range(8):
                    b0 = m * 512 + (mp % 4) * 128
                    MM(po.v(0, 512), R.v(b0, b0 + 128), mgk(m), m == 0, m == 7)
                STT(Xk(mp), po.v(0, 512), dv.v(G2H + mp, G2H + mp + 1), Xk(mp), ALU.mult, ALU.add)
            if stop == "mix":
                break

            if t + 1 < ntiles:
                for k in range(8):
                    DMA("sp", stg[k].ap, xT[k * 128:(k + 1) * 128, tc1:tc1 + 512], "x%d" % k, [], stg[k].keys)

            norm_mod(A3, SH3)
            if t + 1 < ntiles:
                stg_src = lambda k: stg[k]
                ffn(U_F2W1, U_F2W2, G3H, hooks={14: (lambda: norm_sq(stg_src))},
                    mid_hook=(lambda: norm_apply(A1, SH1, src=stg_src)))
            else:
                ffn(U_F2W1, U_F2W2, G3H)
                norm_sq()
                final_norm(t)


        if stop is not None and stop not in ("ada", "attn"):
            for k in range(8):
                o_ = tmp()
                COPY(o_.v(0, 512), Xk(k))
                DMA("sp", outT[k * 128:(k + 1) * 128, 0:512], o_.h[:, 0:512], "ost%d" % o_.idx,
                    o_.keys(0, 512), [("out", 0, k)])

        pr.final_wait("sp", [("ost%d" % i, pr.dmacount.get("ost%d" % i, 0)) for i in range(8)])
        for e in ("pe", "act", "dve", "pool"):
            assert pr.count[e] < 60000, (e, pr.count[e])

        with nc.Block() as block:
            def replay(name):
                def run(e):
                    for waits, fn, inc in pr.streams[name]:
                        for sk, val in waits:
                            e.wait_ge(SEM[sk], val)
                        if fn is not None:
                            fn(e).then_inc(SEM[inc[0]], inc[1])
                return run

            block.sync(replay("sp"))
            block.gpsimd(replay("pool"))
            block.scalar(replay("act"))
            block.vector(replay("dve"))
            block.tensor(replay("pe"))
    return nc, pr


def _fm(v, n):
    return np.ascontiguousarray(np.asarray(v, np.float32).reshape(n, 128).T)


def _kxc(w):
    K = w.shape[0] // 128
    return np.ascontiguousarray(w.reshape(K, 128, w.shape[1]).transpose(1, 0, 2).reshape(128, -1))


def prep_weights(w_ada, ffn1_w_in, ffn1_w_out, w_in, mla_w_uq, mla_w_ukv, w_branch_a, w_branch_b, w_out,
                 ffn2_w_in, ffn2_w_out):
    W = np.zeros((NU, 128, 4096), np.float32)
    wa = np.asarray(w_ada, np.float32).reshape(8, 128, 18, 4, 128)
    W[U_ADA:U_ADA + 18] = wa.transpose(2, 1, 3, 0, 4).reshape(18, 128, 4096)

    def w1_units(w1):
        w1 = np.asarray(w1, np.float32)
        g = w1[:, :FF].reshape(8, 128, JF, 128)
        u = w1[:, FF:].reshape(8, 128, JF, 128)
        gu = np.concatenate([g, u], axis=3)
        gu = gu.transpose(2, 1, 0, 3).reshape(11, 2, 128, 8, 256)
        return gu.transpose(0, 2, 1, 3, 4).reshape(11, 128, 4096)

    def w2_units(w2):
        w2 = np.asarray(w2, np.float32).reshape(JF, 128, 8, 128)
        return w2.transpose(2, 1, 0, 3).reshape(8, 128, FF)

    W[U_F1W1:U_F1W1 + 11] = w1_units(ffn1_w_in)
    W[U_F1W2:U_F1W2 + 8, :, :FF] = w2_units(ffn1_w_out)
    W[U_F2W1:U_F2W1 + 11] = w1_units(ffn2_w_in)
    W[U_F2W2:U_F2W2 + 8, :, :FF] = w2_units(ffn2_w_out)

    win = np.asarray(w_in, np.float32)
    c1 = np.zeros((D, 512), np.float32)
    c1[:, 0:416] = win[:, 0:416]
    c1[:, 416:432] = win[:, 400:416]
    c1[:, 432:448] = win[:, 384:400]
    W[U_WIN + 0] = _kxc(c1)
    for n, c0 in enumerate((416, 928, 1440, 1952, 2464, 2976, 3488)):
        W[U_WIN + 1 + n] = _kxc(win[:, c0:c0 + 512])

    uq = np.asarray(mla_w_uq, np.float32).reshape(256, 8, 96)
    ukv = np.asarray(mla_w_ukv, np.float32).reshape(128, 8, 128)
    uqn = uq[:, :, 0:64].reshape(256, 512)
    pe = uq[:, :, 64:96]
    pes = np.concatenate([pe[:, :, 16:32], pe[:, :, 0:16]], axis=2)
    uqpe = np.concatenate([pe, pes], axis=2).reshape(256, 512)
    ukT = ukv[:, :, 0:64].transpose(1, 2, 0).reshape(4, 128, 128)
    ukT = ukT.transpose(1, 0, 2).reshape(128, 512)
    uv = ukv[:, :, 64:128].reshape(128, 512)
    W[U_MLA, :, 0:1024] = _kxc(uqn)
    W[U_MLA, :, 1024:2048] = _kxc(uqpe)
    W[U_MLA, :, 2048:2560] = ukT
    W[U_MLA, :, 2560:3072] = uv
    W[U_WA] = _kxc(np.asarray(w_branch_a, np.float32))
    W[U_WB] = _kxc(np.asarray(w_branch_b, np.float32))
    wo = np.asarray(w_out, np.float32)
    W[U_WO] = _kxc(wo[:, 0:512])
    W[U_WO + 1] = _kxc(wo[:, 512:1024])
    return W


def prep_bias(rel_bias):
    rb = np.asarray(rel_bias, np.float32)
    kl = np.arange(128)[:, None, None]
    r = np.arange(5)[None, :, None]
    ql = np.arange(128)[None, None, :]
    d = 128 * r + ql - kl
    idx = np.minimum(d, 256) + 256
    g = rb[idx]
    return np.ascontiguousarray(g.transpose(0, 3, 1, 2).reshape(128, 8 * 640))


def prep_vecs(b, c, b_ada, ffn1_norm, mix_norm, ffn2_norm, final_norm, mla_q_norm, mla_kv_norm):
    v = np.zeros((128, NV), np.float32)
    v[:, BADA:BADA + 72] = _fm(b_ada, 72)
    v[:, N1C:N1C + 8] = _fm(ffn1_norm, 8)
    v[:, N2C:N2C + 8] = _fm(mix_norm, 8)
    v[:, N3C:N3C + 8] = _fm(ffn2_norm, 8)
    v[:, NFC:NFC + 8] = _fm(final_norm, 8)
    v[:, QNC:QNC + 2] = _fm(mla_q_norm, 2)
    v[:, KVNC:KVNC + 1] = _fm(mla_kv_norm, 1)
    v[:, CTC:CTC + 8] = _fm(c[b], 8)
    inv_freq = (np.float32(10000.0) ** (-np.arange(0, 32, 2, dtype=np.float32) / np.float32(32))).astype(np.float32)
    iv = (inv_freq.astype(np.float64) / (2 * np.pi)).astype(np.float32)
    v[0:16, IVFC] = iv
    v[16:32, IVFC] = iv
    v[0:16, SGNC] = -1.0
    v[16:32, SGNC] = 1.0
    return v


_CACHE = {}


def kernel(x, c, positions, w_ada, b_ada, ffn1_norm, ffn1_w_in, ffn1_w_out, mix_norm, w_in, mla_q_norm, mla_w_uq,
           mla_kv_norm, mla_w_ukv, rel_bias, w_branch_a, w_branch_b, w_out, ffn2_norm, ffn2_w_in, ffn2_w_out,
           final_norm):
    x = np.asarray(x, np.float32)
    c = np.asarray(c, np.float32)
    positions = np.asarray(positions, np.int32)
    B = x.shape[0]
    W = prep_weights(w_ada[0], ffn1_w_in[0], ffn1_w_out[0], w_in[0], mla_w_uq[0], mla_w_ukv[0], w_branch_a[0],
                     w_branch_b[0], w_out[0], ffn2_w_in[0], ffn2_w_out[0])
    bg = prep_bias(rel_bias[0])
    in_maps = []
    for b in range(B):
        in_maps.append({
            "xT": np.ascontiguousarray(x[b].T),
            "pos": np.ascontiguousarray(positions[b][None, :]),
            "vecs": prep_vecs(b, c, b_ada[0], ffn1_norm[0], mix_norm[0], ffn2_norm[0], final_norm, mla_q_norm[0],
                              mla_kv_norm[0]),
            "biasg": bg,
            "wsrc": W,
        })
    if "nc" not in _CACHE:
        _CACHE["nc"] = build_program()[0]
    nc = _CACHE["nc"]
    res = run_bass_kernel_spmd(nc, in_maps, core_ids=list(range(B)))
    out = np.stack([np.ascontiguousarray(res.results[b]["outT"].T) for b in range(B)], axis=0)
    return out.astype(np.float32)
```

```python
import contextlib
import numpy as np
import concourse.bass as bass
import concourse.mybir as mybir
from concourse.bass_utils import run_bass_kernel_spmd

F32 = mybir.dt.float32
BF16 = mybir.dt.bfloat16
I32 = mybir.dt.int32
AF = mybir.ActivationFunctionType
ALU = mybir.AluOpType

S = 4096
D = 1024
T = 512
NT = S // T
FF = 2816
JF = FF // 128
EPS = 1e-6
NSLOT = 3
NTMP = 4
G = 512

BADA, N1C, N2C, N3C, NFC, QNC, KVNC, CTC, IVFC, SGNC, NV = 0, 72, 80, 88, 96, 104, 106, 107, 115, 116, 120
A1, A2, A3, G1H, G2H, G3H = 0, 8, 16, 24, 32, 40
SH1, SC1, G1, SH2, SC2, G2, SH3, SC3, G3 = 0, 8, 16, 24, 32, 40, 48, 56, 64

U_ADA = 0
U_F1W1 = 18
U_F1W2 = 29
U_WIN = 37
U_MLA = 45
U_WA = 46
U_WB = 47
U_WO = 48
U_F2W1 = 50
U_F2W2 = 61
NU = 69

SCALE_MLA = 96.0 ** -0.5
SCALE_CA = 64.0 ** -0.5
TWO_PI_S = 2.0 * np.pi * (1.0 - 2e-6)


def unit_elems(u):
    if U_F1W2 <= u < U_F1W2 + 8 or U_F2W2 <= u < U_F2W2 + 8:
        return FF
    if u == U_MLA:
        return 3072
    return 4096


class View:
    __slots__ = ("ap", "keys")

    def __init__(self, ap, keys):
        self.ap = ap
        self.keys = keys


class Ten:
    def __init__(self, name, h, esz, idx=None):
        self.name = name
        self.h = h
        self.esz = esz
        self.idx = idx

    def keys(self, lo, hi):
        g0 = (lo * self.esz) // G
        g1 = (hi * self.esz - 1) // G
        return [(self.name, g) for g in range(g0, g1 + 1)]

    def v(self, lo, hi, p0=0, p1=128):
        return View(self.h[p0:p1, lo:hi], self.keys(lo, hi))


class Sub:
    def __init__(self, ten, off):
        self.ten = ten
        self.off = off

    def v(self, lo, hi, p0=0, p1=128):
        return self.ten.v(self.off + lo, self.off + hi, p0, p1)


class Prog:
    ENG = ("pe", "act", "dve", "pool", "sp")

    def __init__(self):
        self.streams = {e: [] for e in self.ENG}
        self.count = {e: 0 for e in ("pe", "act", "dve", "pool")}
        self.seen = {e: {} for e in self.ENG}
        self.lastw = {}
        self.readers = {}
        self.dmacount = {}
        self.know = {}

    def _deps(self, eng, reads, writes):
        deps = {}

        def need(tok, raw):
            if tok is None:
                return
            sk, val = tok
            if sk == eng and eng == "pe":
                return
            if deps.get(sk, 0) < val:
                deps[sk] = val

        lw = self.lastw
        for k in reads:
            need(lw.get(k), True)
        for k in writes:
            need(lw.get(k), False)
            rd = self.readers.get(k)
            if rd:
                for sk, val in rd.items():
                    need((sk, val), False)
        out = []
        seen = self.seen[eng]
        for sk, val in sorted(deps.items(), key=lambda kv: -kv[1]):
            if seen.get(sk, 0) >= val:
                continue
            seen[sk] = val
            out.append((sk, val))
            kn = self.know.get((sk, val))
            if kn:
                for k2, v2 in kn.items():
                    if seen.get(k2, 0) < v2:
                        seen[k2] = v2
        return out

    def _commit(self, tok, reads, writes):
        sk, val = tok
        for k in reads:
            d = self.readers.get(k)
            if d is None:
                d = self.readers[k] = {}
            if d.get(sk, 0) < val:
                d[sk] = val
        for k in writes:
            self.lastw[k] = tok
            self.readers[k] = {}

    def op(self, eng, fn, reads, writes):
        waits = self._deps(eng, reads, writes)
        self.count[eng] += 1
        tok = (eng, self.count[eng])
        self.know[tok] = dict(self.seen[eng])
        self._commit(tok, reads, writes)
        self.streams[eng].append((waits, fn, (eng, 1)))

    def dma(self, queue, fn, semkey, reads, writes):
        waits = self._deps(queue, reads, writes)
        self.dmacount[semkey] = self.dmacount.get(semkey, 0) + 16
        tok = (semkey, self.dmacount[semkey])
        self.know[tok] = dict(self.seen[queue])
        self._commit(tok, reads, writes)
        self.streams[queue].append((waits, fn, (semkey, 16)))
        return tok

    def final_wait(self, eng, toks):
        waits = []
        for sk, val in toks:
            if self.seen[eng].get(sk, 0) < val:
                self.seen[eng][sk] = val
                waits.append((sk, val))
        self.streams[eng].append((waits, None, None))


def build_program(ntiles=NT, stop=None):
    nc = bass.Bass("TRN2", target_bir_lowering=False)
    xT = nc.dram_tensor("xT", [D, S], F32, kind="ExternalInput").ap()
    pos = nc.dram_tensor("pos", [1, S], I32, kind="ExternalInput").ap()
    vecs = nc.dram_tensor("vecs", [128, NV], F32, kind="ExternalInput").ap()
    biasg = nc.dram_tensor("biasg", [128, 8 * 640], F32, kind="ExternalInput").ap()
    wsrc = nc.dram_tensor("wsrc", [NU, 128, 4096], F32, kind="ExternalInput").ap()
    wbf = nc.dram_tensor("wbf", [NU, 128, 4096], BF16, kind="Internal").ap()
    outT = nc.dram_tensor("outT", [D, S], F32, kind="ExternalOutput").ap()

    pr = Prog()
    with contextlib.ExitStack() as es:
        def sb(name, shape, dt):
            h = es.enter_context(nc.sbuf_tensor(name, shape, dt))
            esz = 4 if dt in (F32, I32) else 2
            return Ten(name, h, esz)

        ring = [sb("ring%d" % i, [128, 4096], BF16) for i in range(NSLOT)]
        X = sb("X", [128, 4096], F32)
        hT = sb("hT", [128, 4096], BF16)
        sqm = sb("sqm", [128, 4096], BF16)
        arena = sb("arena", [128, 11264], BF16)
        oA = sb("oA", [128, 2048], BF16)
        oB = sb("oB", [128, 2048], BF16)
        pbMt = sb("pbMt", [128, 2048], BF16)
        pbM = [Sub(pbMt, i * 512) for i in range(4)]
        pbC = [sb("pbC%d" % i, [128, 512], BF16) for i in range(4)]
        ckv = sb("ckv", [128, S], BF16)
        kpe = sb("kpe", [128, S], BF16)
        VE = sb("VE", [128, 32 * 768], BF16)
        cak = sb("cak", [128, 4096], BF16)
        VEc = sb("VEc", [128, 8 * 768], BF16)
        Eb = sb("Eb", [128, 8 * 640], BF16)
        tmps = [sb("tmp%d" % i, [128, 640], F32) for i in range(NTMP)]
        for i, tm in enumerate(tmps):
            tm.idx = i
        rs = sb("rs", [128, 512], F32)
        cs = sb("cs", [128, 1024], F32)
        cosT = Sub(cs, 0)
        sinS = Sub(cs, 512)
        negh = sb("negh", [128, 1], F32)
        epsb = sb("epsb", [128, 1], F32)
        warm = sb("warm", [128, 1], F32)
        ones_f = sb("ones_f", [128, 1], F32)
        vec = sb("vec", [128, NV], F32)
        adaT = sb("adaT", [128, 72], F32)
        dv = sb("dv", [128, 48], F32)
        cact = sb("cact", [128, 8], BF16)
        ones = sb("ones", [128, 128], BF16)
        identb = sb("identb", [128, 128], BF16)
        mk = sb("mk", [128, 256], BF16)

        PS = []
        for i in range(8):
            h = es.enter_context(nc.psum_tensor("ps%d" % i, [128, 512], F32))
            PS.append(Ten("ps%d" % i, h, 4))

        semnames = ["pe", "act", "dve", "pool", "misc", "pos"]
        semnames += ["ring%d" % i for i in range(NSLOT)]
        semnames += ["rgp%d" % i for i in range(NSLOT)] + ["wb%d" % i for i in range(NSLOT)]
        semnames += ["x%d" % k for k in range(8)]
        semnames += ["tmp%d" % i for i in range(NTMP)]
        semnames += ["ost%d" % i for i in range(8)]
        SEM = {n: es.enter_context(nc.semaphore(n)) for n in semnames}

        def ACT(out, in_, func, scale=1.0, bias=None):
            reads = list(in_.keys)
            sc = scale
            if isinstance(scale, View):
                reads += scale.keys
                sc = scale.ap
            bi = bias
            if isinstance(bias, View):
                reads += bias.keys
                bi = bias.ap
            oa, ia = out.ap, in_.ap

            def fn(e):
                if bi is None:
                    return e.activation(out=oa, in_=ia, func=func, scale=sc)
                return e.activation(out=oa, in_=ia, func=func, scale=sc, bias=bi)
            pr.op("act", fn, reads, out.keys)

        def TT(out, a, b, op, eng="dve"):
            oa, aa, ba = out.ap, a.ap, b.ap
            pr.op(eng, lambda e: e.tensor_tensor(out=oa, in0=aa, in1=ba, op=op), a.keys + b.keys, out.keys)

        def TS(out, a, s1, s2, op0, op1=None):
            reads = list(a.keys)
            v1, v2 = s1, s2
            if isinstance(s1, View):
                reads += s1.keys
                v1 = s1.ap
            if isinstance(s2, View):
                reads += s2.keys
                v2 = s2.ap
            oa, aa = out.ap, a.ap

            def fn(e):
                if op1 is None:
                    return e.tensor_scalar(out=oa, in0=aa, scalar1=v1, scalar2=None, op0=op0)
                return e.tensor_scalar(out=oa, in0=aa, scalar1=v1, scalar2=v2, op0=op0, op1=op1)
            pr.op("dve", fn, reads, out.keys)

        def STT(out, a, s, b, op0, op1):
            reads = a.keys + b.keys
            sv = s
            if isinstance(s, View):
                reads = reads + s.keys
                sv = s.ap
            oa, aa, ba = out.ap, a.ap, b.ap
            pr.op("dve", lambda e: e.scalar_tensor_tensor(out=oa, in0=aa, scalar=sv, in1=ba, op0=op0, op1=op1),
                  reads, out.keys)

        def COPY(out, a, eng="dve"):
            oa, aa = out.ap, a.ap
            pr.op(eng, lambda e: e.tensor_copy(out=oa, in_=aa), a.keys, out.keys)

        def ACOPY(out, a):
            oa, aa = out.ap, a.ap
            pr.op("act", lambda e: e.copy(out=oa, in_=aa), a.keys, out.keys)

        def RECIP(out, a):
            oa, aa = out.ap, a.ap
            pr.op("dve", lambda e: e.reciprocal(out=oa, in_=aa), a.keys, out.keys)

        def MEMSET(out, val, eng="dve"):
            oa = out.ap
            pr.op(eng, lambda e: e.memset(oa, val), [], out.keys)

        def POW(out, a):
            oa, aa = out.ap, a.ap
            ba = negh.h[:, 0:1].to_broadcast([128, 512])
            pr.op("pool", lambda e: e.tensor_tensor(out=oa, in0=aa, in1=ba, op=ALU.pow),
                  a.keys + negh.keys(0, 1), out.keys)

        def MM(out, lhsT, rhs, start, stop, skip=False, also=None, alsor=None):
            oa, la, ra = out.ap, lhsT.ap, rhs.ap
            if also:
                out = View(out.ap, out.keys + also)
            if alsor:
                rhs = View(rhs.ap, rhs.keys + alsor)
            if skip:
                pr.op("pe", lambda e: e.matmul(oa, lhsT=la, rhs=ra, start=start, stop=stop, skip_group_check=True),
                      lhsT.keys + rhs.keys, out.keys)
            else:
                pr.op("pe", lambda e: e.matmul(oa, lhsT=la, rhs=ra, start=start, stop=stop),
                      lhsT.keys + rhs.keys, out.keys)

        def DMA(queue, out_ap, in_ap, semkey, reads, writes):
            return pr.dma(queue, lambda e: e.dma_start(out=out_ap, in_=in_ap), semkey, reads, writes)

        tmp_ctr = [0]

        def tmp():
            t_ = tmps[tmp_ctr[0] % NTMP]
            tmp_ctr[0] += 1
            return t_

        def Xk(k):
            return X.v(k * 512, (k + 1) * 512)

        def hTk(k):
            return hT.v(k * 512, (k + 1) * 512)

        def sqk(k):
            return sqm.v(k * 512, (k + 1) * 512)

        def actk(j):
            return arena.v(j * 512, (j + 1) * 512)

        QABS0, QPE0, CAQ0, CQN0 = 0, 4096, 8192, 10240

        stg = []
        for tn in (oA, oB, pbMt):
            for c in range(2):
                stg.append(View(tn.h[:, c * 1024:(c + 1) * 1024].bitcast(F32), tn.keys(c * 1024, (c + 1) * 1024)))
        for c in range(2):
            stg.append(cs.v(c * 512, (c + 1) * 512))

        per_tile = ([U_F1W1 + i for i in range(11)] + [U_F1W2 + m for m in range(8)]
                    + [U_WIN + 0, U_WIN + 1, U_WIN + 2, U_WIN + 3, U_MLA,
                       U_WIN + 4, U_WA, U_WIN + 5, U_WIN + 6, U_WB, U_WIN + 7, U_WO, U_WO + 1]
                    + [U_F2W1 + i for i in range(11)] + [U_F2W2 + m for m in range(8)])
        stream = [U_ADA + a for a in range(4)]
        for i in range(11):
            stream += [U_F1W1 + i, U_ADA + 4 + i]
        for m in range(8):
            stream += [U_F1W2 + m] + ([U_ADA + 15 + m] if m < 3 else [])
        stream += per_tile[19:]
        for _ in range(ntiles - 1):
            stream += per_tile
        st = {"cons": 0, "issued": 0}

        def next_unit(expect=None, live_before=0):
            while st["issued"] < min(len(stream), st["cons"] - live_before + NSLOT):
                n = st["issued"]
                u = stream[n]
                E = unit_elems(u)
                s_ = n % NSLOT
                if n < 18 + len(per_tile):
                    DMA("pool", ring[s_].h[:, 0:E], wsrc[u][:, 0:E], "rgp%d" % s_, [], ring[s_].keys(0, E))
                    if u >= U_F1W1:
                        DMA("sp", wbf[u][:, 0:E], ring[s_].h[:, 0:E], "wb%d" % s_, ring[s_].keys(0, E), [("wbf", u)])
                else:
                    DMA("sp", ring[s_].h[:, 0:E], wbf[u][:, 0:E], "ring%d" % s_, [("wbf", u)], ring[s_].keys(0, E))
                st["issued"] += 1
            n = st["cons"]
            if expect is not None:
                assert stream[n] == expect, (stream[n], expect)
            st["cons"] += 1
            return ring[n % NSLOT]


        DMA("sp", vec.h[:, :], vecs, "misc", [], vec.keys(0, NV))
        MEMSET(ones.v(0, 128), 1.0)
        MEMSET(identb.v(0, 128), 0.0, eng="pool")
        _ia = identb.h[:, :]
        pr.op("pool", lambda e: e.affine_select(out=_ia, in_=_ia, pattern=[[-1, 128]], compare_op=ALU.not_equal,
                                                fill=1.0, base=0, channel_multiplier=1),
              identb.keys(0, 128), identb.keys(0, 128))
        MEMSET(negh.v(0, 1), -0.5, eng="pool")
        MEMSET(epsb.v(0, 1), EPS)
        MEMSET(ones_f.v(0, 1), 1.0)
        MEMSET(mk.v(0, 256), 0.0)
        MEMSET(mk.v(64, 128, 0, 1), 1.0)
        MEMSET(mk.v(128, 192, 0, 1), -30000.0)
        MEMSET(kpe.v(0, S), 0.0)
        for c0 in range(0, 32 * 768, 4096):
            MEMSET(VE.v(c0, c0 + 4096), 1.0)
        MEMSET(VEc.v(0, 4096), 1.0)
        MEMSET(VEc.v(4096, 8 * 768), 1.0)

        for h in range(8):
            tb = tmp()
            DMA("sp", tb.h[:, 0:640], biasg[:, h * 640:(h + 1) * 640], "tmp%d" % tb.idx, [], tb.keys(0, 640))
            ACT(Eb.v(h * 640, (h + 1) * 640), tb.v(0, 640), AF.Identity, scale=1.0 / SCALE_CA)
            MEMSET(Eb.v(h * 640 + 512 + 64, h * 640 + 640, 0, 64), -30000.0)
            MEMSET(Eb.v(h * 640, h * 640 + 64, 64, 128), -30000.0)

        for k in range(8):
            DMA("sp", X.h[:, k * 512:(k + 1) * 512], xT[k * 128:(k + 1) * 128, 0:512], "x%d" % k, [], Xk(k).keys)

        ACT(cact.v(0, 8), vec.v(CTC, CTC + 8), AF.Silu)

        def ada_unit(a):
            R = next_unit(U_ADA + a)
            for cc in range(4):
                c = 4 * a + cc
                for k in range(8):
                    b0 = (cc * 8 + k) * 128
                    MM(PS[7].v(c, c + 1), R.v(b0, b0 + 128), cact.v(k, k + 1), k == 0, k == 7)
            if a == 3:
                TT(adaT.v(0, 16), PS[7].v(0, 16), vec.v(BADA, BADA + 16), ALU.add)
                STT(dv.v(A1, A1 + 8), adaT.v(SC1, SC1 + 8), 1.0, vec.v(N1C, N1C + 8), ALU.add, ALU.mult)
            elif a == 5:
                TT(adaT.v(16, 24), PS[7].v(16, 24), vec.v(BADA + 16, BADA + 24), ALU.add)
                TS(dv.v(G1H, G1H + 8), adaT.v(G1, G1 + 8), 0.5, None, ALU.mult)
            elif a == 17:
                TT(adaT.v(24, 72), PS[7].v(24, 72), vec.v(BADA + 24, BADA + 72), ALU.add)
                for (acol, sccol, ncol) in ((A2, SC2, N2C), (A3, SC3, N3C)):
                    STT(dv.v(acol, acol + 8), adaT.v(sccol, sccol + 8), 1.0, vec.v(ncol, ncol + 8), ALU.add, ALU.mult)
                for (gcol, src) in ((G2H, G2), (G3H, G3)):
                    TS(dv.v(gcol, gcol + 8), adaT.v(src, src + 8), 0.5, None, ALU.mult)

        for a in range(4):
            ada_unit(a)

        if stop == "ada":
            ntiles = 0
            o_ = tmp()
            MEMSET(o_.v(0, 512), 0.0)
            COPY(o_.v(0, 72), adaT.v(0, 72))
            COPY(o_.v(72, 120), dv.v(0, 48))
            DMA("sp", outT[0:128, 0:512], o_.h[:, 0:512], "ost%d" % o_.idx, o_.keys(0, 512), [("out", 0, 0)])
            o2 = tmp()
            COPY(o2.v(0, 512), Eb.v(0, 512))
            DMA("sp", outT[128:256, 0:512], o2.h[:, 0:512], "ost%d" % o2.idx, o2.keys(0, 512), [("out", 0, 1)])

        def stats_rstd(nchunks, sq_of, inv_n, ps_bank, rs_t):
            for k in range(nchunks):
                MM(ps_bank.v(0, 512), ones.v(0, 128), sq_of(k), k == 0, k == nchunks - 1)
            sd = tmp()
            ACT(sd.v(0, 512), ps_bank.v(0, 512), AF.Ln, scale=inv_n, bias=epsb.v(0, 1))
            ACT(rs_t.v(0, 512), sd.v(0, 512), AF.Exp, scale=-0.5)

        def prewarm_ln():
            ACT(warm.v(0, 1), ones_f.v(0, 1), AF.Ln)

        def norm_sq(src=Xk):
            for k in range(8):
                ACT(sqk(k), src(k), AF.Square)

        def norm_apply(acol, bcol, src=Xk):
            stats_rstd(8, sqk, 1.0 / D, PS[6], rs)
            for k in range(8):
                t_ = tmp()
                STT(t_.v(0, 512), src(k), dv.v(acol + k, acol + k + 1), rs.v(0, 512), ALU.mult, ALU.mult)
                ACT(hTk(k), t_.v(0, 512), AF.Identity, bias=adaT.v(bcol + k, bcol + k + 1))

        def norm_mod(acol, bcol, src=Xk):
            norm_sq(src)
            norm_apply(acol, bcol, src)

        def ffn(u_w1, u_w2, ghcol, res=Xk, hooks=None, mid_hook=None, mhooks=None):
            R = None
            hooks = hooks or {}
            mhooks = mhooks or {}
            R = next_unit(u_w1)
            for k in range(8):
                for jj in range(2):
                    for half in range(2):
                        b0 = (jj * 8 + k) * 256 + 128 * half
                        MM(PS[2 * jj + half].v(0, 512), R.v(b0, b0 + 128), hTk(k), k == 0, k == 7)
            for jj in range(2):
                sg = tmp()
                ACT(sg.v(0, 512), PS[2 * jj].v(0, 512), AF.Silu)
                TT(actk(jj), sg.v(0, 512), PS[2 * jj + 1].v(0, 512), ALU.mult)
            for j in range(2, JF):
                if j in hooks:
                    hooks[j]()
                if j % 2 == 0:
                    R = next_unit(u_w1 + j // 2)
                jj = j % 2
                pg, pu = PS[2 * jj], PS[2 * jj + 1]
                for k in range(8):
                    b0 = (jj * 8 + k) * 256
                    MM(pg.v(0, 512), R.v(b0, b0 + 128), hTk(k), k == 0, k == 7,
                       also=(pu.keys(0, 512) if k == 0 else None))
                for k in range(8):
                    b0 = (jj * 8 + k) * 256 + 128
                    MM(pu.v(0, 512), R.v(b0, b0 + 128), hTk(k), k == 0, k == 7)
                sg = tmp()
                ACT(sg.v(0, 512), pg.v(0, 512), AF.Silu)
                TT(actk(j), sg.v(0, 512), pu.v(0, 512), ALU.mult)
            prewarm_ln()
            if mid_hook is not None:
                mid_hook()
            for m in range(8):
                if m in mhooks:
                    mhooks[m]()
                R = next_unit(u_w2 + m)
                py = PS[4 + m % 2]
                for j in range(JF):
                    MM(py.v(0, 512), R.v(j * 128, (j + 1) * 128), actk(j), j == 0, j == JF - 1)
                STT(Xk(m), py.v(0, 512), dv.v(ghcol + m, ghcol + m + 1), res(m), ALU.mult, ALU.add)

        def final_norm(t):
            stats_rstd(8, sqk, 1.0 / D, PS[6], rs)
            for k in range(8):
                STT(Xk(k), Xk(k), vec.v(NFC + k, NFC + k + 1), rs.v(0, 512), ALU.mult, ALU.mult)
                DMA("sp", outT[k * 128:(k + 1) * 128, t * 512:(t + 1) * 512], X.h[:, k * 512:(k + 1) * 512],
                    "ost%d" % k, Xk(k).keys, [("out", t, k)])

        def rope_tables(t):
            a, b, c, d = tmps[0], tmps[1], tmps[2], tmps[3]
            tmp_ctr[0] = 0
            posi = View(a.h[0:32, 0:512].bitcast(I32), a.keys(0, 512))
            DMA("pool", posi.ap, pos[0:1, t * 512:(t + 1) * 512].partition_broadcast(32), "pos", [], posi.keys)
            u_ = b.v(0, 512, 0, 32)
            TS(u_, posi, vec.v(IVFC, IVFC + 1, 0, 32), None, ALU.mult)
            ki = View(c.h[0:32, 0:512].bitcast(I32), c.keys(0, 512))
            COPY(ki, u_)
            f_ = d.v(0, 512, 0, 32)
            TT(f_, u_, ki, ALU.subtract)
            s_ = c.v(0, 512, 0, 32)
            ACT(s_, f_, AF.Sin, scale=TWO_PI_S)
            TS(sinS.v(0, 512, 0, 32), s_, vec.v(SGNC, SGNC + 1, 0, 32), None, ALU.mult)
            v_ = a.v(0, 512, 0, 32)
            TS(v_, u_, 0.25, None, ALU.add)
            COPY(ki, v_)
            TT(f_, v_, ki, ALU.subtract)
            ACT(cosT.v(0, 512, 0, 32), f_, AF.Sin, scale=TWO_PI_S)

        def rope_apply(out, psA, psB):
            t1, t2 = tmp(), tmp()
            TT(t1.v(0, 512, 0, 32), psA, cosT.v(0, 512, 0, 32), ALU.mult)
            TT(t2.v(0, 512, 0, 32), psB, sinS.v(0, 512, 0, 32), ALU.mult)
            TT(out, t1.v(0, 512, 0, 32), t2.v(0, 512, 0, 32), ALU.add)

        def ve_write(dst, base, ps_bank):
            src = ps_bank.h[:, 0:512].rearrange("p (i hh v) -> p i hh v", hh=2, v=64)
            dview = dst.h[:, base:base + 768].rearrange("p (i c) -> p i c", c=192)
            keys_r = ps_bank.keys(0, 512)
            keys_w = dst.keys(base, base + 768)
            COPY(View(dview[:, :, 0:64], keys_w), View(src[:, :, 0, :], keys_r))
            oa, ia = dview[:, :, 128:192], src[:, :, 1, :]
            pr.op("act", lambda e: e.copy(out=oa, in_=ia), keys_r, keys_w)

        def normalise_pair(i, o_t, on_dve=False):
            pe_, po_ = PS[4 + 2 * (i % 2)], PS[5 + 2 * (i % 2)]
            if on_dve:
                t1 = tmp()
                RECIP(t1.v(0, 512, 64, 128), pe_.v(0, 512, 64, 128))
                TT(o_t.v(i * 512, (i + 1) * 512, 0, 64), pe_.v(0, 512, 0, 64), t1.v(0, 512, 64, 128), ALU.mult)
                t2 = tmp()
                RECIP(t2.v(0, 512, 0, 64), po_.v(0, 512, 0, 64))
                TT(o_t.v(i * 512, (i + 1) * 512, 64, 128), po_.v(0, 512, 64, 128), t2.v(0, 512, 0, 64), ALU.mult)
                return
            t1 = tmp()
            ACT(t1.v(0, 512, 64, 128), pe_.v(0, 512, 64, 128), AF.Ln)
            ACT(t1.v(0, 512, 64, 128), t1.v(0, 512, 64, 128), AF.Exp, scale=-1.0)
            TT(o_t.v(i * 512, (i + 1) * 512, 0, 64), pe_.v(0, 512, 0, 64), t1.v(0, 512, 64, 128), ALU.mult)
            t2 = tmp()
            ACT(t2.v(0, 512, 0, 64), po_.v(0, 512, 0, 64), AF.Ln)
            ACT(t2.v(0, 512, 0, 64), t2.v(0, 512, 0, 64), AF.Exp, scale=-1.0)
            TT(o_t.v(i * 512, (i + 1) * 512, 64, 128), po_.v(0, 512, 64, 128), t2.v(0, 512, 0, 64), ALU.mult)

        for t in range(ntiles):
            tc0, tc1 = t * 512, (t + 1) * 512

            if t == 0:
                norm_mod(A1, SH1)
                h0 = {j: (lambda a=3 + j // 2: ada_unit(a)) for j in range(2, JF, 2)}
                ffn(U_F1W1, U_F1W2, G1H, hooks=h0, mid_hook=(lambda: ada_unit(14)),
                    mhooks={m: (lambda a=14 + m: ada_unit(a)) for m in (1, 2, 3)})
            else:
                ffn(U_F1W1, U_F1W2, G1H, res=(lambda m: stg[m]),
                    hooks={2: norm_sq, 6: (lambda tt=t - 1: final_norm(tt))})
            if stop == "ffn1":
                break

            norm_mod(A2, SH2)
            rope_tables(t)
            R = next_unit(U_WIN + 0)
            for k in range(8):
                for (bank, c0, mcols) in ((0, 0, 128), (1, 128, 128), (2, 256, 128), (3, 384, 32), (4, 416, 32)):
                    MM(PS[bank].v(0, 512, 0, mcols), R.v(k * 512 + c0, k * 512 + c0 + mcols), hTk(k), k == 0, k == 7)
            ACT(sqk(0), PS[0].v(0, 512), AF.Square)
            ACT(sqk(1), PS[1].v(0, 512), AF.Square)
            ACT(sqk(2), PS[2].v(0, 512), AF.Square)

            R = next_unit(U_WIN + 1)
            for i in range(4):
                pq = PS[5 + 2 * (i % 2)]
                for k in range(8):
                    MM(pq.v(0, 512), R.v(k * 512 + i * 128, k * 512 + (i + 1) * 128), hTk(k), k == 0, k == 7)
                ACOPY(arena.v(CAQ0 + i * 512, CAQ0 + (i + 1) * 512), pq.v(0, 512))
            stats_rstd(2, sqk, 1.0 / 256, PS[6], rs)
            for kk in range(2):
                STT(arena.v(CQN0 + kk * 512, CQN0 + (kk + 1) * 512), PS[kk].v(0, 512),
                    vec.v(QNC + kk, QNC + kk + 1), rs.v(0, 512), ALU.mult, ALU.mult)
            rs2 = tmp()
            stats_rstd(1, lambda k: sqk(2), 1.0 / 128, PS[6], rs2)
            STT(ckv.v(tc0, tc1), PS[2].v(0, 512), vec.v(KVNC, KVNC + 1), rs2.v(0, 512), ALU.mult, ALU.mult)
            rope_apply(kpe.v(tc0, tc1, 0, 32), PS[3].v(0, 512, 0, 32), PS[4].v(0, 512, 0, 32))

            R = next_unit(U_WIN + 2)
            for i in range(4):
                pk = PS[5 + 2 * (i % 2)]
                for k in range(8):
                    MM(pk.v(0, 512), R.v(k * 512 + i * 128, k * 512 + (i + 1) * 128), hTk(k), k == 0, k == 7)
                cb = i * 1024 + (t % 2) * 512
                COPY(cak.v(cb, cb + 512), pk.v(0, 512))
            R = next_unit(U_WIN + 3)
            for sub in range(4):
                pv = PS[5 + 2 * (sub % 2)]
                for k in range(8):
                    MM(pv.v(0, 512), hT.v(k * 512 + sub * 128, k * 512 + (sub + 1) * 128), R.v(k * 512, (k + 1) * 512),
                       k == 0, k == 7)
                ve_write(VEc, ((4 * t + sub) % 8) * 768, pv)

            R = next_unit(U_MLA)
            UQN, UQPE, UKT, UV = 0, 1024, 2048, 2560
            cqn = lambda kk: arena.v(CQN0 + kk * 512, CQN0 + (kk + 1) * 512)
            for i in range(4):
                pn = PS[i]
                for kk in range(2):
                    b0 = UQN + kk * 512 + i * 128
                    MM(pn.v(0, 512), R.v(b0, b0 + 128), cqn(kk), kk == 0, kk == 1)
                ACOPY(pbM[i].v(0, 512), pn.v(0, 512))
            for h in range(8):
                pA, pB = PS[4 + 2 * (h % 2)], PS[5 + 2 * (h % 2)]
                for (pp, off) in ((pA, 0), (pB, 32)):
                    for kk in range(2):
                        b0 = UQPE + kk * 512 + h * 64 + off
                        MM(pp.v(0, 512, 0, 32), R.v(b0, b0 + 32), cqn(kk), kk == 0, kk == 1)
                rope_apply(arena.v(QPE0 + h * 512, QPE0 + (h + 1) * 512, 0, 32),
                           pA.v(0, 512, 0, 32), pB.v(0, 512, 0, 32))
            for i in range(4):
                qn = pbM[i]
                for hh in range(2):
                    h = 2 * i + hh
                    pa = PS[(2 * i + hh) % 4]
                    MM(pa.v(0, 512), R.v(UKT + i * 128, UKT + (i + 1) * 128, 64 * hh, 64 * hh + 64),
                       qn.v(0, 512, 64 * hh, 64 * hh + 64), True, True)
                    if hh == 0:
                        COPY(arena.v(QABS0 + h * 512, QABS0 + (h + 1) * 512), pa.v(0, 512))
                    else:
                        ACOPY(arena.v(QABS0 + h * 512, QABS0 + (h + 1) * 512), pa.v(0, 512))
            for sub in range(4):
                pv = PS[sub % 2]
                MM(pv.v(0, 512), ckv.v(tc0 + sub * 128, tc0 + (sub + 1) * 128), R.v(UV, UV + 512), True, True)
                ve_write(VE, (4 * t + sub) * 768, pv)

            items = [(i, hh, kt) for i in range(4) for hh in range(2) for kt in range(4 * t + 4)]
            nk = 4 * t + 4

            def mla_S(n):
                i, hh, kt = items[n]
                h = 2 * i + hh
                j = kt - 4 * t
                q0 = 128 * j if j >= 0 else 0
                ps = PS[n % 4]
                MM(ps.v(q0, 512), ckv.v(kt * 128, (kt + 1) * 128),
                   arena.v(QABS0 + h * 512 + q0, QABS0 + (h + 1) * 512), True, False)
                MM(ps.v(q0, 512), kpe.v(kt * 128, (kt + 1) * 128),
                   arena.v(QPE0 + h * 512 + q0, QPE0 + (h + 1) * 512), False, j < 0)
                if j >= 0:
                    MM(ps.v(q0, q0 + 128), mk.v(0, 128), mk.v(128, 256), False, True)
                ACT(pbM[n % 4].v(q0, 512), ps.v(q0, 512), AF.Exp, scale=SCALE_MLA)

            def mla_PV(n, nxt=False):
                i, hh, kt = items[n]
                j = kt - 4 * t
                q0 = 128 * j if j >= 0 else 0
                po = PS[4 + 2 * (i % 2) + hh]
                c0 = kt * 768 + i * 192 + 64 * hh
                MM(po.v(q0, 512), VE.v(c0, c0 + 128), pbM[n % 4].v(q0, 512), kt == 0, kt == nk - 1,
                   alsor=(pbM[(n + 1) % 4].v(0, 512).keys if nxt else None))
                if kt == nk - 1 and hh == 1:
                    normalise_pair(i, oA, on_dve=(i < 3))

            LA = 4
            for n in range(min(LA, len(items))):
                mla_S(n)
            for n in range(0, len(items), 2):
                two = n + 1 < len(items)
                mla_PV(n, nxt=two)
                if two:
                    mla_PV(n + 1)
                for q_ in (n + LA, n + LA + 1):
                    if q_ < len(items):
                        mla_S(q_)

            kt_lo = max(0, 4 * t - 4)
            citems = [(i, hh, Kt) for i in range(4) for hh in range(2) for Kt in range(kt_lo, 4 * t + 4)]

            for h in range(8):
                i_, hh_ = h // 2, h % 2
                src = arena.v(CAQ0 + i_ * 512, CAQ0 + (i_ + 1) * 512, 64 * hh_, 64 * hh_ + 64)
                COPY(arena.v(QABS0 + h * 512, QABS0 + (h + 1) * 512, 64 * hh_, 64 * hh_ + 64), src, eng="pool")
                MEMSET(arena.v(QABS0 + h * 512, QABS0 + (h + 1) * 512, 64 * (1 - hh_), 64 * (1 - hh_) + 64), 0.0,
                       eng="pool")

            def ca_rng(Kt):
                dk = Kt - 4 * t
                s0, s1 = max(0, dk), min(3, dk + 4)
                return dk, 128 * s0, 128 * (s1 + 1)

            def ca_S(n):
                i, hh, Kt = citems[n]
                h = 2 * i + hh
                dk, c0, c1 = ca_rng(Kt)
                w = Kt % 8
                ps = PS[n % 4]
                kb = i * 1024 + w * 128
                MM(ps.v(c0, c1), cak.v(kb, kb + 128),
                   arena.v(QABS0 + h * 512 + c0, QABS0 + h * 512 + c1), True, False)
                e0 = h * 640 + c0 - 128 * dk
                MM(ps.v(c0, c1), identb.v(0, 128), Eb.v(e0, e0 + (c1 - c0)), False, True)
                ACT(pbC[n % 4].v(c0, c1), ps.v(c0, c1), AF.Exp, scale=SCALE_CA)

            def ca_PV(n, nxt=False):
                i, hh, Kt = citems[n]
                dk, c0, c1 = ca_rng(Kt)
                w = Kt % 8
                po = PS[4 + 2 * (i % 2) + hh]
                v0 = w * 768 + i * 192 + 64 * hh
                MM(po.v(c0, c1), VEc.v(v0, v0 + 128), pbC[n % 4].v(c0, c1), Kt == kt_lo, Kt == 4 * t + 3, skip=True,
                   alsor=(pbC[(n + 1) % 4].keys(0, 512) if nxt else None))
                if Kt == 4 * t + 3 and hh == 1:
                    normalise_pair(i, oB)

            for n in range(min(LA, len(citems))):
                ca_S(n)
            for n in range(0, len(citems), 2):
                two = n + 1 < len(citems)
                ca_PV(n, nxt=two)
                if two:
                    ca_PV(n + 1)
                for q_ in (n + LA, n + LA + 1):
                    if q_ < len(citems):
                        ca_S(q_)

            if stop == "attn":
                for q_, o_t in enumerate((oA, oB)):
                    for i in range(4):
                        o_ = tmp()
                        COPY(o_.v(0, 512), o_t.v(i * 512, (i + 1) * 512))
                        DMA("sp", outT[q_ * 128:(q_ + 1) * 128, i * 512:(i + 1) * 512], o_.h[:, 0:512],
                            "ost%d" % o_.idx, o_.keys(0, 512), [("out", q_, i)])
                for q_, src in enumerate((arena.v(QABS0, QABS0 + 512), arena.v(QPE0, QPE0 + 512), ckv.v(0, 512), kpe.v(0, 512),
                                          arena.v(CAQ0, CAQ0 + 512), cak.v(0, 512))):
                    o_ = tmp()
                    COPY(o_.v(0, 512), src)
                    DMA("sp", outT[(2 + q_) * 128:(3 + q_) * 128, 0:512], o_.h[:, 0:512],
                        "ost%d" % o_.idx, o_.keys(0, 512), [("out", 2 + q_, 0)])
                break

            mgk = sqk
            for phase, (ug0, uw, ug1, o_t) in enumerate(((U_WIN + 4, U_WA, U_WIN + 5, oA), (U_WIN + 6, U_WB, U_WIN + 7, oB))):
                Rg = Rw = None
                for m in range(8):
                    if m == 0:
                        Rg = next_unit(ug0)
                        Rw = next_unit(uw, live_before=1)
                    if m == 4:
                        Rg = next_unit(ug1, live_before=1)
                    pg, py = PS[m % 2], PS[2 + m % 2]
                    for k in range(8):
                        b0 = k * 512 + (m % 4) * 128
                        MM(pg.v(0, 512), Rg.v(b0, b0 + 128), hTk(k), k == 0, k == 7)
                    for i in range(4):
                        b0 = i * 1024 + m * 128
                        MM(py.v(0, 512), Rw.v(b0, b0 + 128), o_t.v(i * 512, (i + 1) * 512), i == 0, i == 3)
                    th = tmp()
                    ACT(th.v(0, 512), pg.v(0, 512), AF.Tanh, scale=0.5)
                    if phase == 0:
                        STT(mgk(m), th.v(0, 512), 1.0, py.v(0, 512), ALU.add, ALU.mult)
                    else:
                        u2 = tmp()
                        STT(u2.v(0, 512), th.v(0, 512), 1.0, py.v(0, 512), ALU.add, ALU.mult)
                        TT(mgk(m), mgk(m), u2.v(0, 512), ALU.add)
            prewarm_ln()
            R = None
            for mp in range(8):
                if mp % 4 == 0:
                    R = next_unit(U_WO + mp // 4)
                po = PS[4 + mp % 2]
                for m in range(8):
                    b0 = m * 512 + (mp % 4) * 128
                    MM(po.v(0, 512), R.v(b0, b0 + 128), mgk(m), m == 0, m == 7)
                STT(Xk(mp), po.v(0, 512), dv.v(G2H + mp, G2H + mp + 1), Xk(mp), ALU.mult, ALU.add)
            if stop == "mix":
                break

            if t + 1 < ntiles:
                for k in range(8):
                    DMA("sp", stg[k].ap, xT[k * 128:(k + 1) * 128, tc1:tc1 + 512], "x%d" % k, [], stg[k].keys)

            norm_mod(A3, SH3)
            if t + 1 < ntiles:
                stg_src = lambda k: stg[k]
                ffn(U_F2W1, U_F2W2, G3H, hooks={14: (lambda: norm_sq(stg_src))},
                    mid_hook=(lambda: norm_apply(A1, SH1, src=stg_src)))
            else:
                ffn(U_F2W1, U_F2W2, G3H)
                norm_sq()
                final_norm(t)


        if stop is not None and stop not in ("ada", "attn"):
            for k in range(8):
                o_ = tmp()
                COPY(o_.v(0, 512), Xk(k))
                DMA("sp", outT[k * 128:(k + 1) * 128, 0:512], o_.h[:, 0:512], "ost%d" % o_.idx,
                    o_.keys(0, 512), [("out", 0, k)])

        pr.final_wait("sp", [("ost%d" % i, pr.dmacount.get("ost%d" % i, 0)) for i in range(8)])
        for e in ("pe", "act", "dve", "pool"):
            assert pr.count[e] < 60000, (e, pr.count[e])

        with nc.Block() as block:
            def replay(name):
                def run(e):
                    for waits, fn, inc in pr.streams[name]:
                        for sk, val in waits:
                            e.wait_ge(SEM[sk], val)
                        if fn is not None:
                            fn(e).then_inc(SEM[inc[0]], inc[1])
                return run

            block.sync(replay("sp"))
            block.gpsimd(replay("pool"))
            block.scalar(replay("act"))
            block.vector(replay("dve"))
            block.tensor(replay("pe"))
    return nc, pr


def _fm(v, n):
    return np.ascontiguousarray(np.asarray(v, np.float32).reshape(n, 128).T)


def _kxc(w):
    K = w.shape[0] // 128
    return np.ascontiguousarray(w.reshape(K, 128, w.shape[1]).transpose(1, 0, 2).reshape(128, -1))


def prep_weights(w_ada, ffn1_w_in, ffn1_w_out, w_in, mla_w_uq, mla_w_ukv, w_branch_a, w_branch_b, w_out,
                 ffn2_w_in, ffn2_w_out):
    W = np.zeros((NU, 128, 4096), np.float32)
    wa = np.asarray(w_ada, np.float32).reshape(8, 128, 18, 4, 128)
    W[U_ADA:U_ADA + 18] = wa.transpose(2, 1, 3, 0, 4).reshape(18, 128, 4096)

    def w1_units(w1):
        w1 = np.asarray(w1, np.float32)
        g = w1[:, :FF].reshape(8, 128, JF, 128)
        u = w1[:, FF:].reshape(8, 128, JF, 128)
        gu = np.concatenate([g, u], axis=3)
        gu = gu.transpose(2, 1, 0, 3).reshape(11, 2, 128, 8, 256)
        return gu.transpose(0, 2, 1, 3, 4).reshape(11, 128, 4096)

    def w2_units(w2):
        w2 = np.asarray(w2, np.float32).reshape(JF, 128, 8, 128)
        return w2.transpose(2, 1, 0, 3).reshape(8, 128, FF)

    W[U_F1W1:U_F1W1 + 11] = w1_units(ffn1_w_in)
    W[U_F1W2:U_F1W2 + 8, :, :FF] = w2_units(ffn1_w_out)
    W[U_F2W1:U_F2W1 + 11] = w1_units(ffn2_w_in)
    W[U_F2W2:U_F2W2 + 8, :, :FF] = w2_units(ffn2_w_out)

    win = np.asarray(w_in, np.float32)
    c1 = np.zeros((D, 512), np.float32)
    c1[:, 0:416] = win[:, 0:416]
    c1[:, 416:432] = win[:, 400:416]
    c1[:, 432:448] = win[:, 384:400]
    W[U_WIN + 0] = _kxc(c1)
    for n, c0 in enumerate((416, 928, 1440, 1952, 2464, 2976, 3488)):
        W[U_WIN + 1 + n] = _kxc(win[:, c0:c0 + 512])

    uq = np.asarray(mla_w_uq, np.float32).reshape(256, 8, 96)
    ukv = np.asarray(mla_w_ukv, np.float32).reshape(128, 8, 128)
    uqn = uq[:, :, 0:64].reshape(256, 512)
    pe = uq[:, :, 64:96]
    pes = np.concatenate([pe[:, :, 16:32], pe[:, :, 0:16]], axis=2)
    uqpe = np.concatenate([pe, pes], axis=2).reshape(256, 512)
    ukT = ukv[:, :, 0:64].transpose(1, 2, 0).reshape(4, 128, 128)
    ukT = ukT.transpose(1, 0, 2).reshape(128, 512)
    uv = ukv[:, :, 64:128].reshape(128, 512)
    W[U_MLA, :, 0:1024] = _kxc(uqn)
    W[U_MLA, :, 1024:2048] = _kxc(uqpe)
    W[U_MLA, :, 2048:2560] = ukT
    W[U_MLA, :, 2560:3072] = uv
    W[U_WA] = _kxc(np.asarray(w_branch_a, np.float32))
    W[U_WB] = _kxc(np.asarray(w_branch_b, np.float32))
    wo = np.asarray(w_out, np.float32)
    W[U_WO] = _kxc(wo[:, 0:512])
    W[U_WO + 1] = _kxc(wo[:, 512:1024])
    return W


def prep_bias(rel_bias):
    rb = np.asarray(rel_bias, np.float32)
    kl = np.arange(128)[:, None, None]
    r = np.arange(5)[None, :, None]
    ql = np.arange(128)[None, None, :]
    d = 128 * r + ql - kl
    idx = np.minimum(d, 256) + 256
    g = rb[idx]
    return np.ascontiguousarray(g.transpose(0, 3, 1, 2).reshape(128, 8 * 640))


def prep_vecs(b, c, b_ada, ffn1_norm, mix_norm, ffn2_norm, final_norm, mla_q_norm, mla_kv_norm):
    v = np.zeros((128, NV), np.float32)
    v[:, BADA:BADA + 72] = _fm(b_ada, 72)
    v[:, N1C:N1C + 8] = _fm(ffn1_norm, 8)
    v[:, N2C:N2C + 8] = _fm(mix_norm, 8)
    v[:, N3C:N3C + 8] = _fm(ffn2_norm, 8)
    v[:, NFC:NFC + 8] = _fm(final_norm, 8)
    v[:, QNC:QNC + 2] = _fm(mla_q_norm, 2)
    v[:, KVNC:KVNC + 1] = _fm(mla_kv_norm, 1)
    v[:, CTC:CTC + 8] = _fm(c[b], 8)
    inv_freq = (np.float32(10000.0) ** (-np.arange(0, 32, 2, dtype=np.float32) / np.float32(32))).astype(np.float32)
    iv = (inv_freq.astype(np.float64) / (2 * np.pi)).astype(np.float32)
    v[0:16, IVFC] = iv
    v[16:32, IVFC] = iv
    v[0:16, SGNC] = -1.0
    v[16:32, SGNC] = 1.0
    return v


_CACHE = {}


def kernel(x, c, positions, w_ada, b_ada, ffn1_norm, ffn1_w_in, ffn1_w_out, mix_norm, w_in, mla_q_norm, mla_w_uq,
           mla_kv_norm, mla_w_ukv, rel_bias, w_branch_a, w_branch_b, w_out, ffn2_norm, ffn2_w_in, ffn2_w_out,
           final_norm):
    x = np.asarray(x, np.float32)
    c = np.asarray(c, np.float32)
    positions = np.asarray(positions, np.int32)
    B = x.shape[0]
    W = prep_weights(w_ada[0], ffn1_w_in[0], ffn1_w_out[0], w_in[0], mla_w_uq[0], mla_w_ukv[0], w_branch_a[0],
                     w_branch_b[0], w_out[0], ffn2_w_in[0], ffn2_w_out[0])
    bg = prep_bias(rel_bias[0])
    in_maps = []
    for b in range(B):
        in_maps.append({
            "xT": np.ascontiguousarray(x[b].T),
            "pos": np.ascontiguousarray(positions[b][None, :]),
            "vecs": prep_vecs(b, c, b_ada[0], ffn1_norm[0], mix_norm[0], ffn2_norm[0], final_norm, mla_q_norm[0],
                              mla_kv_norm[0]),
            "biasg": bg,
            "wsrc": W,
        })
    if "nc" not in _CACHE:
        _CACHE["nc"] = build_program()[0]
    nc = _CACHE["nc"]
    res = run_bass_kernel_spmd(nc, in_maps, core_ids=list(range(B)))
    out = np.stack([np.ascontiguousarray(res.results[b]["outT"].T) for b in range(B)], axis=0)
    return out.astype(np.float32)
```

```python
import contextlib
import numpy as np
import concourse.bass as bass
import concourse.mybir as mybir
from concourse.bass_utils import run_bass_kernel_spmd

F32 = mybir.dt.float32
BF16 = mybir.dt.bfloat16
I32 = mybir.dt.int32
AF = mybir.ActivationFunctionType
ALU = mybir.AluOpType

S = 4096
D = 1024
T = 512
NT = S // T
FF = 2816
JF = FF // 128
EPS = 1e-6
NSLOT = 3
NTMP = 4
G = 512

BADA, N1C, N2C, N3C, NFC, QNC, KVNC, CTC, IVFC, SGNC, NV = 0, 72, 80, 88, 96, 104, 106, 107, 115, 116, 120
A1, A2, A3, G1H, G2H, G3H = 0, 8, 16, 24, 32, 40
SH1, SC1, G1, SH2, SC2, G2, SH3, SC3, G3 = 0, 8, 16, 24, 32, 40, 48, 56, 64

U_ADA = 0
U_F1W1 = 18
U_F1W2 = 29
U_WIN = 37
U_MLA = 45
U_WA = 46
U_WB = 47
U_WO = 48
U_F2W1 = 50
U_F2W2 = 61
NU = 69

SCALE_MLA = 96.0 ** -0.5
SCALE_CA = 64.0 ** -0.5
TWO_PI_S = 2.0 * np.pi * (1.0 - 2e-6)


def unit_elems(u):
    if U_F1W2 <= u < U_F1W2 + 8 or U_F2W2 <= u < U_F2W2 + 8:
        return FF
    if u == U_MLA:
        return 3072
    return 4096


class View:
    __slots__ = ("ap", "keys")

    def __init__(self, ap, keys):
        self.ap = ap
        self.keys = keys


class Ten:
    def __init__(self, name, h, esz, idx=None):
        self.name = name
        self.h = h
        self.esz = esz
        self.idx = idx

    def keys(self, lo, hi):
        g0 = (lo * self.esz) // G
        g1 = (hi * self.esz - 1) // G
        return [(self.name, g) for g in range(g0, g1 + 1)]

    def v(self, lo, hi, p0=0, p1=128):
        return View(self.h[p0:p1, lo:hi], self.keys(lo, hi))


class Sub:
    def __init__(self, ten, off):
        self.ten = ten
        self.off = off

    def v(self, lo, hi, p0=0, p1=128):
        return self.ten.v(self.off + lo, self.off + hi, p0, p1)


class Prog:
    ENG = ("pe", "act", "dve", "pool", "sp")

    def __init__(self):
        self.streams = {e: [] for e in self.ENG}
        self.count = {e: 0 for e in ("pe", "act", "dve", "pool")}
        self.seen = {e: {} for e in self.ENG}
        self.lastw = {}
        self.readers = {}
        self.dmacount = {}
        self.know = {}

    def _deps(self, eng, reads, writes):
        deps = {}

        def need(tok, raw):
            if tok is None:
                return
            sk, val = tok
            if sk == eng and eng == "pe":
                return
            if deps.get(sk, 0) < val:
                deps[sk] = val

        lw = self.lastw
        for k in reads:
            need(lw.get(k), True)
        for k in writes:
            need(lw.get(k), False)
            rd = self.readers.get(k)
            if rd:
                for sk, val in rd.items():
                    need((sk, val), False)
        out = []
        seen = self.seen[eng]
        for sk, val in sorted(deps.items(), key=lambda kv: -kv[1]):
            if seen.get(sk, 0) >= val:
                continue
            seen[sk] = val
            out.append((sk, val))
            kn = self.know.get((sk, val))
            if kn:
                for k2, v2 in kn.items():
                    if seen.get(k2, 0) < v2:
                        seen[k2] = v2
        return out

    def _commit(self, tok, reads, writes):
        sk, val = tok
        for k in reads:
            d = self.readers.get(k)
            if d is None:
                d = self.readers[k] = {}
            if d.get(sk, 0) < val:
                d[sk] = val
        for k in writes:
            self.lastw[k] = tok
            self.readers[k] = {}

    def op(self, eng, fn, reads, writes):
        waits = self._deps(eng, reads, writes)
        self.count[eng] += 1
        tok = (eng, self.count[eng])
        self.know[tok] = dict(self.seen[eng])
        self._commit(tok, reads, writes)
        self.streams[eng].append((waits, fn, (eng, 1)))

    def dma(self, queue, fn, semkey, reads, writes):
        waits = self._deps(queue, reads, writes)
        self.dmacount[semkey] = self.dmacount.get(semkey, 0) + 16
        tok = (semkey, self.dmacount[semkey])
        self.know[tok] = dict(self.seen[queue])
        self._commit(tok, reads, writes)
        self.streams[queue].append((waits, fn, (semkey, 16)))
        return tok

    def final_wait(self, eng, toks):
        waits = []
        for sk, val in toks:
            if self.seen[eng].get(sk, 0) < val:
                self.seen[eng][sk] = val
                waits.append((sk, val))
        self.streams[eng].append((waits, None, None))


def build_program(ntiles=NT, stop=None):
    nc = bass.Bass("TRN2", target_bir_lowering=False)
    xT = nc.dram_tensor("xT", [D, S], F32, kind="ExternalInput").ap()
    pos = nc.dram_tensor("pos", [1, S], I32, kind="ExternalInput").ap()
    vecs = nc.dram_tensor("vecs", [128, NV], F32, kind="ExternalInput").ap()
    biasg = nc.dram_tensor("biasg", [128, 8 * 640], F32, kind="ExternalInput").ap()
    wsrc = nc.dram_tensor("wsrc", [NU, 128, 4096], F32, kind="ExternalInput").ap()
    wbf = nc.dram_tensor("wbf", [NU, 128, 4096], BF16, kind="Internal").ap()
    outT = nc.dram_tensor("outT", [D, S], F32, kind="ExternalOutput").ap()

    pr = Prog()
    with contextlib.ExitStack() as es:
        def sb(name, shape, dt):
            h = es.enter_context(nc.sbuf_tensor(name, shape, dt))
            esz = 4 if dt in (F32, I32) else 2
            return Ten(name, h, esz)

        ring = [sb("ring%d" % i, [128, 4096], BF16) for i in range(NSLOT)]
        X = sb("X", [128, 4096], F32)
        hT = sb("hT", [128, 4096], BF16)
        sqm = sb("sqm", [128, 4096], BF16)
        arena = sb("arena", [128, 11264], BF16)
        oA = sb("oA", [128, 2048], BF16)
        oB = sb("oB", [128, 2048], BF16)
        pbMt = sb("pbMt", [128, 2048], BF16)
        pbM = [Sub(pbMt, i * 512) for i in range(4)]
        pbC = [sb("pbC%d" % i, [128, 512], BF16) for i in range(4)]
        ckv = sb("ckv", [128, S], BF16)
        kpe = sb("kpe", [128, S], BF16)
        VE = sb("VE", [128, 32 * 768], BF16)
        cak = sb("cak", [128, 4096], BF16)
        VEc = sb("VEc", [128, 8 * 768], BF16)
        Eb = sb("Eb", [128, 8 * 640], BF16)
        tmps = [sb("tmp%d" % i, [128, 640], F32) for i in range(NTMP)]
        for i, tm in enumerate(tmps):
            tm.idx = i
        rs = sb("rs", [128, 512], F32)
        cs = sb("cs", [128, 1024], F32)
        cosT = Sub(cs, 0)
        sinS = Sub(cs, 512)
        negh = sb("negh", [128, 1], F32)
        epsb = sb("epsb", [128, 1], F32)
        warm = sb("warm", [128, 1], F32)
        ones_f = sb("ones_f", [128, 1], F32)
        vec = sb("vec", [128, NV], F32)
        adaT = sb("adaT", [128, 72], F32)
        dv = sb("dv", [128, 48], F32)
        cact = sb("cact", [128, 8], BF16)
        ones = sb("ones", [128, 128], BF16)
        identb = sb("identb", [128, 128], BF16)
        mk = sb("mk", [128, 256], BF16)

        PS = []
        for i in range(8):
            h = es.enter_context(nc.psum_tensor("ps%d" % i, [128, 512], F32))
            PS.append(Ten("ps%d" % i, h, 4))

        semnames = ["pe", "act", "dve", "pool", "misc", "pos"]
        semnames += ["ring%d" % i for i in range(NSLOT)]
        semnames += ["rgp%d" % i for i in range(NSLOT)] + ["wb%d" % i for i in range(NSLOT)]
        semnames += ["x%d" % k for k in range(8)]
        semnames += ["tmp%d" % i for i in range(NTMP)]
        semnames += ["ost%d" % i for i in range(8)]
        SEM = {n: es.enter_context(nc.semaphore(n)) for n in semnames}

        def ACT(out, in_, func, scale=1.0, bias=None):
            reads = list(in_.keys)
            sc = scale
            if isinstance(scale, View):
                reads += scale.keys
                sc = scale.ap
            bi = bias
            if isinstance(bias, View):
                reads += bias.keys
                bi = bias.ap
            oa, ia = out.ap, in_.ap

            def fn(e):
                if bi is None:
                    return e.activation(out=oa, in_=ia, func=func, scale=sc)
                return e.activation(out=oa, in_=ia, func=func, scale=sc, bias=bi)
            pr.op("act", fn, reads, out.keys)

        def TT(out, a, b, op, eng="dve"):
            oa, aa, ba = out.ap, a.ap, b.ap
            pr.op(eng, lambda e: e.tensor_tensor(out=oa, in0=aa, in1=ba, op=op), a.keys + b.keys, out.keys)

        def TS(out, a, s1, s2, op0, op1=None):
            reads = list(a.keys)
            v1, v2 = s1, s2
            if isinstance(s1, View):
                reads += s1.keys
                v1 = s1.ap
            if isinstance(s2, View):
                reads += s2.keys
                v2 = s2.ap
            oa, aa = out.ap, a.ap

            def fn(e):
                if op1 is None:
                    return e.tensor_scalar(out=oa, in0=aa, scalar1=v1, scalar2=None, op0=op0)
                return e.tensor_scalar(out=oa, in0=aa, scalar1=v1, scalar2=v2, op0=op0, op1=op1)
            pr.op("dve", fn, reads, out.keys)

        def STT(out, a, s, b, op0, op1):
            reads = a.keys + b.keys
            sv = s
            if isinstance(s, View):
                reads = reads + s.keys
                sv = s.ap
            oa, aa, ba = out.ap, a.ap, b.ap
            pr.op("dve", lambda e: e.scalar_tensor_tensor(out=oa, in0=aa, scalar=sv, in1=ba, op0=op0, op1=op1),
                  reads, out.keys)

        def COPY(out, a, eng="dve"):
            oa, aa = out.ap, a.ap
            pr.op(eng, lambda e: e.tensor_copy(out=oa, in_=aa), a.keys, out.keys)

        def ACOPY(out, a):
            oa, aa = out.ap, a.ap
            pr.op("act", lambda e: e.copy(out=oa, in_=aa), a.keys, out.keys)

        def RECIP(out, a):
            oa, aa = out.ap, a.ap
            pr.op("dve", lambda e: e.reciprocal(out=oa, in_=aa), a.keys, out.keys)

        def MEMSET(out, val, eng="dve"):
            oa = out.ap
            pr.op(eng, lambda e: e.memset(oa, val), [], out.keys)

        def POW(out, a):
            oa, aa = out.ap, a.ap
            ba = negh.h[:, 0:1].to_broadcast([128, 512])
            pr.op("pool", lambda e: e.tensor_tensor(out=oa, in0=aa, in1=ba, op=ALU.pow),
                  a.keys + negh.keys(0, 1), out.keys)

        def MM(out, lhsT, rhs, start, stop, skip=False, also=None, alsor=None):
            oa, la, ra = out.ap, lhsT.ap, rhs.ap
            if also:
                out = View(out.ap, out.keys + also)
            if alsor:
                rhs = View(rhs.ap, rhs.keys + alsor)
            if skip:
                pr.op("pe", lambda e: e.matmul(oa, lhsT=la, rhs=ra, start=start, stop=stop, skip_group_check=True),
                      lhsT.keys + rhs.keys, out.keys)
            else:
                pr.op("pe", lambda e: e.matmul(oa, lhsT=la, rhs=ra, start=start, stop=stop),
                      lhsT.keys + rhs.keys, out.keys)

        def DMA(queue, out_ap, in_ap, semkey, reads, writes):
            return pr.dma(queue, lambda e: e.dma_start(out=out_ap, in_=in_ap), semkey, reads, writes)

        tmp_ctr = [0]

        def tmp():
            t_ = tmps[tmp_ctr[0] % NTMP]
            tmp_ctr[0] += 1
            return t_

        def Xk(k):
            return X.v(k * 512, (k + 1) * 512)

        def hTk(k):
            return hT.v(k * 512, (k + 1) * 512)

        def sqk(k):
            return sqm.v(k * 512, (k + 1) * 512)

        def actk(j):
            return arena.v(j * 512, (j + 1) * 512)

        QABS0, QPE0, CAQ0, CQN0 = 0, 4096, 8192, 10240

        stg = []
        for tn in (oA, oB, pbMt):
            for c in range(2):
                stg.append(View(tn.h[:, c * 1024:(c + 1) * 1024].bitcast(F32), tn.keys(c * 1024, (c + 1) * 1024)))
        for c in range(2):
            stg.append(cs.v(c * 512, (c + 1) * 512))

        per_tile = ([U_F1W1 + i for i in range(11)] + [U_F1W2 + m for m in range(8)]
                    + [U_WIN + 0, U_WIN + 1, U_WIN + 2, U_WIN + 3, U_MLA,
                       U_WIN + 4, U_WA, U_WIN + 5, U_WIN + 6, U_WB, U_WIN + 7, U_WO, U_WO + 1]
                    + [U_F2W1 + i for i in range(11)] + [U_F2W2 + m for m in range(8)])
        stream = [U_ADA + a for a in range(4)]
        for i in range(11):
            stream += [U_F1W1 + i, U_ADA + 4 + i]
        for m in range(8):
            stream += [U_F1W2 + m] + ([U_ADA + 15 + m] if m < 3 else [])
        stream += per_tile[19:]
        for _ in range(ntiles - 1):
            stream += per_tile
        st = {"cons": 0, "issued": 0}

        def next_unit(expect=None, live_before=0):
            while st["issued"] < min(len(stream), st["cons"] - live_before + NSLOT):
                n = st["issued"]
                u = stream[n]
                E = unit_elems(u)
                s_ = n % NSLOT
                if n < 18 + len(per_tile):
                    DMA("pool", ring[s_].h[:, 0:E], wsrc[u][:, 0:E], "rgp%d" % s_, [], ring[s_].keys(0, E))
                    if u >= U_F1W1:
                        DMA("sp", wbf[u][:, 0:E], ring[s_].h[:, 0:E], "wb%d" % s_, ring[s_].keys(0, E), [("wbf", u)])
                else:
                    DMA("sp", ring[s_].h[:, 0:E], wbf[u][:, 0:E], "ring%d" % s_, [("wbf", u)], ring[s_].keys(0, E))
                st["issued"] += 1
            n = st["cons"]
            if expect is not None:
                assert stream[n] == expect, (stream[n], expect)
            st["cons"] += 1
            return ring[n % NSLOT]


        DMA("sp", vec.h[:, :], vecs, "misc", [], vec.keys(0, NV))
        MEMSET(ones.v(0, 128), 1.0)
        MEMSET(identb.v(0, 128), 0.0, eng="pool")
        _ia = identb.h[:, :]
        pr.op("pool", lambda e: e.affine_select(out=_ia, in_=_ia, pattern=[[-1, 128]], compare_op=ALU.not_equal,
                                                fill=1.0, base=0, channel_multiplier=1),
              identb.keys(0, 128), identb.keys(0, 128))
        MEMSET(negh.v(0, 1), -0.5, eng="pool")
        MEMSET(epsb.v(0, 1), EPS)
        MEMSET(ones_f.v(0, 1), 1.0)
        MEMSET(mk.v(0, 256), 0.0)
        MEMSET(mk.v(64, 128, 0, 1), 1.0)
        MEMSET(mk.v(128, 192, 0, 1), -30000.0)
        MEMSET(kpe.v(0, S), 0.0)
        for c0 in range(0, 32 * 768, 4096):
            MEMSET(VE.v(c0, c0 + 4096), 1.0)
        MEMSET(VEc.v(0, 4096), 1.0)
        MEMSET(VEc.v(4096, 8 * 768), 1.0)

        for h in range(8):
            tb = tmp()
            DMA("sp", tb.h[:, 0:640], biasg[:, h * 640:(h + 1) * 640], "tmp%d" % tb.idx, [], tb.keys(0, 640))
            ACT(Eb.v(h * 640, (h + 1) * 640), tb.v(0, 640), AF.Identity, scale=1.0 / SCALE_CA)
            MEMSET(Eb.v(h * 640 + 512 + 64, h * 640 + 640, 0, 64), -30000.0)
            MEMSET(Eb.v(h * 640, h * 640 + 64, 64, 128), -30000.0)

        for k in range(8):
            DMA("sp", X.h[:, k * 512:(k + 1) * 512], xT[k * 128:(k + 1) * 128, 0:512], "x%d" % k, [], Xk(k).keys)

        ACT(cact.v(0, 8), vec.v(CTC, CTC + 8), AF.Silu)

        def ada_unit(a):
            R = next_unit(U_ADA + a)
            for cc in range(4):
                c = 4 * a + cc
                for k in range(8):
                    b0 = (cc * 8 + k) * 128
                    MM(PS[7].v(c, c + 1), R.v(b0, b0 + 128), cact.v(k, k + 1), k == 0, k == 7)
            if a == 3:
                TT(adaT.v(0, 16), PS[7].v(0, 16), vec.v(BADA, BADA + 16), ALU.add)
                STT(dv.v(A1, A1 + 8), adaT.v(SC1, SC1 + 8), 1.0, vec.v(N1C, N1C + 8), ALU.add, ALU.mult)
            elif a == 5:
                TT(adaT.v(16, 24), PS[7].v(16, 24), vec.v(BADA + 16, BADA + 24), ALU.add)
                TS(dv.v(G1H, G1H + 8), adaT.v(G1, G1 + 8), 0.5, None, ALU.mult)
            elif a == 17:
                TT(adaT.v(24, 72), PS[7].v(24, 72), vec.v(BADA + 24, BADA + 72), ALU.add)
                for (acol, sccol, ncol) in ((A2, SC2, N2C), (A3, SC3, N3C)):
                    STT(dv.v(acol, acol + 8), adaT.v(sccol, sccol + 8), 1.0, vec.v(ncol, ncol + 8), ALU.add, ALU.mult)
                for (gcol, src) in ((G2H, G2), (G3H, G3)):
                    TS(dv.v(gcol, gcol + 8), adaT.v(src, src + 8), 0.5, None, ALU.mult)

        for a in range(4):
            ada_unit(a)

        if stop == "ada":
            ntiles = 0
            o_ = tmp()
            MEMSET(o_.v(0, 512), 0.0)
            COPY(o_.v(0, 72), adaT.v(0, 72))
            COPY(o_.v(72, 120), dv.v(0, 48))
            DMA("sp", outT[0:128, 0:512], o_.h[:, 0:512], "ost%d" % o_.idx, o_.keys(0, 512), [("out", 0, 0)])
            o2 = tmp()
            COPY(o2.v(0, 512), Eb.v(0, 512))
            DMA("sp", outT[128:256, 0:512], o2.h[:, 0:512], "ost%d" % o2.idx, o2.keys(0, 512), [("out", 0, 1)])

        def stats_rstd(nchunks, sq_of, inv_n, ps_bank, rs_t):
            for k in range(nchunks):
                MM(ps_bank.v(0, 512), ones.v(0, 128), sq_of(k), k == 0, k == nchunks - 1)
            sd = tmp()
            ACT(sd.v(0, 512), ps_bank.v(0, 512), AF.Ln, scale=inv_n, bias=epsb.v(0, 1))
            ACT(rs_t.v(0, 512), sd.v(0, 512), AF.Exp, scale=-0.5)

        def prewarm_ln():
            ACT(warm.v(0, 1), ones_f.v(0, 1), AF.Ln)

        def norm_sq(src=Xk):
            for k in range(8):
                ACT(sqk(k), src(k), AF.Square)

        def norm_apply(acol, bcol, src=Xk):
            stats_rstd(8, sqk, 1.0 / D, PS[6], rs)
            for k in range(8):
                t_ = tmp()
                STT(t_.v(0, 512), src(k), dv.v(acol + k, acol + k + 1), rs.v(0, 512), ALU.mult, ALU.mult)
                ACT(hTk(k), t_.v(0, 512), AF.Identity, bias=adaT.v(bcol + k, bcol + k + 1))

        def norm_mod(acol, bcol, src=Xk):
            norm_sq(src)
            norm_apply(acol, bcol, src)

        def ffn(u_w1, u_w2, ghcol, res=Xk, hooks=None, mid_hook=None, mhooks=None):
            R = None
            hooks = hooks or {}
            mhooks = mhooks or {}
            R = next_unit(u_w1)
            for k in range(8):
                for jj in range(2):
                    for half in range(2):
                        b0 = (jj * 8 + k) * 256 + 128 * half
                        MM(PS[2 * jj + half].v(0, 512), R.v(b0, b0 + 128), hTk(k), k == 0, k == 7)
            for jj in range(2):
                sg = tmp()
                ACT(sg.v(0, 512), PS[2 * jj].v(0, 512), AF.Silu)
                TT(actk(jj), sg.v(0, 512), PS[2 * jj + 1].v(0, 512), ALU.mult)
            for j in range(2, JF):
                if j in hooks:
                    hooks[j]()
                if j % 2 == 0:
                    R = next_unit(u_w1 + j // 2)
                jj = j % 2
                pg, pu = PS[2 * jj], PS[2 * jj + 1]
                for k in range(8):
                    b0 = (jj * 8 + k) * 256
                    MM(pg.v(0, 512), R.v(b0, b0 + 128), hTk(k), k == 0, k == 7,
                       also=(pu.keys(0, 512) if k == 0 else None))
                for k in range(8):
                    b0 = (jj * 8 + k) * 256 + 128
                    MM(pu.v(0, 512), R.v(b0, b0 + 128), hTk(k), k == 0, k == 7)
                sg = tmp()
                ACT(sg.v(0, 512), pg.v(0, 512), AF.Silu)
                TT(actk(j), sg.v(0, 512), pu.v(0, 512), ALU.mult)
            prewarm_ln()
            if mid_hook is not None:
                mid_hook()
            for m in range(8):
                if m in mhooks:
                    mhooks[m]()
                R = next_unit(u_w2 + m)
                py = PS[4 + m % 2]
                for j in range(JF):
                    MM(py.v(0, 512), R.v(j * 128, (j + 1) * 128), actk(j), j == 0, j == JF - 1)
                STT(Xk(m), py.v(0, 512), dv.v(ghcol + m, ghcol + m + 1), res(m), ALU.mult, ALU.add)

        def final_norm(t):
            stats_rstd(8, sqk, 1.0 / D, PS[6], rs)
            for k in range(8):
                STT(Xk(k), Xk(k), vec.v(NFC + k, NFC + k + 1), rs.v(0, 512), ALU.mult, ALU.mult)
                DMA("sp", outT[k * 128:(k + 1) * 128, t * 512:(t + 1) * 512], X.h[:, k * 512:(k + 1) * 512],
                    "ost%d" % k, Xk(k).keys, [("out", t, k)])

        def rope_tables(t):
            a, b, c, d = tmps[0], tmps[1], tmps[2], tmps[3]
            tmp_ctr[0] = 0
            posi = View(a.h[0:32, 0:512].bitcast(I32), a.keys(0, 512))
            DMA("pool", posi.ap, pos[0:1, t * 512:(t + 1) * 512].partition_broadcast(32), "pos", [], posi.keys)
            u_ = b.v(0, 512, 0, 32)
            TS(u_, posi, vec.v(IVFC, IVFC + 1, 0, 32), None, ALU.mult)
            ki = View(c.h[0:32, 0:512].bitcast(I32), c.keys(0, 512))
            COPY(ki, u_)
            f_ = d.v(0, 512, 0, 32)
            TT(f_, u_, ki, ALU.subtract)
            s_ = c.v(0, 512, 0, 32)
            ACT(s_, f_, AF.Sin, scale=TWO_PI_S)
            TS(sinS.v(0, 512, 0, 32), s_, vec.v(SGNC, SGNC + 1, 0, 32), None, ALU.mult)
            v_ = a.v(0, 512, 0, 32)
            TS(v_, u_, 0.25, None, ALU.add)
            COPY(ki, v_)
            TT(f_, v_, ki, ALU.subtract)
            ACT(cosT.v(0, 512, 0, 32), f_, AF.Sin, scale=TWO_PI_S)

        def rope_apply(out, psA, psB):
            t1, t2 = tmp(), tmp()
            TT(t1.v(0, 512, 0, 32), psA, cosT.v(0, 512, 0, 32), ALU.mult)
            TT(t2.v(0, 512, 0, 32), psB, sinS.v(0, 512, 0, 32), ALU.mult)
            TT(out, t1.v(0, 512, 0, 32), t2.v(0, 512, 0, 32), ALU.add)

        def ve_write(dst, base, ps_bank):
            src = ps_bank.h[:, 0:512].rearrange("p (i hh v) -> p i hh v", hh=2, v=64)
            dview = dst.h[:, base:base + 768].rearrange("p (i c) -> p i c", c=192)
            keys_r = ps_bank.keys(0, 512)
            keys_w = dst.keys(base, base + 768)
            COPY(View(dview[:, :, 0:64], keys_w), View(src[:, :, 0, :], keys_r))
            oa, ia = dview[:, :, 128:192], src[:, :, 1, :]
            pr.op("act", lambda e: e.copy(out=oa, in_=ia), keys_r, keys_w)

        def normalise_pair(i, o_t, on_dve=False):
            pe_, po_ = PS[4 + 2 * (i % 2)], PS[5 + 2 * (i % 2)]
            if on_dve:
                t1 = tmp()
                RECIP(t1.v(0, 512, 64, 128), pe_.v(0, 512, 64, 128))
                TT(o_t.v(i * 512, (i + 1) * 512, 0, 64), pe_.v(0, 512, 0, 64), t1.v(0, 512, 64, 128), ALU.mult)
                t2 = tmp()
                RECIP(t2.v(0, 512, 0, 64), po_.v(0, 512, 0, 64))
                TT(o_t.v(i * 512, (i + 1) * 512, 64, 128), po_.v(0, 512, 64, 128), t2.v(0, 512, 0, 64), ALU.mult)
                return
            t1 = tmp()
            ACT(t1.v(0, 512, 64, 128), pe_.v(0, 512, 64, 128), AF.Ln)
            ACT(t1.v(0, 512, 0, 64), po_.v(0, 512, 0, 64), AF.Ln)
            ACT(t1.v(0, 512), t1.v(0, 512), AF.Exp, scale=-1.0)
            TT(o_t.v(i * 512, (i + 1) * 512, 0, 64), pe_.v(0, 512, 0, 64), t1.v(0, 512, 64, 128), ALU.mult)
            TT(o_t.v(i * 512, (i + 1) * 512, 64, 128), po_.v(0, 512, 64, 128), t1.v(0, 512, 0, 64), ALU.mult)

        for t in range(ntiles):
            tc0, tc1 = t * 512, (t + 1) * 512

            if t == 0:
                norm_mod(A1, SH1)
                h0 = {j: (lambda a=3 + j // 2: ada_unit(a)) for j in range(2, JF, 2)}
                ffn(U_F1W1, U_F1W2, G1H, hooks=h0, mid_hook=(lambda: ada_unit(14)),
                    mhooks={m: (lambda a=14 + m: ada_unit(a)) for m in (1, 2, 3)})
            else:
                ffn(U_F1W1, U_F1W2, G1H, res=(lambda m: stg[m]),
                    hooks={2: norm_sq, 6: (lambda tt=t - 1: final_norm(tt))})
            if stop == "ffn1":
                break

            norm_mod(A2, SH2)
            rope_tables(t)
            R = next_unit(U_WIN + 0)
            for k in range(8):
                for (bank, c0, mcols) in ((0, 0, 128), (1, 128, 128), (2, 256, 128), (3, 384, 32), (4, 416, 32)):
                    MM(PS[bank].v(0, 512, 0, mcols), R.v(k * 512 + c0, k * 512 + c0 + mcols), hTk(k), k == 0, k == 7)
            ACT(sqk(0), PS[0].v(0, 512), AF.Square)
            ACT(sqk(1), PS[1].v(0, 512), AF.Square)
            stats_rstd(2, sqk, 1.0 / 256, PS[6], rs)
            for kk in range(2):
                STT(arena.v(CQN0 + kk * 512, CQN0 + (kk + 1) * 512), PS[kk].v(0, 512),
                    vec.v(QNC + kk, QNC + kk + 1), rs.v(0, 512), ALU.mult, ALU.mult)
            ACT(sqk(2), PS[2].v(0, 512), AF.Square)
            rs2 = tmp()
            stats_rstd(1, lambda k: sqk(2), 1.0 / 128, PS[6], rs2)
            STT(ckv.v(tc0, tc1), PS[2].v(0, 512), vec.v(KVNC, KVNC + 1), rs2.v(0, 512), ALU.mult, ALU.mult)
            rope_apply(kpe.v(tc0, tc1, 0, 32), PS[3].v(0, 512, 0, 32), PS[4].v(0, 512, 0, 32))

            R = next_unit(U_WIN + 1)
            for i in range(4):
                pq = PS[5 + 2 * (i % 2)]
                for k in range(8):
                    MM(pq.v(0, 512), R.v(k * 512 + i * 128, k * 512 + (i + 1) * 128), hTk(k), k == 0, k == 7)
                ACOPY(arena.v(CAQ0 + i * 512, CAQ0 + (i + 1) * 512), pq.v(0, 512))
            R = next_unit(U_WIN + 2)
            for i in range(4):
                pk = PS[5 + 2 * (i % 2)]
                for k in range(8):
                    MM(pk.v(0, 512), R.v(k * 512 + i * 128, k * 512 + (i + 1) * 128), hTk(k), k == 0, k == 7)
                cb = i * 1024 + (t % 2) * 512
                COPY(cak.v(cb, cb + 512), pk.v(0, 512))
            R = next_unit(U_WIN + 3)
            for sub in range(4):
                pv = PS[5 + 2 * (sub % 2)]
                for k in range(8):
                    MM(pv.v(0, 512), hT.v(k * 512 + sub * 128, k * 512 + (sub + 1) * 128), R.v(k * 512, (k + 1) * 512),
                       k == 0, k == 7)
                ve_write(VEc, ((4 * t + sub) % 8) * 768, pv)

            R = next_unit(U_MLA)
            UQN, UQPE, UKT, UV = 0, 1024, 2048, 2560
            cqn = lambda kk: arena.v(CQN0 + kk * 512, CQN0 + (kk + 1) * 512)
            for i in range(4):
                pn = PS[i]
                for kk in range(2):
                    b0 = UQN + kk * 512 + i * 128
                    MM(pn.v(0, 512), R.v(b0, b0 + 128), cqn(kk), kk == 0, kk == 1)
                ACOPY(pbM[i].v(0, 512), pn.v(0, 512))
            for h in range(8):
                pA, pB = PS[4 + 2 * (h % 2)], PS[5 + 2 * (h % 2)]
                for (pp, off) in ((pA, 0), (pB, 32)):
                    for kk in range(2):
                        b0 = UQPE + kk * 512 + h * 64 + off
                        MM(pp.v(0, 512, 0, 32), R.v(b0, b0 + 32), cqn(kk), kk == 0, kk == 1)
                rope_apply(arena.v(QPE0 + h * 512, QPE0 + (h + 1) * 512, 0, 32),
                           pA.v(0, 512, 0, 32), pB.v(0, 512, 0, 32))
            for i in range(4):
                qn = pbM[i]
                for hh in range(2):
                    h = 2 * i + hh
                    pa = PS[(2 * i + hh) % 4]
                    MM(pa.v(0, 512), R.v(UKT + i * 128, UKT + (i + 1) * 128, 64 * hh, 64 * hh + 64),
                       qn.v(0, 512, 64 * hh, 64 * hh + 64), True, True)
                    if hh == 0:
                        COPY(arena.v(QABS0 + h * 512, QABS0 + (h + 1) * 512), pa.v(0, 512))
                    else:
                        ACOPY(arena.v(QABS0 + h * 512, QABS0 + (h + 1) * 512), pa.v(0, 512))
            for sub in range(4):
                pv = PS[sub % 2]
                MM(pv.v(0, 512), ckv.v(tc0 + sub * 128, tc0 + (sub + 1) * 128), R.v(UV, UV + 512), True, True)
                ve_write(VE, (4 * t + sub) * 768, pv)

            items = [(i, hh, kt) for i in range(4) for hh in range(2) for kt in range(4 * t + 4)]
            nk = 4 * t + 4

            def mla_S(n):
                i, hh, kt = items[n]
                h = 2 * i + hh
                j = kt - 4 * t
                q0 = 128 * j if j >= 0 else 0
                ps = PS[n % 4]
                MM(ps.v(q0, 512), ckv.v(kt * 128, (kt + 1) * 128),
                   arena.v(QABS0 + h * 512 + q0, QABS0 + (h + 1) * 512), True, False)
                MM(ps.v(q0, 512), kpe.v(kt * 128, (kt + 1) * 128),
                   arena.v(QPE0 + h * 512 + q0, QPE0 + (h + 1) * 512), False, j < 0)
                if j >= 0:
                    MM(ps.v(q0, q0 + 128), mk.v(0, 128), mk.v(128, 256), False, True)
                ACT(pbM[n % 4].v(q0, 512), ps.v(q0, 512), AF.Exp, scale=SCALE_MLA)

            def mla_PV(n, nxt=False):
                i, hh, kt = items[n]
                j = kt - 4 * t
                q0 = 128 * j if j >= 0 else 0
                po = PS[4 + 2 * (i % 2) + hh]
                c0 = kt * 768 + i * 192 + 64 * hh
                MM(po.v(q0, 512), VE.v(c0, c0 + 128), pbM[n % 4].v(q0, 512), kt == 0, kt == nk - 1,
                   alsor=(pbM[(n + 1) % 4].v(0, 512).keys if nxt else None))
                if kt == nk - 1 and hh == 1:
                    normalise_pair(i, oA, on_dve=(i < 3))

            LA = 4
            for n in range(min(LA, len(items))):
                mla_S(n)
            for n in range(0, len(items), 2):
                two = n + 1 < len(items)
                mla_PV(n, nxt=two)
                if two:
                    mla_PV(n + 1)
                for q_ in (n + LA, n + LA + 1):
                    if q_ < len(items):
                        mla_S(q_)

            kt_lo = max(0, 4 * t - 4)
            citems = [(i, hh, Kt) for i in range(4) for hh in range(2) for Kt in range(kt_lo, 4 * t + 4)]

            for h in range(8):
                i_, hh_ = h // 2, h % 2
                src = arena.v(CAQ0 + i_ * 512, CAQ0 + (i_ + 1) * 512, 64 * hh_, 64 * hh_ + 64)
                COPY(arena.v(QABS0 + h * 512, QABS0 + (h + 1) * 512, 64 * hh_, 64 * hh_ + 64), src, eng="pool")
                MEMSET(arena.v(QABS0 + h * 512, QABS0 + (h + 1) * 512, 64 * (1 - hh_), 64 * (1 - hh_) + 64), 0.0,
                       eng="pool")

            def ca_rng(Kt):
                dk = Kt - 4 * t
                s0, s1 = max(0, dk), min(3, dk + 4)
                return dk, 128 * s0, 128 * (s1 + 1)

            def ca_S(n):
                i, hh, Kt = citems[n]
                h = 2 * i + hh
                dk, c0, c1 = ca_rng(Kt)
                w = Kt % 8
                ps = PS[n % 4]
                kb = i * 1024 + w * 128
                MM(ps.v(c0, c1), cak.v(kb, kb + 128),
                   arena.v(QABS0 + h * 512 + c0, QABS0 + h * 512 + c1), True, False)
                e0 = h * 640 + c0 - 128 * dk
                MM(ps.v(c0, c1), identb.v(0, 128), Eb.v(e0, e0 + (c1 - c0)), False, True)
                ACT(pbC[n % 4].v(c0, c1), ps.v(c0, c1), AF.Exp, scale=SCALE_CA)

            def ca_PV(n, nxt=False):
                i, hh, Kt = citems[n]
                dk, c0, c1 = ca_rng(Kt)
                w = Kt % 8
                po = PS[4 + 2 * (i % 2) + hh]
                v0 = w * 768 + i * 192 + 64 * hh
                MM(po.v(c0, c1), VEc.v(v0, v0 + 128), pbC[n % 4].v(c0, c1), Kt == kt_lo, Kt == 4 * t + 3, skip=True,
                   alsor=(pbC[(n + 1) % 4].keys(0, 512) if nxt else None))
                if Kt == 4 * t + 3 and hh == 1:
                    normalise_pair(i, oB)

            for n in range(min(LA, len(citems))):
                ca_S(n)
            for n in range(0, len(citems), 2):
                two = n + 1 < len(citems)
                ca_PV(n, nxt=two)
                if two:
                    ca_PV(n + 1)
                for q_ in (n + LA, n + LA + 1):
                    if q_ < len(citems):
                        ca_S(q_)

            if stop == "attn":
                for q_, o_t in enumerate((oA, oB)):
                    for i in range(4):
                        o_ = tmp()
                        COPY(o_.v(0, 512), o_t.v(i * 512, (i + 1) * 512))
                        DMA("sp", outT[q_ * 128:(q_ + 1) * 128, i * 512:(i + 1) * 512], o_.h[:, 0:512],
                            "ost%d" % o_.idx, o_.keys(0, 512), [("out", q_, i)])
                for q_, src in enumerate((arena.v(QABS0, QABS0 + 512), arena.v(QPE0, QPE0 + 512), ckv.v(0, 512), kpe.v(0, 512),
                                          arena.v(CAQ0, CAQ0 + 512), cak.v(0, 512))):
                    o_ = tmp()
                    COPY(o_.v(0, 512), src)
                    DMA("sp", outT[(2 + q_) * 128:(3 + q_) * 128, 0:512], o_.h[:, 0:512],
                        "ost%d" % o_.idx, o_.keys(0, 512), [("out", 2 + q_, 0)])
                break

            mgk = sqk
            for phase, (ug0, uw, ug1, o_t) in enumerate(((U_WIN + 4, U_WA, U_WIN + 5, oA), (U_WIN + 6, U_WB, U_WIN + 7, oB))):
                Rg = Rw = None
                for m in range(8):
                    if m == 0:
                        Rg = next_unit(ug0)
                        Rw = next_unit(uw, live_before=1)
                    if m == 4:
                        Rg = next_unit(ug1, live_before=1)
                    pg, py = PS[m % 2], PS[2 + m % 2]
                    for k in range(8):
                        b0 = k * 512 + (m % 4) * 128
                        MM(pg.v(0, 512), Rg.v(b0, b0 + 128), hTk(k), k == 0, k == 7)
                    for i in range(4):
                        b0 = i * 1024 + m * 128
                        MM(py.v(0, 512), Rw.v(b0, b0 + 128), o_t.v(i * 512, (i + 1) * 512), i == 0, i == 3)
                    th = tmp()
                    ACT(th.v(0, 512), pg.v(0, 512), AF.Tanh, scale=0.5)
                    if phase == 0:
                        STT(mgk(m), th.v(0, 512), 1.0, py.v(0, 512), ALU.add, ALU.mult)
                    else:
                        u2 = tmp()
                        STT(u2.v(0, 512), th.v(0, 512), 1.0, py.v(0, 512), ALU.add, ALU.mult)
                        TT(mgk(m), mgk(m), u2.v(0, 512), ALU.add)
            prewarm_ln()
            R = None
            for mp in range(8):
                if mp % 4 == 0:
                    R = next_unit(U_WO + mp // 4)
                po = PS[4 + mp % 2]
                for m in range(8):
                    b0 = m * 512 + (mp % 4) * 128
                    MM(po.v(0, 512), R.v(b0, b0 + 128), mgk(m), m == 0, m == 7)
                STT(Xk(mp), po.v(0, 512), dv.v(G2H + mp, G2H + mp + 1), Xk(mp), ALU.mult, ALU.add)
            if stop == "mix":
                break

            if t + 1 < ntiles:
                for k in range(8):
                    DMA("sp", stg[k].ap, xT[k * 128:(k + 1) * 128, tc1:tc1 + 512], "x%d" % k, [], stg[k].keys)

            norm_mod(A3, SH3)
            if t + 1 < ntiles:
                stg_src = lambda k: stg[k]
                ffn(U_F2W1, U_F2W2, G3H, hooks={14: (lambda: norm_sq(stg_src))},
                    mid_hook=(lambda: norm_apply(A1, SH1, src=stg_src)))
            else:
                ffn(U_F2W1, U_F2W2, G3H)
                norm_sq()
                final_norm(t)


        if stop is not None and stop not in ("ada", "attn"):
            for k in range(8):
                o_ = tmp()
                COPY(o_.v(0, 512), Xk(k))
                DMA("sp", outT[k * 128:(k + 1) * 128, 0:512], o_.h[:, 0:512], "ost%d" % o_.idx,
                    o_.keys(0, 512), [("out", 0, k)])

        pr.final_wait("sp", [("ost%d" % i, pr.dmacount.get("ost%d" % i, 0)) for i in range(8)])
        for e in ("pe", "act", "dve", "pool"):
            assert pr.count[e] < 60000, (e, pr.count[e])

        with nc.Block() as block:
            def replay(name):
                def run(e):
                    for waits, fn, inc in pr.streams[name]:
                        for sk, val in waits:
                            e.wait_ge(SEM[sk], val)
                        if fn is not None:
                            fn(e).then_inc(SEM[inc[0]], inc[1])
                return run

            block.sync(replay("sp"))
            block.gpsimd(replay("pool"))
            block.scalar(replay("act"))
            block.vector(replay("dve"))
            block.tensor(replay("pe"))
    return nc, pr


def _fm(v, n):
    return np.ascontiguousarray(np.asarray(v, np.float32).reshape(n, 128).T)


def _kxc(w):
    K = w.shape[0] // 128
    return np.ascontiguousarray(w.reshape(K, 128, w.shape[1]).transpose(1, 0, 2).reshape(128, -1))


def prep_weights(w_ada, ffn1_w_in, ffn1_w_out, w_in, mla_w_uq, mla_w_ukv, w_branch_a, w_branch_b, w_out,
                 ffn2_w_in, ffn2_w_out):
    W = np.zeros((NU, 128, 4096), np.float32)
    wa = np.asarray(w_ada, np.float32).reshape(8, 128, 18, 4, 128)
    W[U_ADA:U_ADA + 18] = wa.transpose(2, 1, 3, 0, 4).reshape(18, 128, 4096)

    def w1_units(w1):
        w1 = np.asarray(w1, np.float32)
        g = w1[:, :FF].reshape(8, 128, JF, 128)
        u = w1[:, FF:].reshape(8, 128, JF, 128)
        gu = np.concatenate([g, u], axis=3)
        gu = gu.transpose(2, 1, 0, 3).reshape(11, 2, 128, 8, 256)
        return gu.transpose(0, 2, 1, 3, 4).reshape(11, 128, 4096)

    def w2_units(w2):
        w2 = np.asarray(w2, np.float32).reshape(JF, 128, 8, 128)
        return w2.transpose(2, 1, 0, 3).reshape(8, 128, FF)

    W[U_F1W1:U_F1W1 + 11] = w1_units(ffn1_w_in)
    W[U_F1W2:U_F1W2 + 8, :, :FF] = w2_units(ffn1_w_out)
    W[U_F2W1:U_F2W1 + 11] = w1_units(ffn2_w_in)
    W[U_F2W2:U_F2W2 + 8, :, :FF] = w2_units(ffn2_w_out)

    win = np.asarray(w_in, np.float32)
    c1 = np.zeros((D, 512), np.float32)
    c1[:, 0:416] = win[:, 0:416]
    c1[:, 416:432] = win[:, 400:416]
    c1[:, 432:448] = win[:, 384:400]
    W[U_WIN + 0] = _kxc(c1)
    for n, c0 in enumerate((416, 928, 1440, 1952, 2464, 2976, 3488)):
        W[U_WIN + 1 + n] = _kxc(win[:, c0:c0 + 512])

    uq = np.asarray(mla_w_uq, np.float32).reshape(256, 8, 96)
    ukv = np.asarray(mla_w_ukv, np.float32).reshape(128, 8, 128)
    uqn = uq[:, :, 0:64].reshape(256, 512)
    pe = uq[:, :, 64:96]
    pes = np.concatenate([pe[:, :, 16:32], pe[:, :, 0:16]], axis=2)
    uqpe = np.concatenate([pe, pes], axis=2).reshape(256, 512)
    ukT = ukv[:, :, 0:64].transpose(1, 2, 0).reshape(4, 128, 128)
    ukT = ukT.transpose(1, 0, 2).reshape(128, 512)
    uv = ukv[:, :, 64:128].reshape(128, 512)
    W[U_MLA, :, 0:1024] = _kxc(uqn)
    W[U_MLA, :, 1024:2048] = _kxc(uqpe)
    W[U_MLA, :, 2048:2560] = ukT
    W[U_MLA, :, 2560:3072] = uv
    W[U_WA] = _kxc(np.asarray(w_branch_a, np.float32))
    W[U_WB] = _kxc(np.asarray(w_branch_b, np.float32))
    wo = np.asarray(w_out, np.float32)
    W[U_WO] = _kxc(wo[:, 0:512])
    W[U_WO + 1] = _kxc(wo[:, 512:1024])
    return W


def prep_bias(rel_bias):
    rb = np.asarray(rel_bias, np.float32)
    kl = np.arange(128)[:, None, None]
    r = np.arange(5)[None, :, None]
    ql = np.arange(128)[None, None, :]
    d = 128 * r + ql - kl
    idx = np.minimum(d, 256) + 256
    g = rb[idx]
    return np.ascontiguousarray(g.transpose(0, 3, 1, 2).reshape(128, 8 * 640))


def prep_vecs(b, c, b_ada, ffn1_norm, mix_norm, ffn2_norm, final_norm, mla_q_norm, mla_kv_norm):
    v = np.zeros((128, NV), np.float32)
    v[:, BADA:BADA + 72] = _fm(b_ada, 72)
    v[:, N1C:N1C + 8] = _fm(ffn1_norm, 8)
    v[:, N2C:N2C + 8] = _fm(mix_norm, 8)
    v[:, N3C:N3C + 8] = _fm(ffn2_norm, 8)
    v[:, NFC:NFC + 8] = _fm(final_norm, 8)
    v[:, QNC:QNC + 2] = _fm(mla_q_norm, 2)
    v[:, KVNC:KVNC + 1] = _fm(mla_kv_norm, 1)
    v[:, CTC:CTC + 8] = _fm(c[b], 8)
    inv_freq = (np.float32(10000.0) ** (-np.arange(0, 32, 2, dtype=np.float32) / np.float32(32))).astype(np.float32)
    iv = (inv_freq.astype(np.float64) / (2 * np.pi)).astype(np.float32)
    v[0:16, IVFC] = iv
    v[16:32, IVFC] = iv
    v[0:16, SGNC] = -1.0
    v[16:32, SGNC] = 1.0
    return v


_CACHE = {}


def kernel(x, c, positions, w_ada, b_ada, ffn1_norm, ffn1_w_in, ffn1_w_out, mix_norm, w_in, mla_q_norm, mla_w_uq,
           mla_kv_norm, mla_w_ukv, rel_bias, w_branch_a, w_branch_b, w_out, ffn2_norm, ffn2_w_in, ffn2_w_out,
           final_norm):
    x = np.asarray(x, np.float32)
    c = np.asarray(c, np.float32)
    positions = np.asarray(positions, np.int32)
    B = x.shape[0]
    W = prep_weights(w_ada[0], ffn1_w_in[0], ffn1_w_out[0], w_in[0], mla_w_uq[0], mla_w_ukv[0], w_branch_a[0],
                     w_branch_b[0], w_out[0], ffn2_w_in[0], ffn2_w_out[0])
    bg = prep_bias(rel_bias[0])
    in_maps = []
    for b in range(B):
        in_maps.append({
            "xT": np.ascontiguousarray(x[b].T),
            "pos": np.ascontiguousarray(positions[b][None, :]),
            "vecs": prep_vecs(b, c, b_ada[0], ffn1_norm[0], mix_norm[0], ffn2_norm[0], final_norm, mla_q_norm[0],
                              mla_kv_norm[0]),
            "biasg": bg,
            "wsrc": W,
        })
    if "nc" not in _CACHE:
        _CACHE["nc"] = build_program()[0]
    nc = _CACHE["nc"]
    res = run_bass_kernel_spmd(nc, in_maps, core_ids=list(range(B)))
    out = np.stack([np.ascontiguousarray(res.results[b]["outT"].T) for b in range(B)], axis=0)
    return out.astype(np.float32)
```
